# Optimizing a Trainium2 kernel written in Bass

```python
import jax, jax.numpy as jnp
from jax import lax
import numpy as np

D_MODEL = 1024
BATCH = 4
SEQ = 8192
DEPTH = 4
DEC_BATCH = 8
DEC_SEQ = 32
PAST_LEN = 4096

CHUNK = 64
N_MIXERS = 3
N_A = (DEPTH + 2) // 3
N_B = (DEPTH + 1) // 3
N_C = DEPTH // 3
RMS_EPS = 1e-6

HG_HEADS = 8
HG_DF = 128
HG_DI = D_MODEL // HG_HEADS
HG_QF = HG_HEADS * HG_DF
HG_VI = HG_HEADS * HG_DI
HG_IN = 2 * HG_QF + 2 * HG_VI

D_RNN = D_MODEL
RG_BLOCKS = 8
RG_BW = D_RNN // RG_BLOCKS
CONV_W = 4
RG_C = 8.0

MLA_HEADS = 8
Q_LORA = 384
KV_LORA = 256
QK_NOPE = 128
QK_ROPE = 64
V_HEAD = 128
ROPE_BASE = 10000.0
Q_BLOCK = 128
MLA_DOWN = Q_LORA + KV_LORA + QK_ROPE

PK_HEADS = 8
N_KEYS = 128
N_EXPERTS = N_KEYS * N_KEYS
PK_DKEY = 128
PK_DHALF = PK_DKEY // 2
PK_TOPK = 16
PEER_TOK_BLOCK = 256

kernel_name = "hybrid_streaming_hgrn2_rglru_mla_peer_step"


def rmsnorm(x, g):
    xf = x.astype(jnp.float32)
    y = xf * lax.rsqrt(jnp.mean(xf * xf, axis=-1, keepdims=True) + RMS_EPS)
    return (y * g.astype(jnp.float32)).astype(x.dtype)


def hgrn2_mixer(x, s0, w_in, lb, g_norm, w_out):
    B, T, _ = x.shape
    f32 = jnp.float32
    q, f, iv, g = jnp.split(x @ w_in, [HG_QF, 2 * HG_QF, 2 * HG_QF + HG_VI], axis=-1)
    q = jax.nn.silu(q.astype(f32))
    f = f.astype(f32)
    logf = jnp.logaddexp(jnp.log(lb), jnp.log1p(-lb) + jax.nn.log_sigmoid(f))
    k = (1.0 - lb) * jax.nn.sigmoid(-f)
    blk = min(CHUNK, T)
    nb = T // blk

    def blocks(a, d):
        return a.reshape(B, nb, blk, HG_HEADS, d).transpose(1, 0, 3, 2, 4)

    causal = jnp.tril(jnp.ones((blk, blk), dtype=bool))[:, :, None]

    def step(S, inp):
        qc, kc, lc, ic = inp
        bcum = jnp.cumsum(lc, axis=2)
        decay = jnp.exp(jnp.where(causal, bcum[:, :, :, None, :] - bcum[:, :, None, :, :], -jnp.inf))
        scores = jnp.einsum('bhtf,bhsf,bhtsf->bhts', qc, kc, decay)
        o = (jnp.einsum('bhts,bhsi->bhti', scores, ic)
             + jnp.einsum('bhtf,bhfi->bhti', qc * jnp.exp(bcum), S))
        btot = bcum[:, :, -1:, :]
        S = (jnp.exp(btot[:, :, 0, :, None]) * S
             + jnp.einsum('bhsf,bhsi->bhfi', kc * jnp.exp(btot - bcum), ic))
        return S, o

    S_fin, o = lax.scan(step, s0.astype(f32),
                        (blocks(q, HG_DF), blocks(k, HG_DF), blocks(logf, HG_DF),
                         blocks(iv.astype(f32), HG_DI)))
    o = o.transpose(1, 0, 3, 2, 4).reshape(B, T, HG_VI)
    o = rmsnorm(o, g_norm) * jax.nn.silu(g.astype(f32))
    return o.astype(x.dtype) @ w_out, S_fin.astype(x.dtype)


def rglru_mixer(x, h0, conv0, w_in, conv_w, conv_b, w_a, b_a, w_x, b_x, lam, w_out):
    B, T, _ = x.shape
    f32 = jnp.float32
    gate_br, u = jnp.split((x @ w_in).astype(f32), 2, axis=-1)
    upad = jnp.concatenate([conv0.astype(f32), u], axis=1)
    xc = conv_b.astype(f32) + sum(conv_w[j].astype(f32) * upad[:, j:j + T] for j in range(CONV_W))
    xb = xc.reshape(B, T, RG_BLOCKS, RG_BW)
    r = jax.nn.sigmoid(jnp.einsum('btnc,ncd->btnd', xb, w_a.astype(f32)).reshape(B, T, D_RNN) + b_a)
    ig = jax.nn.sigmoid(jnp.einsum('btnc,ncd->btnd', xb, w_x.astype(f32)).reshape(B, T, D_RNN) + b_x)
    log_a = -RG_C * r * jax.nn.softplus(-lam.astype(f32))
    a = jnp.exp(log_a)
    inp = jnp.sqrt(-jnp.expm1(2.0 * log_a)) * (ig * xc)

    def combine(left, right):
        a_l, b_l = left
        a_r, b_r = right
        return a_l * a_r, a_r * b_l + b_r

    a_cum, b_cum = lax.associative_scan(combine, (a, inp), axis=1)
    h = a_cum * h0.astype(f32)[:, None, :] + b_cum
    y = (h * jax.nn.gelu(gate_br)).astype(x.dtype) @ w_out
    return y, h[:, -1].astype(x.dtype), upad[:, T:].astype(x.dtype)


def rope_tables(pos):
    inv = ROPE_BASE ** (-jnp.arange(0, QK_ROPE, 2, dtype=jnp.float32) / QK_ROPE)
    ang = pos.astype(jnp.float32)[:, None] * inv[None, :]
    return jnp.cos(ang), jnp.sin(ang)


def apply_rope(x, cos, sin):
    x1, x2 = jnp.split(x.astype(jnp.float32), 2, axis=-1)
    return jnp.concatenate([x1 * cos - x2 * sin, x2 * cos + x1 * sin], axis=-1).astype(x.dtype)


def mla_mixer(x, ckv_past, kr_past, w_down, q_norm, w_uq, kv_norm, w_uk, w_uv, w_out):
    B, T, _ = x.shape
    P = ckv_past.shape[1]
    f32 = jnp.float32
    cq, ckv, kr = jnp.split(x @ w_down, [Q_LORA, Q_LORA + KV_LORA], axis=-1)
    q = (rmsnorm(cq, q_norm) @ w_uq).reshape(B, T, MLA_HEADS, QK_NOPE + QK_ROPE)
    ckv = rmsnorm(ckv, kv_norm)
    q_pos = P + jnp.arange(T)
    cos, sin = rope_tables(q_pos)
    q_nope = q[..., :QK_NOPE]
    q_rope = apply_rope(q[..., QK_NOPE:], cos[:, None, :], sin[:, None, :])
    kr = apply_rope(kr, cos, sin)
    ckv_all = jnp.concatenate([ckv_past.astype(x.dtype), ckv], axis=1)
    kr_all = jnp.concatenate([kr_past.astype(x.dtype), kr], axis=1)
    k_nope = jnp.einsum('bsc,chn->bshn', ckv_all, w_uk)
    v = jnp.einsum('bsc,chv->bshv', ckv_all, w_uv)
    k_chunk = jnp.arange(P + T) // CHUNK
    qblk = min(Q_BLOCK, T)
    nb = T // qblk
    scale = (QK_NOPE + QK_ROPE) ** -0.5

    def attend(blk):
        qn, qr, qp = blk
        s = jnp.einsum('bthn,bshn->bhts', qn, k_nope) + jnp.einsum('bthr,bsr->bhts', qr, kr_all)
        visible = k_chunk[None, :] <= (qp // CHUNK)[:, None]
        s = jnp.where(visible, s.astype(f32) * scale, -jnp.inf)
        p = jax.nn.softmax(s, axis=-1).astype(v.dtype)
        return jnp.einsum('bhts,bshv->bthv', p, v)

    def to_blocks(a):
        return a.reshape(B, nb, qblk, *a.shape[2:]).swapaxes(0, 1)

    o = lax.map(attend, (to_blocks(q_nope), to_blocks(q_rope), q_pos.reshape(nb, qblk)))
    o = o.swapaxes(0, 1).reshape(B, T, MLA_HEADS * V_HEAD).astype(x.dtype)
    return o @ w_out, ckv, kr


def peer_ffn(x, w_query, sub_keys, expert_u, expert_v):
    B, T, D = x.shape
    ntok = B * T
    blk = min(PEER_TOK_BLOCK, ntok)
    pad = (-ntok) % blk
    tok = jnp.pad(x.reshape(ntok, D), ((0, pad), (0, 0)))

    def block(xt):
        q = (xt @ w_query).reshape(blk, PK_HEADS, 2, PK_DHALF)
        s = jnp.einsum('thpd,hpkd->thpk', q, sub_keys).astype(jnp.float32)
        sv, si = lax.top_k(s, PK_TOPK)
        cand = sv[:, :, 0, :, None] + sv[:, :, 1, None, :]
        cidx = si[:, :, 0, :, None] * N_KEYS + si[:, :, 1, None, :]
        top_s, top_j = lax.top_k(cand.reshape(blk, PK_HEADS, PK_TOPK * PK_TOPK), PK_TOPK)
        eidx = jnp.take_along_axis(cidx.reshape(blk, PK_HEADS, -1), top_j, axis=-1)
        gate = jax.nn.softmax(top_s, axis=-1)
        u = expert_u[eidx]
        v = expert_v[eidx]
        act = jax.nn.gelu(jnp.einsum('td,thkd->thk', xt, u).astype(jnp.float32))
        return jnp.einsum('thk,thkd->td', (gate * act).astype(v.dtype), v)

    out = lax.map(block, tok.reshape(-1, blk, D))
    return out.reshape(-1, D)[:ntok].reshape(B, T, D).astype(x.dtype)


def setup_inputs(seed: int = 0) -> dict:
    key = jax.random.key(seed)
    ks = iter(jax.random.split(key, 48))
    f32 = jnp.float32

    def nrm(shape, scale):
        return jax.random.normal(next(ks), shape, f32) * scale

    def gain(shape):
        return 1.0 + 0.05 * jax.random.normal(next(ks), shape, f32)

    a_init = jax.random.uniform(next(ks), (N_B, D_RNN), f32, 0.9, 0.999)
    return {
        "x_prompt": nrm((BATCH, SEQ, D_MODEL), 1.0),
        "x_sample": nrm((DEC_BATCH, DEC_SEQ, D_MODEL), 1.0),
        "state_hgrn": nrm((N_A, DEC_BATCH, HG_HEADS, HG_DF, HG_DI), 0.5),
        "state_rglru_h": nrm((N_B, DEC_BATCH, D_RNN), 0.5),
        "state_rglru_conv": nrm((N_B, DEC_BATCH, CONV_W - 1, D_RNN), 1.0),
        "cache_mla_ckv": nrm((N_C, DEC_BATCH, PAST_LEN, KV_LORA), 1.0),
        "cache_mla_krope": nrm((N_C, DEC_BATCH, PAST_LEN, QK_ROPE), 1.0),
        "norm_mix": gain((DEPTH, D_MODEL)),
        "norm_ffn": gain((DEPTH, D_MODEL)),
        "norm_final": gain((D_MODEL,)),
        "hg_w_in": nrm((N_A, D_MODEL, HG_IN), D_MODEL ** -0.5),
        "hg_lb_logits": nrm((DEPTH, HG_QF), 0.5),
        "hg_norm": gain((N_A, HG_VI)),
        "hg_w_out": nrm((N_A, HG_VI, D_MODEL), HG_VI ** -0.5),
        "rg_w_in": nrm((N_B, D_MODEL, 2 * D_RNN), D_MODEL ** -0.5),
        "rg_conv_w": nrm((N_B, CONV_W, D_RNN), CONV_W ** -0.5),
        "rg_conv_b": nrm((N_B, D_RNN), 0.02),
        "rg_w_a": nrm((N_B, RG_BLOCKS, RG_BW, RG_BW), RG_BW ** -0.5),
        "rg_b_a": nrm((N_B, D_RNN), 0.1),
        "rg_w_x": nrm((N_B, RG_BLOCKS, RG_BW, RG_BW), RG_BW ** -0.5),
        "rg_b_x": nrm((N_B, D_RNN), 0.1),
        "rg_lambda": jnp.log(a_init) - jnp.log1p(-a_init),
        "rg_w_out": nrm((N_B, D_RNN, D_MODEL), D_RNN ** -0.5),
        "mla_w_down": nrm((N_C, D_MODEL, MLA_DOWN), D_MODEL ** -0.5),
        "mla_q_norm": gain((N_C, Q_LORA)),
        "mla_w_uq": nrm((N_C, Q_LORA, MLA_HEADS * (QK_NOPE + QK_ROPE)), Q_LORA ** -0.5),
        "mla_kv_norm": gain((N_C, KV_LORA)),
        "mla_w_uk": nrm((N_C, KV_LORA, MLA_HEADS, QK_NOPE), KV_LORA ** -0.5),
        "mla_w_uv": nrm((N_C, KV_LORA, MLA_HEADS, V_HEAD), KV_LORA ** -0.5),
        "mla_w_out": nrm((N_C, MLA_HEADS * V_HEAD, D_MODEL), (MLA_HEADS * V_HEAD) ** -0.5),
        "pk_w_query": nrm((DEPTH, D_MODEL, PK_HEADS * PK_DKEY), D_MODEL ** -0.5),
        "pk_sub_keys": nrm((DEPTH, PK_HEADS, 2, N_KEYS, PK_DHALF), PK_DHALF ** -0.5),
        "pk_u": nrm((DEPTH, N_EXPERTS, D_MODEL), D_MODEL ** -0.5),
        "pk_v": nrm((DEPTH, N_EXPERTS, D_MODEL), (PK_HEADS * PK_TOPK) ** -0.5),
    }


def reference(x_prompt, x_sample, state_hgrn, state_rglru_h, state_rglru_conv, cache_mla_ckv,
              cache_mla_krope, norm_mix, norm_ffn, norm_final, hg_w_in, hg_lb_logits, hg_norm,
              hg_w_out, rg_w_in, rg_conv_w, rg_conv_b, rg_w_a, rg_b_a, rg_w_x, rg_b_x, rg_lambda,
              rg_w_out, mla_w_down, mla_q_norm, mla_w_uq, mla_kv_norm, mla_w_uk, mla_w_uv,
              mla_w_out, pk_w_query, pk_sub_keys, pk_u, pk_v):
    B = x_prompt.shape[0]
    dt = x_prompt.dtype
    lb_cum = jnp.cumsum(jax.nn.softmax(hg_lb_logits.astype(jnp.float32), axis=0), axis=0)
    lb_all = lb_cum - lb_cum[0]
    xp, xs = x_prompt, x_sample
    hg_p, hg_s, rh_p, rh_s, rc_p, rc_s, ck_p, ck_s, kr_p, kr_s = ([] for _ in range(10))
    for i in range(DEPTH):
        kind, slot = i % N_MIXERS, i // N_MIXERS
        hp, hs = rmsnorm(xp, norm_mix[i]), rmsnorm(xs, norm_mix[i])
        if kind == 0:
            w = (hg_w_in[slot], lb_all[i], hg_norm[slot], hg_w_out[slot])
            mp, sp = hgrn2_mixer(hp, jnp.zeros((B, HG_HEADS, HG_DF, HG_DI), dt), *w)
            ms, ss = hgrn2_mixer(hs, state_hgrn[slot], *w)
            hg_p.append(sp)
            hg_s.append(ss)
        elif kind == 1:
            w = (rg_w_in[slot], rg_conv_w[slot], rg_conv_b[slot], rg_w_a[slot], rg_b_a[slot],
                 rg_w_x[slot], rg_b_x[slot], rg_lambda[slot], rg_w_out[slot])
            mp, hpn, cpn = rglru_mixer(hp, jnp.zeros((B, D_RNN), dt),
                                       jnp.zeros((B, CONV_W - 1, D_RNN), dt), *w)
            ms, hsn, csn = rglru_mixer(hs, state_rglru_h[slot], state_rglru_conv[slot], *w)
            rh_p.append(hpn)
            rh_s.append(hsn)
            rc_p.append(cpn)
            rc_s.append(csn)
        else:
            w = (mla_w_down[slot], mla_q_norm[slot], mla_w_uq[slot], mla_kv_norm[slot],
                 mla_w_uk[slot], mla_w_uv[slot], mla_w_out[slot])
            mp, ckp, krp = mla_mixer(hp, jnp.zeros((B, 0, KV_LORA), dt),
                                     jnp.zeros((B, 0, QK_ROPE), dt), *w)
            ms, cks, krs = mla_mixer(hs, cache_mla_ckv[slot], cache_mla_krope[slot], *w)
            ck_p.append(ckp)
            ck_s.append(cks)
            kr_p.append(krp)
            kr_s.append(krs)
        xp = xp + mp
        xs = xs + ms
        pw = (pk_w_query[i], pk_sub_keys[i], pk_u[i], pk_v[i])
        xp = xp + peer_ffn(rmsnorm(xp, norm_ffn[i]), *pw)
        xs = xs + peer_ffn(rmsnorm(xs, norm_ffn[i]), *pw)
    return (rmsnorm(xp, norm_final), rmsnorm(xs, norm_final),
            jnp.stack(hg_p), jnp.stack(hg_s),
            jnp.stack(rh_p), jnp.stack(rh_s),
            jnp.stack(rc_p), jnp.stack(rc_s),
            jnp.stack(ck_p), jnp.stack(ck_s),
            jnp.stack(kr_p), jnp.stack(kr_s))
```

```python
import os
import numpy as np
import concourse.bass as bass
import concourse.mybir as mybir
from concourse.bass_utils import run_bass_kernel_spmd

F32 = mybir.dt.float32
BF16 = mybir.dt.bfloat16
I32 = mybir.dt.int32
U32 = mybir.dt.uint32
AF = mybir.ActivationFunctionType
ALU = mybir.AluOpType
AX = mybir.AxisListType

D = 1024
DEPTH = 4
EPS = 1e-6
NEXP = 16384
PAST = 4096
TS = 32
QSCALE = 192.0 ** -0.5
GC1 = 0.044715
GC2 = 1.5957691216057308
ARENA = 42000


class Tok:
    __slots__ = ("t", "lw", "rd", "name")

    def __init__(self, t=None, name=""):
        self.t = t
        self.lw = {}
        self.rd = {}
        self.name = name


class Rec:
    def __init__(self):
        self.call = None

    def __getattr__(self, name):
        def f(*a, **k):
            self.call = (name, a, k)
            return self
        return f


def _rec(fn):
    r = Rec()
    fn(r)
    return r.call


class Sched:
    ENG = ("pe", "dve", "act", "pool", "sp")
    NSLOT = {"sp": 24, "pool": 24}

    def __init__(self, nc):
        self.nc = nc
        self.prog = {k: [] for k in self.ENG}
        self.sem = {k: nc.alloc_semaphore(f"s_{k}") for k in self.ENG}
        self.cnt = {k: 0 for k in self.ENG}
        self.waited = {k: {} for k in self.ENG}
        self.slots = {q: [[nc.alloc_semaphore(f"d_{q}{i}"), 0] for i in range(n)] for q, n in self.NSLOT.items()}
        self.slot_i = {q: 0 for q in self.NSLOT}
        self.semobj = {}
        for k in self.ENG:
            self.semobj[("e", k)] = self.sem[k]
        for q, sl in self.slots.items():
            for i, s in enumerate(sl):
                self.semobj[("d", q, i)] = s[0]
        self.out_events = []
        self.n_inst = 0

    def sb(self, name, shape, dtype):
        return Tok(self.nc.alloc_sbuf_tensor(name, list(shape), dtype), name)

    def ps(self, name, shape, dtype):
        return Tok(self.nc.alloc_psum_tensor(name, list(shape), dtype), name)

    def _wait(self, eng, key, val):
        if eng == "pe" and key == ("e", "pe"):
            return
        w = self.waited[eng]
        if w.get(key, 0) >= val:
            return
        w[key] = val
        self.prog[eng].append(("w", key, val))

    def _deps(self, eng, reads, writes):
        for t in reads:
            for k, v in t.lw.items():
                self._wait(eng, k, v)
        for t in writes:
            for k, v in t.lw.items():
                self._wait(eng, k, v)
            for k, v in t.rd.items():
                self._wait(eng, k, v)

    def _commit(self, ev, reads, writes):
        k, v = ev
        for t in reads:
            if t.rd.get(k, 0) < v:
                t.rd[k] = v
        for t in writes:
            t.lw = {k: v}
            t.rd = {}

    def _ser(self, eng):
        if not os.environ.get("KSER"):
            return
        for k in self.ENG:
            if self.cnt[k] > 0:
                self._wait(eng, ("e", k), self.cnt[k])
        for q, sl in self.slots.items():
            for i, s in enumerate(sl):
                if s[1] > 0:
                    self._wait(eng, ("d", q, i), s[1])

    def op(self, eng, fn, reads=(), writes=()):
        self._ser(eng)
        self._deps(eng, reads, writes)
        self.cnt[eng] += 1
        ev = (("e", eng), self.cnt[eng])
        self.prog[eng].append(("o", _rec(fn), ev[0], 1))
        self._commit(ev, reads, writes)
        self.n_inst += 1
        return ev

    def _dma_common(self, q, fn, reads, writes, out_dram):
        self._ser(q)
        self._deps(q, reads, writes)
        i = self.slot_i[q]
        self.slot_i[q] = (i + 1) % len(self.slots[q])
        s = self.slots[q][i]
        key = ("d", q, i)
        if s[1] > 0:
            self._wait(q, key, s[1])
        s[1] += 16
        ev = (key, s[1])
        self.prog[q].append(("o", _rec(fn), key, 16))
        self._commit(ev, reads, writes)
        if out_dram:
            self.out_events.append(ev)
        self.n_inst += 1
        return ev

    def dma(self, q, out, in_, reads=(), writes=(), out_dram=False, **kw):
        return self._dma_common(q, lambda e: e.dma_start(out=out, in_=in_, **kw), reads, writes, out_dram)

    def gather(self, out, table, idx_ap, reads=(), writes=()):
        def fn(e):
            return e.indirect_dma_start(out=out, out_offset=None, in_=table,
                                        in_offset=bass.IndirectOffsetOnAxis(ap=idx_ap, axis=0))
        return self._dma_common("pool", fn, reads, writes, False)

    def barrier(self):
        evs = [(("e", k), self.cnt[k]) for k in self.ENG if self.cnt[k] > 0]
        for q, sl in self.slots.items():
            for i, s in enumerate(sl):
                if s[1] > 0:
                    evs.append((("d", q, i), s[1]))
        for e in self.ENG:
            for k, v in evs:
                self._wait(e, k, v)

    def finish(self):
        for (k, v) in self.out_events:
            self._wait("sp", k, v)
        nc = self.nc
        S = self
        with nc.Block() as block:
            def replay(eng_name):
                def body(e):
                    for item in S.prog[eng_name]:
                        if item[0] == "w":
                            e.wait_ge(S.semobj[item[1]], item[2])
                        else:
                            _, (name, a, k), key, inc = item
                            getattr(e, name)(*a, **k).then_inc(S.semobj[key], inc)
                return body

            block.tensor(replay("pe"))
            block.vector(replay("dve"))
            block.scalar(replay("act"))
            block.gpsimd(replay("pool"))
            block.sync(replay("sp"))


class Seq:
    def __init__(self, name, T, nt, past):
        self.name, self.T, self.nt, self.past = name, T, nt, past
        self.ntiles = T // nt


class K:
    def __init__(self, SEQ, depth=DEPTH):
        self.SEQ = SEQ
        self.depth = depth
        nc = self.nc = bass.Bass("TRN2", target_bir_lowering=False)
        S = self.S = Sched(nc)
        self.seqs = [Seq("p", SEQ, 128, 0), Seq("s", TS, TS, PAST)]
        self.d = {}
        di = lambda n, sh, dt=F32: self.d.__setitem__(n, nc.dram_tensor(n, list(sh), dt, kind="ExternalInput"))
        do = lambda n, sh, dt=F32: self.d.__setitem__(n, nc.dram_tensor(n, list(sh), dt, kind="ExternalOutput"))
        di("xp", [SEQ, D]); di("xs", [TS, D])
        di("st_hg", [2, 8, 128, 128]); di("st_rh", [1, D]); di("st_rc", [1, 3, D])
        di("c_ckv", [1, PAST, 256]); di("c_kr", [1, PAST, 64])
        di("cst", [128, 272]); di("cos_p", [SEQ, 32]); di("sin_p", [SEQ, 32]); di("cos_s", [TS, 32]); di("sin_s", [TS, 32])
        di("norm_mix", [4, D]); di("norm_ffn", [4, D]); di("norm_final", [1, D])
        di("hg_w_in", [2, D, 4096]); di("hg_lb_logits", [4, D]); di("hg_norm", [2, D]); di("hg_w_out", [2, D, D])
        di("rg_w_in", [1, D, 2048]); di("rg_conv_w", [1, 4, D]); di("rg_conv_b", [1, D])
        di("rg_w_a", [1, 8, 128, 128]); di("rg_b_a", [1, D]); di("rg_w_x", [1, 8, 128, 128]); di("rg_b_x", [1, D])
        di("rg_lambda", [1, D]); di("rg_w_out", [1, D, D])
        di("mla_w_down", [1, D, 704]); di("mla_q_norm", [1, 384]); di("mla_w_uq", [1, 384, 1536])
        di("mla_kv_norm", [1, 256]); di("mla_w_uk", [1, 256, 8, 128]); di("mla_w_uv", [1, 256, 8, 128])
        di("mla_w_out", [1, D, D])
        di("pk_w_query", [4, D, D]); di("pk_sub_keys", [4, 8, 2, 128, 64])
        di("pk_u", [depth, NEXP, D]); di("pk_v", [depth, NEXP, D])
        do("yp", [SEQ, D]); do("ys", [TS, D])
        do("hgp", [2, 8, 128, 128]); do("hgs", [2, 8, 128, 128])
        do("rhp", [1, D]); do("rhs", [1, D]); do("rcp", [1, 3, D]); do("rcs", [1, 3, D])
        do("ckp", [1, SEQ, 256]); do("cks", [1, TS, 256]); do("krp", [1, SEQ, 64]); do("krs", [1, TS, 64])
        self.dbg = os.environ.get("KDBG", "")
        if self.dbg:
            do("dbg", [128, 2048])
        self.scr = {"p": nc.dram_tensor("scr_p", [SEQ, D], F32, kind="Internal"),
                    "s": nc.dram_tensor("scr_s", [TS, D], F32, kind="Internal")}
        self.scr_tok = {(s.name, i): Tok(None, "scr") for s in self.seqs for i in range(s.ntiles)}
        self.ident = S.sb("ident", [128, 128], F32)
        self.maskU = S.sb("maskU", [128, 128], F32)
        self.ones = S.sb("ones", [128, 128], F32)
        self.iota16 = S.sb("iota16", [128, 16], F32)
        self.xb = [S.sb(f"xb{i}", [128, D], F32) for i in range(2)]
        self.xn = S.sb("xn", [128, D], F32)
        self.junk = S.sb("junk", [128, D], BF16)
        self.hT = S.sb("hT", [128, 8, 128], BF16)
        self.st = S.sb("st", [128, 8], F32)
        self.arena = nc.alloc_sbuf_tensor("arena", [128, ARENA], F32)
        self.psA = [S.ps(f"psA{i}", [128, 512], F32) for i in range(4)]
        self.psB = [S.ps(f"psB{i}", [128, 512], F32) for i in range(4)]
        self.psi = 0
        self.xi = 0
        self.consts()
        for L in range(depth):
            kind = L % 3
            S.barrier()
            self.aoff = 0
            if kind == 0:
                self.hgrn_pass(L)
            elif kind == 1:
                self.rglru_pass(L)
            else:
                self.mla_pass(L)
            S.barrier()
            self.aoff = 0
            self.peer_pass(L, last=(L == depth - 1))
        S.finish()

    def av(self, name, shape, dtype):
        n = int(np.prod(shape[1:]))
        n4 = n if dtype in (F32, I32, U32) else (n + 1) // 2
        ap = self.arena[:, self.aoff:self.aoff + n4]
        self.aoff += n4
        assert self.aoff <= ARENA, (name, self.aoff)
        if dtype != F32:
            ap = ap.bitcast(dtype)
            if dtype == BF16 and n % 2:
                ap = ap[:, 0:n]
        if len(shape) == 3:
            ap = ap.rearrange("p (a b) -> p a b", a=shape[1])
        elif len(shape) == 4:
            ap = ap.rearrange("p (a b c) -> p a b c", a=shape[1], b=shape[2])
        return Tok(ap, name)

    def psum(self):
        t = self.psA[self.psi % 4]
        self.psi += 1
        return t

    def mm(self, out_t, out_ap, a_t, a_ap, b_t, b_ap, start=True, stop=True):
        self.S.op("pe", lambda e: e.matmul(out_ap, a_ap, b_ap, start=start, stop=stop),
                  reads=[a_t, b_t], writes=[out_t])

    def tr(self, out_t, out_ap, in_t, in_ap, n_in_part):
        idn = self.ident
        self.S.op("pe", lambda e: e.transpose(out_ap, in_ap, idn.t[:n_in_part, :n_in_part]),
                  reads=[in_t, idn], writes=[out_t])

    def V(self, fn, r=(), w=()):
        self.S.op("dve", fn, r, w)

    def A(self, fn, r=(), w=()):
        self.S.op("act", fn, r, w)

    def consts(self):
        S = self.S
        c = self.d["cst"]
        S.dma("sp", self.ident.t[:], c[:, 0:128], (), [self.ident])
        S.dma("sp", self.maskU.t[:], c[:, 128:256], (), [self.maskU])
        S.dma("sp", self.iota16.t[:], c[:, 256:272], (), [self.iota16])
        self.V(lambda e: e.memset(self.ones.t[:], 1.0), (), [self.ones])

    def bc(self, row_ap, n=128):
        return row_ap.partition_broadcast(n)

    def load_w(self, dst, src_ap):
        kc = dst.t.shape[1]
        N = dst.t.shape[2]
        src = src_ap.rearrange("(kc p) n -> p kc n", p=128)
        q = "pool" if dst.t.dtype != F32 else "sp"
        for n0 in range(0, N, 1024):
            n1 = min(N, n0 + 1024)
            self.S.dma(q, dst.t[:, :, n0:n1], src[:, :, n0:n1], (), [dst])

    def load_x(self, seq, i, L, buf):
        src = (self.d["xp" if seq.name == "p" else "xs"] if L < 0 else self.scr[seq.name])
        r0 = i * seq.nt
        self.S.dma("sp", buf.t[:seq.nt, :], src[r0:r0 + seq.nt, :], [self.scr_tok[(seq.name, i)]], [buf])

    def store_x(self, seq, i, buf):
        r0 = i * seq.nt
        self.S.dma("sp", self.scr[seq.name][r0:r0 + seq.nt, :], buf.t[:seq.nt, :], [buf], [self.scr_tok[(seq.name, i)]])

    def tiles(self, first_layer_src):
        lst = [(s, i) for s in self.seqs for i in range(s.ntiles)]
        bufs = {}
        b0 = self.xb[self.xi % 2]; self.xi += 1
        self.load_x(lst[0][0], lst[0][1], first_layer_src, b0)
        bufs[0] = b0
        for j, (s, i) in enumerate(lst):
            if j + 1 < len(lst):
                b = self.xb[self.xi % 2]; self.xi += 1
                self.load_x(lst[j + 1][0], lst[j + 1][1], first_layer_src, b)
                bufs[j + 1] = b
            yield s, i, bufs.pop(j)

    def rstd(self, src_t, src_ap, nt, n, out_ap, scratch_ap):
        st = self.st
        jk = self.junk
        self.V(lambda e: e.scalar_tensor_tensor(out=scratch_ap, in0=src_ap, scalar=1.0, in1=src_ap, op0=ALU.mult, op1=ALU.mult,
                                                accum_out=st.t[:nt, 0:1]), [src_t], [jk, st])
        self.V(lambda e: e.tensor_scalar(out=st.t[:nt, 1:2], in0=st.t[:nt, 0:1], scalar1=1.0 / n, scalar2=EPS,
                                         op0=ALU.mult, op1=ALU.add), [st], [st])
        self.A(lambda e: e.activation(out=st.t[:nt, 2:3], in_=st.t[:nt, 1:2], func=AF.Sqrt), [st], [st])
        self.V(lambda e: e.reciprocal(out=out_ap, in_=st.t[:nt, 2:3]), [st], [st])

    def norm_T(self, x, gB, nt):
        xn, hT, st = self.xn, self.hT, self.st
        self.rstd(x, x.t[:nt, :], nt, D, st.t[:nt, 3:4], self.junk.t[:nt, :])
        self.V(lambda e: e.scalar_tensor_tensor(out=xn.t[:nt, :], in0=x.t[:nt, :], scalar=st.t[:nt, 3:4], in1=gB.t[:nt, :],
                                                op0=ALU.mult, op1=ALU.mult), [x, st, gB], [xn])
        self.transpose_to(xn, xn.t, nt, 8, hT, lambda kc: hT.t[:, kc, :nt])

    def transpose_to(self, src, src_ap, nt, nch, dst, dst_fn, scale=None):
        for g0 in range(0, nch, 4):
            g1 = min(nch, g0 + 4)
            pb = self.psum()
            for kc in range(g0, g1):
                self.tr(pb, pb.t[:, (kc - g0) * 128:(kc - g0) * 128 + nt], src, src_ap[:nt, kc * 128:(kc + 1) * 128], nt)
            for kc in range(g0, g1):
                o = dst_fn(kc)
                i_ = pb.t[:, (kc - g0) * 128:(kc - g0) * 128 + nt]
                if scale is None:
                    self.V(lambda e, o=o, i_=i_: e.tensor_copy(out=o, in_=i_), [pb], [dst])
                else:
                    self.V(lambda e, o=o, i_=i_: e.tensor_scalar(out=o, in0=i_, scalar1=scale, scalar2=None, op0=ALU.mult), [pb], [dst])

    def gelu_tanh(self, dst_t, dst_ap, src_t, src_ap, tmp_t, tmp_ap):
        self.V(lambda e: e.tensor_tensor(out=tmp_ap, in0=src_ap, in1=src_ap, op=ALU.mult), [src_t], [tmp_t])
        self.V(lambda e: e.tensor_scalar(out=tmp_ap, in0=tmp_ap, scalar1=GC1, scalar2=1.0, op0=ALU.mult, op1=ALU.add), [tmp_t], [tmp_t])
        self.V(lambda e: e.tensor_tensor(out=tmp_ap, in0=tmp_ap, in1=src_ap, op=ALU.mult), [tmp_t, src_t], [tmp_t])
        self.A(lambda e: e.activation(out=tmp_ap, in_=tmp_ap, func=AF.Sigmoid, scale=GC2), [tmp_t], [tmp_t])
        self.V(lambda e: e.tensor_tensor(out=dst_ap, in0=tmp_ap, in1=src_ap, op=ALU.mult), [tmp_t, src_t], [dst_t])

    def add_resid(self, x, nt, ybanks):
        for j, pb in enumerate(ybanks):
            self.V(lambda e, pb=pb, j=j: e.tensor_tensor(out=x.t[:nt, j * 512:(j + 1) * 512], in0=x.t[:nt, j * 512:(j + 1) * 512],
                                                         in1=pb.t[:nt, :], op=ALU.add), [x, pb], [x])

    def out_proj(self, lhs_t, lhs_fn, w, nt, x):
        banks = []
        for j in range(2):
            pb = self.psum()
            for kc in range(8):
                self.mm(pb, pb.t[:nt, :], lhs_t, lhs_fn(kc), w, w.t[:, kc, j * 512:(j + 1) * 512], kc == 0, kc == 7)
            banks.append(pb)
        self.add_resid(x, nt, banks)

    def peer_pass(self, L, last):
        S, d = self.S, self.d
        wq = self.av("wq", [128, 8, D], BF16)
        BD = self.av("BD", [128, 8, 256], F32)
        gB = self.av("gffn", [128, D], F32)
        kraw = self.av("kraw", [128, 16, 64], F32)
        G = [self.av(f"G{i}", [128, D], F32) for i in range(8)]
        bufA = self.av("bufA", [128, 2048], F32)
        bufB = self.av("bufB", [128, 2048], F32)
        bufC = self.av("bufC", [128, 2048], F32)
        qT = self.av("qT", [128, 8, 128], F32)
        sv = self.av("sv", [128, 8, 2, 16], F32)
        si = self.av("si", [128, 8, 2, 16], U32)
        sif = self.av("sif", [128, 8, 2, 16], F32)
        ts = self.av("ts", [128, 8, 16], F32)
        tj = self.av("tj", [128, 8, 16], U32)
        ab = self.av("ab", [128, 2, 128], I32)
        abf = self.av("abf", [128, 2, 128], F32)
        sel = self.av("sel", [128, 2, 128], F32)
        eidx = self.av("eidx", [128, 128], I32)
        gate = self.av("gate", [128, 8, 16], F32)
        z = self.av("z", [128, 2, 8], F32)
        actp = self.av("actp", [128, 128], F32)
        acth = self.av("acth", [128, 128], F32)
        wgt = self.av("wgt", [128, 128], F32)
        if last:
            self.gfin = self.av("gfin", [128, D], F32)
            S.dma("sp", self.gfin.t[:], self.bc(d["norm_final"][0, :]), (), [self.gfin])
        self.load_w(wq, d["pk_w_query"][L])
        S.dma("sp", gB.t[:], self.bc(d["norm_ffn"][L, :]), (), [gB])
        S.dma("sp", kraw.t[:], d["pk_sub_keys"][L].rearrange("h p k d -> k (h p) d"), (), [kraw])
        self.V(lambda e: e.memset(BD.t[:], 0.0), (), [BD])
        for h in range(8):
            pb = self.psum()
            self.tr(pb, pb.t[:, 0:128], kraw, kraw.t[:, 2 * h:2 * h + 2, :].rearrange("k a d -> k (a d)"), 128)
            self.V(lambda e, h=h, pb=pb: e.tensor_copy(out=BD.t[0:64, h, 0:128], in_=pb.t[0:64, 0:128]), [pb], [BD])
            self.V(lambda e, h=h, pb=pb: e.tensor_copy(out=BD.t[64:128, h, 128:256], in_=pb.t[64:128, 0:128]), [pb], [BD])
        pku = d["pk_u"].ap().rearrange("l e d -> (l e) d")
        pkv = d["pk_v"].ap().rearrange("l e d -> (l e) d")
        gi = 0
        for seq, i, x in self.tiles(L):
            nt = seq.nt
            self.norm_T(x, gB, nt)
            xn, hT = self.xn, self.hT
            if "nopeer" in self.dbg:
                self.peer_tail(seq, i, x, nt, last)
                continue
            for c in range(8):
                pb = self.psum()
                for kc in range(8):
                    self.mm(pb, pb.t[:, :nt], wq, wq.t[:, kc, c * 128:(c + 1) * 128], hT, hT.t[:, kc, :nt], kc == 0, kc == 7)
                self.V(lambda e, c=c, pb=pb: e.tensor_copy(out=qT.t[:, c, :nt], in_=pb.t[:, :nt]), [pb], [qT])
            for j in range(4):
                pb = self.psum()
                for hh in range(2):
                    h = 2 * j + hh
                    self.mm(pb, pb.t[:nt, hh * 256:(hh + 1) * 256], qT, qT.t[:, h, :nt], BD, BD.t[:, h, :])
                self.V(lambda e, j=j, pb=pb: e.tensor_copy(out=bufA.t[:nt, j * 512:(j + 1) * 512], in_=pb.t[:nt, :]), [pb], [bufA])
            sW = bufA.t.rearrange("p (g k) -> p g k", g=16)
            sW2 = bufB.t.rearrange("p (g k) -> p g k", g=16)
            for g in range(16):
                h, p = g // 2, g % 2
                self.V(lambda e, g=g, h=h, p=p: e.max(out=sv.t[:nt, h, p, 0:8], in_=sW[:nt, g, :]), [bufA], [sv])
                self.V(lambda e, g=g, h=h, p=p: e.match_replace(out=sW2[:nt, g, :], in_to_replace=sv.t[:nt, h, p, 0:8],
                                                               in_values=sW[:nt, g, :], imm_value=-3.0e38), [bufA, sv], [bufB])
                self.V(lambda e, g=g, h=h, p=p: e.max(out=sv.t[:nt, h, p, 8:16], in_=sW2[:nt, g, :]), [bufB], [sv])
                self.V(lambda e, g=g, h=h, p=p: e.max_index(out=si.t[:nt, h, p, 0:8], in_max=sv.t[:nt, h, p, 0:8],
                                                           in_values=sW[:nt, g, :]), [bufA, sv], [si])
                self.V(lambda e, g=g, h=h, p=p: e.max_index(out=si.t[:nt, h, p, 8:16], in_max=sv.t[:nt, h, p, 8:16],
                                                           in_values=sW2[:nt, g, :]), [bufB, sv], [si])
            cand = bufA.t.rearrange("p (h a b) -> p h a b", h=8, a=16)
            cand2 = bufB.t.rearrange("p (h a b) -> p h a b", h=8, a=16)
            candf = bufA.t.rearrange("p (h n) -> p h n", h=8)
            cand2f = bufB.t.rearrange("p (h n) -> p h n", h=8)
            self.V(lambda e: e.tensor_tensor(out=cand[:nt], in0=sv.t[:nt, :, 0, :].unsqueeze(3).broadcast_to([nt, 8, 16, 16]),
                                             in1=sv.t[:nt, :, 1, :].unsqueeze(2).broadcast_to([nt, 8, 16, 16]), op=ALU.add), [sv], [bufA])
            for h in range(8):
                self.V(lambda e, h=h: e.max(out=ts.t[:nt, h, 0:8], in_=candf[:nt, h, :]), [bufA], [ts])
                self.V(lambda e, h=h: e.match_replace(out=cand2f[:nt, h, :], in_to_replace=ts.t[:nt, h, 0:8],
                                                      in_values=candf[:nt, h, :], imm_value=-3.0e38), [bufA, ts], [bufB])
                self.V(lambda e, h=h: e.max(out=ts.t[:nt, h, 8:16], in_=cand2f[:nt, h, :]), [bufB], [ts])
                self.V(lambda e, h=h: e.max_index(out=tj.t[:nt, h, 0:8], in_max=ts.t[:nt, h, 0:8], in_values=candf[:nt, h, :]), [bufA, ts], [tj])
                self.V(lambda e, h=h: e.max_index(out=tj.t[:nt, h, 8:16], in_max=ts.t[:nt, h, 8:16], in_values=cand2f[:nt, h, :]), [bufB, ts], [tj])
            self.V(lambda e: e.tensor_tensor(out=gate.t[:nt], in0=ts.t[:nt], in1=ts.t[:nt, :, 0:1].broadcast_to([nt, 8, 16]),
                                             op=ALU.subtract), [ts], [gate])
            self.A(lambda e: e.activation(out=gate.t[:nt], in_=gate.t[:nt], func=AF.Exp), [gate], [gate])
            self.V(lambda e: e.tensor_reduce(out=z.t[:nt, 0, :], in_=gate.t[:nt], axis=AX.X, op=ALU.add), [gate], [z])
            self.V(lambda e: e.reciprocal(out=z.t[:nt, 1, :], in_=z.t[:nt, 0, :]), [z], [z])
            self.V(lambda e: e.tensor_tensor(out=gate.t[:nt], in0=gate.t[:nt], in1=z.t[:nt, 1, :].unsqueeze(2).broadcast_to([nt, 8, 16]),
                                             op=ALU.mult), [gate, z], [gate])
            tji = tj.t.bitcast(I32).rearrange("p h k -> p (h k)")
            self.V(lambda e: e.tensor_single_scalar(out=ab.t[:nt, 0, :], in_=tji[:nt], scalar=4, op=ALU.arith_shift_right), [tj], [ab])
            self.V(lambda e: e.tensor_single_scalar(out=ab.t[:nt, 1, :], in_=tji[:nt], scalar=15, op=ALU.bitwise_and), [tj], [ab])
            self.V(lambda e: e.tensor_copy(out=abf.t[:nt], in_=ab.t[:nt]), [ab], [abf])
            self.V(lambda e: e.tensor_copy(out=sif.t[:nt], in_=si.t[:nt]), [si], [sif])
            oh = bufC.t.rearrange("p (r a) -> p r a", a=16)
            oh4 = bufC.t.rearrange("p (h k a) -> p h k a", h=8, k=16)
            for w_ in range(2):
                self.V(lambda e, w_=w_: e.tensor_tensor(out=oh[:nt], in0=abf.t[:nt, w_, :].unsqueeze(2).broadcast_to([nt, 128, 16]),
                                                        in1=self.iota16.t[:nt].unsqueeze(1).broadcast_to([nt, 128, 16]), op=ALU.is_equal),
                       [abf, self.iota16], [bufC])
                self.V(lambda e, w_=w_: e.tensor_tensor(out=oh4[:nt], in0=oh4[:nt],
                                                        in1=sif.t[:nt, :, w_, :].unsqueeze(2).broadcast_to([nt, 8, 16, 16]), op=ALU.mult),
                       [sif, bufC], [bufC])
                self.V(lambda e, w_=w_: e.tensor_reduce(out=sel.t[:nt, w_, :], in_=oh[:nt], axis=AX.X, op=ALU.add), [bufC], [sel])
            self.V(lambda e: e.scalar_tensor_tensor(out=abf.t[:nt, 0, :], in0=sel.t[:nt, 0, :], scalar=128.0, in1=sel.t[:nt, 1, :],
                                                    op0=ALU.mult, op1=ALU.add), [sel], [abf])
            if L:
                self.V(lambda e: e.tensor_scalar(out=abf.t[:nt, 0, :], in0=abf.t[:nt, 0, :], scalar1=float(L * NEXP), scalar2=None, op0=ALU.add), [abf], [abf])
            self.V(lambda e: e.tensor_copy(out=eidx.t[:nt], in_=abf.t[:nt, 0, :]), [abf], [eidx])
            if "nogather" in self.dbg:
                if i == 0 and seq.name == "p":
                    S.dma("sp", d["dbg"][:, 0:128], abf.t[:, 0, :], [abf], (), out_dram=True)
                    S.dma("sp", d["dbg"][:, 128:256], gate.t[:].rearrange("p h k -> p (h k)"), [gate], (), out_dram=True)
                    S.dma("sp", d["dbg"][:, 256:512], sv.t[:].rearrange("p h a k -> p (h a k)"), [sv], (), out_dram=True)
                    S.dma("sp", d["dbg"][:, 512:768], sif.t[:].rearrange("p h a k -> p (h a k)"), [sif], (), out_dram=True)
                    S.dma("sp", d["dbg"][:, 768:896], ts.t[:].rearrange("p h k -> p (h k)"), [ts], (), out_dram=True)
                self.peer_tail(seq, i, x, nt, last)
                continue
            for r in range(128):
                g_ = G[gi % 8]; gi += 1
                S.gather(g_.t[:nt, :], pku, eidx.t[:nt, r:r + 1], [eidx], [g_])
                self.V(lambda e, g_=g_, r=r: e.scalar_tensor_tensor(out=self.junk.t[:nt, :], in0=g_.t[:nt, :], scalar=1.0, in1=xn.t[:nt, :],
                                                                   op0=ALU.mult, op1=ALU.mult, accum_out=actp.t[:nt, r:r + 1]),
                       [g_, xn], [self.junk, actp])
            self.gelu_tanh(acth, acth.t[:nt], actp, actp.t[:nt], wgt, wgt.t[:nt])
            self.V(lambda e: e.tensor_tensor(out=wgt.t[:nt], in0=acth.t[:nt], in1=gate.t[:nt].rearrange("p h k -> p (h k)"), op=ALU.mult),
                   [acth, gate], [wgt])
            for r in range(128):
                g_ = G[gi % 8]; gi += 1
                S.gather(g_.t[:nt, :], pkv, eidx.t[:nt, r:r + 1], [eidx], [g_])
                self.V(lambda e, g_=g_, r=r: e.scalar_tensor_tensor(out=x.t[:nt, :], in0=g_.t[:nt, :], scalar=wgt.t[:nt, r:r + 1], in1=x.t[:nt, :],
                                                                   op0=ALU.mult, op1=ALU.add), [g_, wgt, x], [x])
            self.peer_tail(seq, i, x, nt, last)

    def peer_tail(self, seq, i, x, nt, last):
        S, d, xn = self.S, self.d, self.xn
        if True:
            if not last:
                self.store_x(seq, i, x)
            else:
                self.rstd(x, x.t[:nt, :], nt, D, self.st.t[:nt, 3:4], self.junk.t[:nt, :])
                self.V(lambda e: e.scalar_tensor_tensor(out=xn.t[:nt, :], in0=x.t[:nt, :], scalar=self.st.t[:nt, 3:4], in1=self.gfin.t[:nt, :],
                                                        op0=ALU.mult, op1=ALU.mult), [x, self.st, self.gfin], [xn])
                dst = d["yp" if seq.name == "p" else "ys"]
                S.dma("sp", dst[i * nt:(i + 1) * nt, :], xn.t[:nt, :], [xn], (), out_dram=True)

    def hgrn_pass(self, L):
        S, d = self.S, self.d
        slot = L // 3
        w_in = self.av("hg_w_in", [128, 8, 4096], BF16)
        w_out = self.av("hg_w_out", [128, 8, D], BF16)
        gB = self.av("gmix", [128, D], F32)
        gnB = self.av("gn", [128, D], F32)
        St = self.av("Sst", [128, 8, 128], F32)
        Sr = self.av("Sr", [128, 128], BF16)
        lbl = self.av("lbl", [128, 4, 8], F32)
        oml = self.av("oml", [128, 8], F32)
        qTa = self.av("qTa", [128, 8, 128], F32)
        kTa = self.av("kTa", [128, 8, 128], F32)
        lfa = self.av("lfa", [128, 8, 128], F32)
        bcA = self.av("bcA", [128, 8, 64], F32)
        dA = self.av("dA", [128, 8, 64], F32)
        exA = self.av("exA", [128, 3, 8, 64], F32)
        scA = self.av("scA", [128, 8, 2], F32)
        sc5 = self.av("sc5", [128, 8], F32)
        qt = self.av("qt", [128, 64], BF16)
        kt = self.av("kt", [128, 64], BF16)
        kh = self.av("kh", [128, 64], F32)
        khat = self.av("khat", [64, 128], BF16)
        scT = self.av("scT", [64, 64], BF16)
        iv = self.av("iv", [64, D], BF16)
        sg = self.av("sg", [64, D], F32)
        o_sb = self.av("o_sb", [64, D], F32)
        onT = self.av("onT", [128, 8, 128], BF16)
        self.load_w(w_in, d["hg_w_in"][slot])
        self.load_w(w_out, d["hg_w_out"][slot])
        S.dma("sp", gB.t[:], self.bc(d["norm_mix"][L, :]), (), [gB])
        S.dma("sp", gnB.t[:], self.bc(d["hg_norm"][slot, :]), (), [gnB])
        for l_ in range(4):
            S.dma("sp", lbl.t[:, l_, :], d["hg_lb_logits"][l_, :].rearrange("(h p) -> p h", p=128), (), [lbl], allow_slow_non_contiguous=True)
        self.A(lambda e: e.activation(out=lbl.t[:], in_=lbl.t[:], func=AF.Exp), [lbl], [lbl])
        self.V(lambda e: e.tensor_reduce(out=oml.t[:], in_=lbl.t[:].rearrange("p l h -> p h l"), axis=AX.X, op=ALU.add), [lbl], [oml])
        self.V(lambda e: e.reciprocal(out=oml.t[:], in_=oml.t[:]), [oml], [oml])
        if L == 0:
            self.V(lambda e: e.memset(oml.t[:], 1.0), (), [oml])
        else:
            self.V(lambda e: e.tensor_reduce(out=sc5.t[:], in_=lbl.t[:, 1:L + 1, :].rearrange("p l h -> p h l"), axis=AX.X, op=ALU.add), [lbl], [sc5])
            self.V(lambda e: e.tensor_tensor(out=sc5.t[:], in0=sc5.t[:], in1=oml.t[:], op=ALU.mult), [sc5, oml], [sc5])
            self.V(lambda e: e.tensor_scalar(out=oml.t[:], in0=sc5.t[:], scalar1=-1.0, scalar2=1.0, op0=ALU.mult, op1=ALU.add), [sc5], [oml])
        cur = None
        for seq, i, x in self.tiles(L - 1 if L == 0 else L):
            nt = seq.nt
            if cur is not seq:
                if cur is not None:
                    self.hg_out(cur, slot, St)
                cur = seq
                if seq.name == "p":
                    self.V(lambda e: e.memset(St.t[:], 0.0), (), [St])
                else:
                    S.dma("sp", St.t[:], d["st_hg"][slot].rearrange("h f i -> f h i"), (), [St])
            self.norm_T(x, gB, nt)
            hT = self.hT
            for h in range(8):
                pq = self.psum()
                pf_ = self.psum()
                for kc in range(8):
                    self.mm(pq, pq.t[:, :nt], w_in, w_in.t[:, kc, h * 128:(h + 1) * 128], hT, hT.t[:, kc, :nt], kc == 0, kc == 7)
                for kc in range(8):
                    self.mm(pf_, pf_.t[:, :nt], w_in, w_in.t[:, kc, 1024 + h * 128:1024 + (h + 1) * 128], hT, hT.t[:, kc, :nt], kc == 0, kc == 7)
                self.V(lambda e, h=h, pq=pq: e.tensor_copy(out=qTa.t[:, h, :nt], in_=pq.t[:, :nt]), [pq], [qTa])
                self.V(lambda e, h=h, pf_=pf_: e.tensor_copy(out=kTa.t[:, h, :nt], in_=pf_.t[:, :nt]), [pf_], [kTa])
            self.A(lambda e: e.activation(out=qTa.t[:, :, :nt], in_=qTa.t[:, :, :nt], func=AF.Silu), [qTa], [qTa])
            self.A(lambda e: e.activation(out=kTa.t[:, :, :nt], in_=kTa.t[:, :, :nt], func=AF.Sigmoid, scale=-1.0), [kTa], [kTa])
            self.V(lambda e: e.tensor_tensor(out=kTa.t[:, :, :nt], in0=kTa.t[:, :, :nt], in1=oml.t[:, :].unsqueeze(2).broadcast_to([128, 8, nt]),
                                             op=ALU.mult), [kTa, oml], [kTa])
            self.A(lambda e: e.activation(out=lfa.t[:, :, :nt], in_=kTa.t[:, :, :nt], func=AF.Ln, scale=-1.0, bias=1.0), [kTa], [lfa])
            CL = min(int(os.environ.get("KCL", "64")), nt)
            for c0 in range(0, nt, CL):
                for j in range(4):
                    pb = self.psum()
                    for kc in range(8):
                        self.mm(pb, pb.t[:CL, :], hT, hT.t[:, kc, c0:c0 + CL], w_in, w_in.t[:, kc, 2048 + j * 512:2048 + (j + 1) * 512], kc == 0, kc == 7)
                    if j < 2:
                        self.V(lambda e, j=j, pb=pb: e.tensor_copy(out=iv.t[:CL, j * 512:(j + 1) * 512], in_=pb.t[:CL, :]), [pb], [iv])
                    else:
                        self.V(lambda e, j=j, pb=pb: e.tensor_copy(out=sg.t[:CL, (j - 2) * 512:(j - 1) * 512], in_=pb.t[:CL, :]), [pb], [sg])
                self.A(lambda e: e.activation(out=sg.t[:CL, :], in_=sg.t[:CL, :], func=AF.Silu), [sg], [sg])
                mid = CL // 2 - 1
                for h in range(8):
                    self.V(lambda e, h=h: e.tensor_tensor_scan(out=bcA.t[:, h, :CL], data0=self.ones.t[:, :CL], data1=lfa.t[:, h, c0:c0 + CL],
                                                               initial=0.0, op0=ALU.mult, op1=ALU.add), [self.ones, lfa], [bcA])
                self.V(lambda e: e.tensor_copy(out=scA.t[:, :, 0:1], in_=bcA.t[:, :, mid:mid + 1]), [bcA], [scA])
                self.V(lambda e: e.tensor_copy(out=scA.t[:, :, 1:2], in_=bcA.t[:, :, CL - 1:CL]), [bcA], [scA])
                self.V(lambda e: e.tensor_tensor(out=dA.t[:, :, :CL], in0=bcA.t[:, :, :CL], in1=scA.t[:, :, 0:1].broadcast_to([128, 8, CL]),
                                                 op=ALU.subtract), [bcA, scA], [dA])
                self.A(lambda e: e.activation(out=exA.t[:, 0, :, :CL], in_=dA.t[:, :, :CL], func=AF.Exp), [dA], [exA])
                self.A(lambda e: e.activation(out=exA.t[:, 1, :, :CL], in_=dA.t[:, :, :CL], func=AF.Exp, scale=-1.0), [dA], [exA])
                self.V(lambda e: e.tensor_tensor(out=dA.t[:, :, :CL], in0=bcA.t[:, :, :CL], in1=scA.t[:, :, 1:2].broadcast_to([128, 8, CL]),
                                                 op=ALU.subtract), [bcA, scA], [dA])
                self.A(lambda e: e.activation(out=exA.t[:, 2, :, :CL], in_=dA.t[:, :, :CL], func=AF.Exp, scale=-1.0), [dA], [exA])
                self.A(lambda e: e.activation(out=scA.t[:, :, :], in_=scA.t[:, :, :], func=AF.Exp), [scA], [scA])
                for h in range(8):
                    self.V(lambda e, h=h: e.tensor_tensor(out=qt.t[:, :CL], in0=qTa.t[:, h, c0:c0 + CL], in1=exA.t[:, 0, h, :CL], op=ALU.mult), [qTa, exA], [qt])
                    self.V(lambda e, h=h: e.tensor_tensor(out=kt.t[:, :CL], in0=kTa.t[:, h, c0:c0 + CL], in1=exA.t[:, 1, h, :CL], op=ALU.mult), [kTa, exA], [kt])
                    self.V(lambda e, h=h: e.tensor_tensor(out=kh.t[:, :CL], in0=kTa.t[:, h, c0:c0 + CL], in1=exA.t[:, 2, h, :CL], op=ALU.mult), [kTa, exA], [kh])
                    self.V(lambda e, h=h: e.tensor_scalar(out=Sr.t[:], in0=St.t[:, h, :], scalar1=scA.t[:, h, 0:1], scalar2=None, op0=ALU.mult), [St, scA], [Sr])
                    pk = self.psum()
                    self.tr(pk, pk.t[:CL, 0:128], kh, kh.t[:, :CL], 128)
                    self.V(lambda e, pk=pk: e.tensor_copy(out=khat.t[:CL, :], in_=pk.t[:CL, 0:128]), [pk], [khat])
                    psc = self.psum()
                    self.mm(psc, psc.t[:CL, :CL], kt, kt.t[:, :CL], qt, qt.t[:, :CL])
                    self.V(lambda e, psc=psc: e.tensor_tensor(out=scT.t[:CL, :CL], in0=psc.t[:CL, :CL], in1=self.maskU.t[:CL, :CL], op=ALU.mult),
                           [psc, self.maskU], [scT])
                    po = self.psum()
                    self.mm(po, po.t[:CL, 0:128], scT, scT.t[:CL, :CL], iv, iv.t[:CL, h * 128:(h + 1) * 128], True, False)
                    self.mm(po, po.t[:CL, 0:128], qt, qt.t[:, :CL], Sr, Sr.t[:], False, True)
                    self.V(lambda e, h=h, po=po: e.tensor_copy(out=o_sb.t[:CL, h * 128:(h + 1) * 128], in_=po.t[:CL, 0:128]), [po], [o_sb])
                    pn = self.psum()
                    self.mm(pn, pn.t[:, 0:128], khat, khat.t[:CL, :], iv, iv.t[:CL, h * 128:(h + 1) * 128])
                    self.V(lambda e, h=h, pn=pn: e.scalar_tensor_tensor(out=St.t[:, h, :], in0=St.t[:, h, :], scalar=scA.t[:, h, 1:2], in1=pn.t[:, 0:128],
                                                                       op0=ALU.mult, op1=ALU.add), [St, scA, pn], [St])
                self.rstd(o_sb, o_sb.t[:CL, :], CL, D, self.st.t[:CL, 4:5], self.junk.t[:CL, :])
                self.V(lambda e: e.scalar_tensor_tensor(out=o_sb.t[:CL, :], in0=o_sb.t[:CL, :], scalar=self.st.t[:CL, 4:5], in1=gnB.t[:CL, :],
                                                        op0=ALU.mult, op1=ALU.mult), [o_sb, self.st, gnB], [o_sb])
                self.V(lambda e: e.tensor_tensor(out=o_sb.t[:CL, :], in0=o_sb.t[:CL, :], in1=sg.t[:CL, :], op=ALU.mult), [o_sb, sg], [o_sb])
                self.transpose_to(o_sb, o_sb.t, CL, 8, onT, lambda kc, c0=c0: onT.t[:, kc, c0:c0 + CL])
            self.out_proj(onT, lambda kc: onT.t[:, kc, :nt], w_out, nt, x)
            self.store_x(seq, i, x)
        self.hg_out(cur, slot, St)

    def hg_out(self, seq, slot, St):
        dst = self.d["hgp" if seq.name == "p" else "hgs"]
        self.S.dma("sp", dst[slot].rearrange("h f i -> f h i"), St.t[:], [St], (), out_dram=True)

    def rglru_pass(self, L):
        S, d = self.S, self.d
        w_in = self.av("rg_w_in", [128, 8, 2048], BF16)
        w_out = self.av("rg_w_out", [128, 8, D], BF16)
        wa = self.av("rg_wa", [128, 8, 128], F32)
        wx = self.av("rg_wx", [128, 8, 128], F32)
        gB = self.av("gmix", [128, D], F32)
        cw = self.av("cw", [128, 4, 8], F32)
        pv = self.av("pv", [128, 5, 8], F32)
        ub = self.av("ub", [128, 8, 3 + 128], F32)
        hprev = self.av("hprev", [128, 8], F32)
        g1 = self.av("g1", [128, 8, 128], F32)
        tmp = self.av("tmp", [128, 8, 128], F32)
        gel = self.av("gel", [128, 8, 128], F32)
        xc = self.av("xc", [128, 8, 128], F32)
        rr = self.av("rr", [128, 8, 128], F32)
        ig = self.av("ig", [128, 8, 128], F32)
        aa = self.av("aa", [128, 8, 128], F32)
        mm_ = self.av("mm_", [128, 8, 128], F32)
        hb = self.av("hb", [128, 8, 128], F32)
        hgT = self.av("hgT", [128, 8, 128], BF16)
        self.load_w(w_in, d["rg_w_in"][0])
        self.load_w(w_out, d["rg_w_out"][0])
        S.dma("sp", wa.t[:], d["rg_w_a"][0].rearrange("n c d -> c n d"), (), [wa])
        S.dma("sp", wx.t[:], d["rg_w_x"][0].rearrange("n c d -> c n d"), (), [wx])
        S.dma("sp", gB.t[:], self.bc(d["norm_mix"][L, :]), (), [gB])
        for j_ in range(4):
            S.dma("sp", cw.t[:, j_, :], d["rg_conv_w"][0][j_, :].rearrange("(n p) -> p n", p=128), (), [cw], allow_slow_non_contiguous=True)
        for k_, nm in enumerate(["rg_conv_b", "rg_b_a", "rg_b_x", "rg_lambda"]):
            S.dma("sp", pv.t[:, k_, :], d[nm][0].rearrange("(n p) -> p n", p=128), (), [pv], allow_slow_non_contiguous=True)
        self.A(lambda e: e.activation(out=pv.t[:, 3, :], in_=pv.t[:, 3, :], func=AF.Exp, scale=-1.0), [pv], [pv])
        self.A(lambda e: e.activation(out=pv.t[:, 3, :], in_=pv.t[:, 3, :], func=AF.Ln, bias=1.0), [pv], [pv])
        self.V(lambda e: e.tensor_scalar(out=pv.t[:, 4, :], in0=pv.t[:, 3, :], scalar1=-16.0, scalar2=None, op0=ALU.mult), [pv], [pv])
        self.V(lambda e: e.tensor_scalar(out=pv.t[:, 3, :], in0=pv.t[:, 3, :], scalar1=-8.0, scalar2=None, op0=ALU.mult), [pv], [pv])
        cur = None
        for seq, i, x in self.tiles(L):
            nt = seq.nt
            if cur is not seq:
                if cur is not None:
                    self.rg_out(cur, hprev, ub)
                cur = seq
                if seq.name == "p":
                    self.V(lambda e: e.memset(ub.t[:], 0.0), (), [ub])
                    self.V(lambda e: e.memset(hprev.t[:], 0.0), (), [hprev])
                else:
                    S.dma("sp", hprev.t[:], d["st_rh"][0].rearrange("(n p) -> p n", p=128), (), [hprev], allow_slow_non_contiguous=True)
                    for j_ in range(3):
                        S.dma("sp", ub.t[:, :, j_], d["st_rc"][0][j_, :].rearrange("(n p) -> p n", p=128), (), [ub], allow_slow_non_contiguous=True)
            self.norm_T(x, gB, nt)
            hT = self.hT
            for cc in range(8):
                pg = self.psum()
                pu = self.psum()
                for kc in range(8):
                    self.mm(pg, pg.t[:, :nt], w_in, w_in.t[:, kc, cc * 128:(cc + 1) * 128], hT, hT.t[:, kc, :nt], kc == 0, kc == 7)
                for kc in range(8):
                    self.mm(pu, pu.t[:, :nt], w_in, w_in.t[:, kc, 1024 + cc * 128:1024 + (cc + 1) * 128], hT, hT.t[:, kc, :nt], kc == 0, kc == 7)
                self.V(lambda e, pg=pg, cc=cc: e.tensor_copy(out=g1.t[:, cc, :nt], in_=pg.t[:, :nt]), [pg], [g1])
                self.V(lambda e, pu=pu, cc=cc: e.tensor_copy(out=ub.t[:, cc, 3:3 + nt], in_=pu.t[:, :nt]), [pu], [ub])
                self.V(lambda e, cc=cc: e.tensor_scalar(out=xc.t[:, cc, :nt], in0=ub.t[:, cc, 0:nt], scalar1=cw.t[:, 0, cc:cc + 1], scalar2=pv.t[:, 0, cc:cc + 1],
                                                        op0=ALU.mult, op1=ALU.add), [ub, cw, pv], [xc])
                for j in range(1, 4):
                    self.V(lambda e, cc=cc, j=j: e.scalar_tensor_tensor(out=xc.t[:, cc, :nt], in0=ub.t[:, cc, j:j + nt], scalar=cw.t[:, j, cc:cc + 1],
                                                                       in1=xc.t[:, cc, :nt], op0=ALU.mult, op1=ALU.add), [ub, cw, xc], [xc])
                pr = self.psum()
                pi_ = self.psum()
                self.mm(pr, pr.t[:, :nt], wa, wa.t[:, cc, :], xc, xc.t[:, cc, :nt])
                self.mm(pi_, pi_.t[:, :nt], wx, wx.t[:, cc, :], xc, xc.t[:, cc, :nt])
                self.V(lambda e, cc=cc, pr=pr: e.tensor_scalar(out=rr.t[:, cc, :nt], in0=pr.t[:, :nt], scalar1=pv.t[:, 1, cc:cc + 1], scalar2=None, op0=ALU.add), [pr, pv], [rr])
                self.V(lambda e, cc=cc, pi_=pi_: e.tensor_scalar(out=ig.t[:, cc, :nt], in0=pi_.t[:, :nt], scalar1=pv.t[:, 2, cc:cc + 1], scalar2=None, op0=ALU.add), [pi_, pv], [ig])
            self.gelu_tanh(gel, gel.t[:, :, :nt], g1, g1.t[:, :, :nt], tmp, tmp.t[:, :, :nt])
            self.A(lambda e: e.activation(out=rr.t[:, :, :nt], in_=rr.t[:, :, :nt], func=AF.Sigmoid), [rr], [rr])
            self.A(lambda e: e.activation(out=ig.t[:, :, :nt], in_=ig.t[:, :, :nt], func=AF.Sigmoid), [ig], [ig])
            self.V(lambda e: e.tensor_tensor(out=rr.t[:, :, :nt], in0=rr.t[:, :, :nt], in1=pv.t[:, 3, :].unsqueeze(2).broadcast_to([128, 8, nt]), op=ALU.mult), [rr, pv], [rr])
            self.A(lambda e: e.activation(out=aa.t[:, :, :nt], in_=rr.t[:, :, :nt], func=AF.Exp), [rr], [aa])
            self.A(lambda e: e.activation(out=mm_.t[:, :, :nt], in_=rr.t[:, :, :nt], func=AF.Exp, scale=2.0), [rr], [mm_])
            self.A(lambda e: e.activation(out=mm_.t[:, :, :nt], in_=mm_.t[:, :, :nt], func=AF.Sqrt, scale=-1.0, bias=1.0), [mm_], [mm_])
            self.V(lambda e: e.tensor_tensor(out=ig.t[:, :, :nt], in0=ig.t[:, :, :nt], in1=xc.t[:, :, :nt], op=ALU.mult), [ig, xc], [ig])
            self.V(lambda e: e.tensor_tensor(out=ig.t[:, :, :nt], in0=ig.t[:, :, :nt], in1=mm_.t[:, :, :nt], op=ALU.mult), [ig, mm_], [ig])
            for cc in range(8):
                self.V(lambda e, cc=cc: e.tensor_tensor_scan(out=hb.t[:, cc, :nt], data0=aa.t[:, cc, :nt], data1=ig.t[:, cc, :nt], initial=hprev.t[:, cc:cc + 1],
                                                             op0=ALU.mult, op1=ALU.add), [aa, ig, hprev], [hb])
                self.V(lambda e, cc=cc: e.tensor_copy(out=hprev.t[:, cc:cc + 1], in_=hb.t[:, cc, nt - 1:nt]), [hb], [hprev])
            self.V(lambda e: e.tensor_tensor(out=hgT.t[:, :, :nt], in0=hb.t[:, :, :nt], in1=gel.t[:, :, :nt], op=ALU.mult), [hb, gel], [hgT])
            self.V(lambda e: e.tensor_copy(out=tmp.t[:, :, 0:3], in_=ub.t[:, :, nt:nt + 3]), [ub], [tmp])
            self.V(lambda e: e.tensor_copy(out=ub.t[:, :, 0:3], in_=tmp.t[:, :, 0:3]), [tmp], [ub])
            self.out_proj(hgT, lambda kc: hgT.t[:, kc, :nt], w_out, nt, x)
            self.store_x(seq, i, x)
        self.rg_out(cur, hprev, ub)

    def rg_out(self, seq, hprev, ub):
        d = self.d
        p = seq.name == "p"
        self.S.dma("sp", d["rhp" if p else "rhs"][0].rearrange("(n p) -> p n", p=128), hprev.t[:], [hprev], (), out_dram=True, allow_slow_non_contiguous=True)
        for j_ in range(3):
            self.S.dma("sp", d["rcp" if p else "rcs"][0][j_, :].rearrange("(n p) -> p n", p=128), ub.t[:, :, j_], [ub], (), out_dram=True, allow_slow_non_contiguous=True)

    def mla_pass(self, L):
        S, d = self.S, self.d
        NB = max(self.SEQ // 128, (PAST + TS + 127) // 128)
        w_dn = self.av("w_dn", [128, 8, 704], BF16)
        w_uq = self.av("w_uq", [128, 3, 1536], BF16)
        w_ukT = self.av("w_ukT", [128, 8, 256], BF16)
        w_uv = self.av("w_uv", [128, 2, 1024], BF16)
        w_out = self.av("w_out", [128, 8, D], BF16)
        ukraw = self.av("ukraw", [128, 2, 1024], F32)
        gB = self.av("gmix", [128, D], F32)
        qnB = self.av("qnB", [128, 384], F32)
        kvB = self.av("kvB", [128, 256], F32)
        KTc = self.av("KTc", [128, 2, NB * 128], BF16)
        KTr = self.av("KTr", [128, NB * 128], BF16)
        Va = self.av("Va", [128, NB, 258], BF16)
        dn = self.av("dn", [128, 704], F32)
        cqn = self.av("cqn", [128, 384], F32)
        ckn = self.av("ckn", [128, 256], F32)
        krr = self.av("krr", [128, 64], F32)
        cs = self.av("cs", [128, 2, 32], F32)
        cqT = self.av("cqT", [128, 3, 128], BF16)
        qnT = self.av("qnT", [128, 128], BF16)
        qra = self.av("qra", [128, 8, 66], F32)
        qr0 = self.av("qr0", [128, 8, 64], F32)
        rt = self.av("rt", [128, 8, 32], F32)
        mx = self.av("mx", [128, 8, 4], F32)
        PT = self.av("PT", [128, 4, 128], BF16)
        ol = self.av("ol", [128, 256], F32)
        olT = self.av("olT", [128, 2, 128], BF16)
        oT = self.av("oT", [128, 8, 128], BF16)
        past_c_ap = ukraw.t[:, 0, :].rearrange("p (b c) -> p b c", b=4)
        past_r_ap = ukraw.t.rearrange("p a (b c) -> p (a b) c", c=64)
        qaT = {s.name: self.av("qaT" + s.name, [128, 2, 8 * s.nt], BF16) for s in self.seqs}
        qrT = {s.name: self.av("qrT" + s.name, [128, 8 * s.nt], BF16) for s in self.seqs}
        self.load_w(w_dn, d["mla_w_down"][0])
        self.load_w(w_uq, d["mla_w_uq"][0])
        self.load_w(w_out, d["mla_w_out"][0])
        S.dma("pool", w_uv.t[:], d["mla_w_uv"][0].rearrange("(kc p) h v -> p kc (h v)", p=128), (), [w_uv])
        S.dma("sp", ukraw.t[:], d["mla_w_uk"][0].rearrange("(kc p) h n -> p kc (h n)", p=128), (), [ukraw])
        S.dma("sp", gB.t[:], self.bc(d["norm_mix"][L, :]), (), [gB])
        S.dma("sp", qnB.t[:], self.bc(d["mla_q_norm"][0, :]), (), [qnB])
        S.dma("sp", kvB.t[:], self.bc(d["mla_kv_norm"][0, :]), (), [kvB])
        for h in range(8):
            pb = self.psum()
            for kc in range(2):
                self.tr(pb, pb.t[:, kc * 128:(kc + 1) * 128], ukraw, ukraw.t[:, kc, h * 128:(h + 1) * 128], 128)
            self.V(lambda e, h=h, pb=pb: e.tensor_copy(out=w_ukT.t[:, h, :], in_=pb.t[:, 0:256]), [pb], [w_ukT])
        self.V(lambda e: e.memset(KTr.t[64:65, :], 1.0), (), [KTr])
        self.V(lambda e: e.memset(Va.t[:, :, 256:257], 1.0), (), [Va])
        for seq in self.seqs:
            nt = seq.nt
            p = seq.name == "p"
            nb0 = seq.past // 128
            if seq.past:
                S.dma("pool", Va.t[:, 0:nb0, 0:256], d["c_ckv"][0].rearrange("(b p) c -> p b c", p=128), (), [Va])
                for b0 in range(0, nb0, 4):
                    S.dma("sp", past_c_ap, d["c_ckv"][0][b0 * 128:(b0 + 4) * 128, :].rearrange("(b p) c -> p b c", p=128), (), [ukraw])
                    for b in range(4):
                        pb = self.psum()
                        for kc in range(2):
                            self.tr(pb, pb.t[:, kc * 128:(kc + 1) * 128], ukraw, past_c_ap[:, b, kc * 128:(kc + 1) * 128], 128)
                        bb = b0 + b
                        self.V(lambda e, pb=pb, bb=bb: e.tensor_copy(out=KTc.t[:, :, bb * 128:(bb + 1) * 128],
                                                               in_=pb.t[:, 0:256].rearrange("p (a b) -> p a b", a=2)), [pb], [KTc])
                S.dma("sp", past_r_ap, d["c_kr"][0].rearrange("(b p) c -> p b c", p=128), (), [ukraw])
                for b0 in range(0, nb0, 4):
                    pb = self.psum()
                    for b in range(4):
                        self.tr(pb, pb.t[0:64, b * 128:(b + 1) * 128], ukraw, past_r_ap[:, b0 + b, :], 128)
                    self.V(lambda e, pb=pb, b0=b0: e.tensor_copy(out=KTr.t[0:64, b0 * 128:(b0 + 4) * 128], in_=pb.t[0:64, :]), [pb], [KTr])
            qa, qr = qaT[seq.name], qrT[seq.name]
            for (s_, i, x) in self.tiles_seq(seq, L):
                self.norm_T(x, gB, nt)
                hT = self.hT
                S.dma("sp", cs.t[:nt, 0, :], d["cos_p" if p else "cos_s"][i * nt:(i + 1) * nt, :], (), [cs])
                S.dma("sp", cs.t[:nt, 1, :], d["sin_p" if p else "sin_s"][i * nt:(i + 1) * nt, :], (), [cs])
                for (n0, n1) in ((0, 512), (512, 704)):
                    pb = self.psum()
                    for kc in range(8):
                        self.mm(pb, pb.t[:nt, 0:n1 - n0], hT, hT.t[:, kc, :nt], w_dn, w_dn.t[:, kc, n0:n1], kc == 0, kc == 7)
                    self.V(lambda e, pb=pb, n0=n0, n1=n1: e.tensor_copy(out=dn.t[:nt, n0:n1], in_=pb.t[:nt, 0:n1 - n0]), [pb], [dn])
                st = self.st
                self.rstd(dn, dn.t[:nt, 0:384], nt, 384, st.t[:nt, 5:6], self.junk.t[:nt, 0:384])
                self.V(lambda e: e.scalar_tensor_tensor(out=cqn.t[:nt, :], in0=dn.t[:nt, 0:384], scalar=st.t[:nt, 5:6], in1=qnB.t[:nt, :],
                                                        op0=ALU.mult, op1=ALU.mult), [dn, st, qnB], [cqn])
                self.rstd(dn, dn.t[:nt, 384:640], nt, 256, st.t[:nt, 6:7], self.junk.t[:nt, 0:256])
                self.V(lambda e: e.scalar_tensor_tensor(out=ckn.t[:nt, :], in0=dn.t[:nt, 384:640], scalar=st.t[:nt, 6:7], in1=kvB.t[:nt, :],
                                                        op0=ALU.mult, op1=ALU.mult), [dn, st, kvB], [ckn])
                S.dma("sp", d["ckp" if p else "cks"][0][i * nt:(i + 1) * nt, :], ckn.t[:nt, :], [ckn], (), out_dram=True)
                self.rope(krr.t[:nt, 0:32], krr.t[:nt, 32:64], dn.t[:nt, 640:672], dn.t[:nt, 672:704], cs.t[:nt, 0, :], cs.t[:nt, 1, :],
                          rt.t[:nt, 0, :], [dn, cs], krr, rt)
                S.dma("sp", d["krp" if p else "krs"][0][i * nt:(i + 1) * nt, :], krr.t[:nt, :], [krr], (), out_dram=True)
                kb = nb0 + (i * nt) // 128
                kcol = seq.past + i * nt
                self.V(lambda e, kb=kb: e.tensor_copy(out=Va.t[:nt, kb, 0:256], in_=ckn.t[:nt, :]), [ckn], [Va])
                pb = self.psum()
                for kc in range(2):
                    self.tr(pb, pb.t[:, kc * 128:kc * 128 + nt], ckn, ckn.t[:nt, kc * 128:(kc + 1) * 128], nt)
                for kc in range(2):
                    self.V(lambda e, pb=pb, kc=kc, kcol=kcol: e.tensor_copy(out=KTc.t[:, kc, kcol:kcol + nt], in_=pb.t[:, kc * 128:kc * 128 + nt]), [pb], [KTc])
                pb = self.psum()
                self.tr(pb, pb.t[0:64, 0:nt], krr, krr.t[:nt, :], nt)
                self.V(lambda e, pb=pb, kcol=kcol: e.tensor_copy(out=KTr.t[0:64, kcol:kcol + nt], in_=pb.t[0:64, 0:nt]), [pb], [KTr])
                self.transpose_to(cqn, cqn.t, nt, 3, cqT, lambda kc: cqT.t[:, kc, :nt])
                pb = self.psum()
                for kc in range(3):
                    self.mm(pb, pb.t[:nt, :], cqT, cqT.t[:, kc, :nt], w_uq,
                            w_uq.t[:, kc, :].rearrange("p (h n) -> p h n", h=8)[:, :, 128:192], kc == 0, kc == 2)
                self.V(lambda e, pb=pb: e.tensor_scalar(out=qr0.t[:nt], in0=pb.t[:nt, :].rearrange("p (h n) -> p h n", h=8), scalar1=QSCALE, scalar2=None, op0=ALU.mult), [pb], [qr0])
                cosb = cs.t[:nt, 0, :].unsqueeze(1).broadcast_to([nt, 8, 32])
                sinb = cs.t[:nt, 1, :].unsqueeze(1).broadcast_to([nt, 8, 32])
                self.rope(qra.t[:nt, :, 0:32], qra.t[:nt, :, 32:64], qr0.t[:nt, :, 0:32], qr0.t[:nt, :, 32:64], cosb, sinb, rt.t[:nt], [qr0, cs], qra, rt)
                for h in range(8):
                    pn = self.psum()
                    for kc in range(3):
                        self.mm(pn, pn.t[:, :nt], w_uq, w_uq.t[:, kc, h * 192:h * 192 + 128], cqT, cqT.t[:, kc, :nt], kc == 0, kc == 2)
                    self.V(lambda e, pn=pn: e.tensor_scalar(out=qnT.t[:, :nt], in0=pn.t[:, :nt], scalar1=QSCALE, scalar2=None, op0=ALU.mult), [pn], [qnT])
                    pa = self.psum()
                    for kc in range(2):
                        self.mm(pa, pa.t[:, kc * 128:kc * 128 + nt], w_ukT, w_ukT.t[:, h, kc * 128:(kc + 1) * 128], qnT, qnT.t[:, :nt])
                    for kc in range(2):
                        self.V(lambda e, pa=pa, kc=kc, h=h: e.tensor_copy(out=qa.t[:, kc, h * nt:(h + 1) * nt], in_=pa.t[:, kc * 128:kc * 128 + nt]), [pa], [qa])
                nkeys = kcol + nt
                self.V(lambda e: e.memset(qra.t[:nt, :, 64:65], 0.0), (), [qra])
                self.qr_transpose(qra, qr, nt)
                for h in range(8):
                    for k0 in range(0, nkeys, 512):
                        k1 = min(nkeys, k0 + 512)
                        pb = self.psum()
                        for kc in range(2):
                            self.mm(pb, pb.t[:nt, 0:k1 - k0], qa, qa.t[:, kc, h * nt:(h + 1) * nt], KTc, KTc.t[:, kc, k0:k1], kc == 0, False)
                        self.mm(pb, pb.t[:nt, 0:k1 - k0], qr, qr.t[0:64, h * nt:(h + 1) * nt], KTr, KTr.t[0:64, k0:k1], False, True)
                        ci = 1 + (k0 // 512) % 2 if k0 else 0
                        self.V(lambda e, pb=pb, h=h, ci=ci, k0=k0, k1=k1: e.tensor_reduce(out=mx.t[:nt, h, ci:ci + 1], in_=pb.t[:nt, 0:k1 - k0],
                                                                                         axis=AX.X, op=ALU.max), [pb], [mx])
                        if k0:
                            self.V(lambda e, h=h, ci=ci: e.tensor_tensor(out=mx.t[:nt, h, 0:1], in0=mx.t[:nt, h, 0:1], in1=mx.t[:nt, h, ci:ci + 1],
                                                                         op=ALU.max), [mx], [mx])
                self.V(lambda e: e.tensor_scalar(out=qra.t[:nt, :, 64:65], in0=mx.t[:nt, :, 0:1], scalar1=-1.0, scalar2=None, op0=ALU.mult), [mx], [qra])
                self.qr_transpose(qra, qr, nt)
                nkb = (nkeys + 127) // 128
                for hg in range(2):
                    acc = self.psB
                    for kb_ in range(nkb):
                        k0 = kb_ * 128
                        kl = min(128, nkeys - k0)
                        pb = self.psum()
                        for kc in range(2):
                            self.mm(pb, pb.t[:kl, 0:4 * nt], KTc, KTc.t[:, kc, k0:k0 + kl], qa, qa.t[:, kc, hg * 4 * nt:(hg + 1) * 4 * nt], kc == 0, False)
                        self.mm(pb, pb.t[:kl, 0:4 * nt], KTr, KTr.t[0:65, k0:k0 + kl], qr, qr.t[0:65, hg * 4 * nt:(hg + 1) * 4 * nt], False, True)
                        self.A(lambda e, pb=pb, kl=kl: e.activation(out=PT.t[:kl, :, :nt], in_=pb.t[:kl, 0:4 * nt].rearrange("p (h t) -> p h t", h=4),
                                                                    func=AF.Exp), [pb], [PT])
                        if p and kb_ == nkb - 1:
                            self.V(lambda e: e.memset(PT.t[64:128, :, 0:64], 0.0), (), [PT])
                        for hh in range(4):
                            self.mm(acc[hh], acc[hh].t[:nt, 0:257], PT, PT.t[:kl, hh, :nt], Va, Va.t[:kl, kb_, 0:257], kb_ == 0, kb_ == nkb - 1)
                    for hh in range(4):
                        h = hg * 4 + hh
                        a_ = acc[hh]
                        self.V(lambda e, a_=a_: e.reciprocal(out=st.t[:nt, 7:8], in_=a_.t[:nt, 256:257]), [a_], [st])
                        self.V(lambda e, a_=a_: e.tensor_scalar(out=ol.t[:nt, :], in0=a_.t[:nt, 0:256], scalar1=st.t[:nt, 7:8], scalar2=None, op0=ALU.mult),
                               [a_, st], [ol])
                        self.transpose_to(ol, ol.t, nt, 2, olT, lambda kc: olT.t[:, kc, :nt])
                        po = self.psum()
                        for kc in range(2):
                            self.mm(po, po.t[:, :nt], w_uv, w_uv.t[:, kc, h * 128:(h + 1) * 128], olT, olT.t[:, kc, :nt], kc == 0, kc == 1)
                        self.V(lambda e, po=po, h=h: e.tensor_copy(out=oT.t[:, h, :nt], in_=po.t[:, :nt]), [po], [oT])
                self.out_proj(oT, lambda kc: oT.t[:, kc, :nt], w_out, nt, x)
                self.store_x(seq, i, x)

    def qr_transpose(self, qra, qr, nt):
        for h0 in range(0, 8, 4):
            pb = self.psum()
            for hh in range(4):
                self.tr(pb, pb.t[0:65, hh * nt:(hh + 1) * nt], qra, qra.t[:nt, h0 + hh, 0:65], nt)
            self.V(lambda e, pb=pb, h0=h0: e.tensor_copy(out=qr.t[0:65, h0 * nt:(h0 + 4) * nt], in_=pb.t[0:65, 0:4 * nt]), [pb], [qr])

    def rope(self, o1, o2, x1, x2, cos, sin, tmp, rtoks, otok, ttok):
        self.V(lambda e: e.tensor_tensor(out=o1, in0=x1, in1=cos, op=ALU.mult), rtoks, [otok])
        self.V(lambda e: e.tensor_tensor(out=tmp, in0=x2, in1=sin, op=ALU.mult), rtoks, [ttok])
        self.V(lambda e: e.tensor_tensor(out=o1, in0=o1, in1=tmp, op=ALU.subtract), [otok, ttok], [otok])
        self.V(lambda e: e.tensor_tensor(out=o2, in0=x2, in1=cos, op=ALU.mult), rtoks, [otok])
        self.V(lambda e: e.tensor_tensor(out=tmp, in0=x1, in1=sin, op=ALU.mult), rtoks, [ttok])
        self.V(lambda e: e.tensor_tensor(out=o2, in0=o2, in1=tmp, op=ALU.add), [otok, ttok], [otok])

    def tiles_seq(self, seq, L):
        for i in range(seq.ntiles):
            b = self.xb[self.xi % 2]; self.xi += 1
            self.load_x(seq, i, L, b)
            yield seq, i, b


_CACHE = {}


def rope_tab(pos):
    inv = (np.float32(10000.0) ** (-np.arange(0, 64, 2, dtype=np.float32) / np.float32(64))).astype(np.float32)
    ang = pos.astype(np.float32)[:, None] * inv[None, :]
    return np.cos(ang).astype(np.float32), np.sin(ang).astype(np.float32)


def make_in_maps(inp, SEQ, ncores, depth=DEPTH):
    f = lambda a: np.ascontiguousarray(np.asarray(a, dtype=np.float32))
    cp, sp_ = rope_tab(np.arange(SEQ))
    cs_, ss_ = rope_tab(PAST + np.arange(TS))
    shared = {k: f(inp[k]) for k in ["norm_mix", "norm_ffn", "hg_w_in", "hg_lb_logits", "hg_norm", "hg_w_out", "rg_w_in", "rg_conv_w",
                                     "rg_conv_b", "rg_w_a", "rg_b_a", "rg_w_x", "rg_b_x", "rg_lambda", "rg_w_out", "mla_w_down", "mla_q_norm",
                                     "mla_w_uq", "mla_kv_norm", "mla_w_uk", "mla_w_uv", "mla_w_out", "pk_w_query", "pk_sub_keys", "pk_u", "pk_v"]}
    shared["norm_final"] = f(inp["norm_final"]).reshape(1, D)
    shared["pk_u"] = shared["pk_u"][:depth]
    shared["pk_v"] = shared["pk_v"][:depth]
    cst = np.zeros((128, 272), np.float32)
    cst[:, 0:128] = np.eye(128, dtype=np.float32)
    cst[:, 128:256] = np.triu(np.ones((128, 128), np.float32))
    cst[:, 256:272] = np.arange(16, dtype=np.float32)[None, :]
    shared.update(cos_p=cp, sin_p=sp_, cos_s=cs_, sin_s=ss_, cst=cst)
    maps = []
    for c in range(ncores):
        m = dict(shared)
        m["xp"] = f(inp["x_prompt"][c // 2, :SEQ])
        m["xs"] = f(inp["x_sample"][c])
        m["st_hg"] = f(inp["state_hgrn"][:, c])
        m["st_rh"] = f(inp["state_rglru_h"][:, c])
        m["st_rc"] = f(inp["state_rglru_conv"][:, c])
        m["c_ckv"] = f(inp["cache_mla_ckv"][:, c])
        m["c_kr"] = f(inp["cache_mla_krope"][:, c])
        maps.append(m)
    return maps


def assemble(res, SEQ, ncores=8):
    r = res
    ev = list(range(0, ncores, 2))
    st = lambda k, cs: np.stack([r[c][k] for c in cs], axis=0)
    yp = st("yp", ev)
    ys = st("ys", range(ncores))
    hgp = np.stack([r[c]["hgp"] for c in ev], axis=1)
    hgs = np.stack([r[c]["hgs"] for c in range(ncores)], axis=1)
    rhp = np.stack([r[c]["rhp"] for c in ev], axis=1)
    rhs = np.stack([r[c]["rhs"] for c in range(ncores)], axis=1)
    rcp = np.stack([r[c]["rcp"] for c in ev], axis=1)
    rcs = np.stack([r[c]["rcs"] for c in range(ncores)], axis=1)
    ckp = np.stack([r[c]["ckp"] for c in ev], axis=1)
    cks = np.stack([r[c]["cks"] for c in range(ncores)], axis=1)
    krp = np.stack([r[c]["krp"] for c in ev], axis=1)
    krs = np.stack([r[c]["krs"] for c in range(ncores)], axis=1)
    return (yp, ys, hgp, hgs, rhp, rhs, rcp, rcs, ckp, cks, krp, krs)


def kernel(**inputs):
    SEQ = inputs["x_prompt"].shape[1]
    key = SEQ
    if key not in _CACHE:
        _CACHE[key] = K(SEQ).nc
    nc = _CACHE[key]
    maps = make_in_maps(inputs, SEQ, 8)
    res = run_bass_kernel_spmd(nc, maps, core_ids=list(range(8)))
    return assemble(res.results, SEQ, 8)
```

```python
import os
import numpy as np
import concourse.bass as bass
import concourse.mybir as mybir
from concourse.bass_utils import run_bass_kernel_spmd

F32 = mybir.dt.float32
BF16 = mybir.dt.bfloat16
I32 = mybir.dt.int32
U32 = mybir.dt.uint32
AF = mybir.ActivationFunctionType
ALU = mybir.AluOpType
AX = mybir.AxisListType

D = 1024
DEPTH = 4
EPS = 1e-6
NEXP = 16384
PAST = 4096
TS = 32
QSCALE = 192.0 ** -0.5
GC1 = 0.044715
GC2 = 1.5957691216057308
ARENA = 42000


class Tok:
    __slots__ = ("t", "lw", "rd", "name")

    def __init__(self, t=None, name=""):
        self.t = t
        self.lw = {}
        self.rd = {}
        self.name = name


class Rec:
    def __init__(self):
        self.call = None

    def __getattr__(self, name):
        def f(*a, **k):
            self.call = (name, a, k)
            return self
        return f


def _rec(fn):
    r = Rec()
    fn(r)
    return r.call


class Sched:
    ENG = ("pe", "dve", "act", "pool", "sp")
    NSLOT = {"sp": 24, "pool": 24}

    def __init__(self, nc):
        self.nc = nc
        self.prog = {k: [] for k in self.ENG}
        self.sem = {k: nc.alloc_semaphore(f"s_{k}") for k in self.ENG}
        self.cnt = {k: 0 for k in self.ENG}
        self.waited = {k: {} for k in self.ENG}
        self.slots = {q: [[nc.alloc_semaphore(f"d_{q}{i}"), 0] for i in range(n)] for q, n in self.NSLOT.items()}
        self.slot_i = {q: 0 for q in self.NSLOT}
        self.semobj = {}
        for k in self.ENG:
            self.semobj[("e", k)] = self.sem[k]
        for q, sl in self.slots.items():
            for i, s in enumerate(sl):
                self.semobj[("d", q, i)] = s[0]
        self.out_events = []
        self.n_inst = 0

    def sb(self, name, shape, dtype):
        return Tok(self.nc.alloc_sbuf_tensor(name, list(shape), dtype), name)

    def ps(self, name, shape, dtype):
        return Tok(self.nc.alloc_psum_tensor(name, list(shape), dtype), name)

    def _wait(self, eng, key, val):
        if eng == "pe" and key == ("e", "pe"):
            return
        w = self.waited[eng]
        if w.get(key, 0) >= val:
            return
        w[key] = val
        self.prog[eng].append(("w", key, val))

    def _deps(self, eng, reads, writes):
        for t in reads:
            for k, v in t.lw.items():
                self._wait(eng, k, v)
        for t in writes:
            for k, v in t.lw.items():
                self._wait(eng, k, v)
            for k, v in t.rd.items():
                self._wait(eng, k, v)

    def _commit(self, ev, reads, writes):
        k, v = ev
        for t in reads:
            if t.rd.get(k, 0) < v:
                t.rd[k] = v
        for t in writes:
            t.lw = {k: v}
            t.rd = {}

    def _ser(self, eng):
        if not os.environ.get("KSER"):
            return
        for k in self.ENG:
            if self.cnt[k] > 0:
                self._wait(eng, ("e", k), self.cnt[k])
        for q, sl in self.slots.items():
            for i, s in enumerate(sl):
                if s[1] > 0:
                    self._wait(eng, ("d", q, i), s[1])

    def op(self, eng, fn, reads=(), writes=()):
        self._ser(eng)
        self._deps(eng, reads, writes)
        self.cnt[eng] += 1
        ev = (("e", eng), self.cnt[eng])
        self.prog[eng].append(("o", _rec(fn), ev[0], 1))
        self._commit(ev, reads, writes)
        self.n_inst += 1
        return ev

    def _dma_common(self, q, fn, reads, writes, out_dram):
        self._ser(q)
        self._deps(q, reads, writes)
        i = self.slot_i[q]
        self.slot_i[q] = (i + 1) % len(self.slots[q])
        s = self.slots[q][i]
        key = ("d", q, i)
        if s[1] > 0:
            self._wait(q, key, s[1])
        s[1] += 16
        ev = (key, s[1])
        self.prog[q].append(("o", _rec(fn), key, 16))
        self._commit(ev, reads, writes)
        if out_dram:
            self.out_events.append(ev)
        self.n_inst += 1
        return ev

    def dma(self, q, out, in_, reads=(), writes=(), out_dram=False, **kw):
        return self._dma_common(q, lambda e: e.dma_start(out=out, in_=in_, **kw), reads, writes, out_dram)

    def gather(self, out, table, idx_ap, reads=(), writes=()):
        def fn(e):
            return e.indirect_dma_start(out=out, out_offset=None, in_=table,
                                        in_offset=bass.IndirectOffsetOnAxis(ap=idx_ap, axis=0))
        return self._dma_common("pool", fn, reads, writes, False)

    def barrier(self):
        evs = [(("e", k), self.cnt[k]) for k in self.ENG if self.cnt[k] > 0]
        for q, sl in self.slots.items():
            for i, s in enumerate(sl):
                if s[1] > 0:
                    evs.append((("d", q, i), s[1]))
        for e in self.ENG:
            for k, v in evs:
                self._wait(e, k, v)

    def finish(self):
        for (k, v) in self.out_events:
            self._wait("sp", k, v)
        nc = self.nc
        S = self
        with nc.Block() as block:
            def replay(eng_name):
                def body(e):
                    for item in S.prog[eng_name]:
                        if item[0] == "w":
                            e.wait_ge(S.semobj[item[1]], item[2])
                        else:
                            _, (name, a, k), key, inc = item
                            getattr(e, name)(*a, **k).then_inc(S.semobj[key], inc)
                return body

            block.tensor(replay("pe"))
            block.vector(replay("dve"))
            block.scalar(replay("act"))
            block.gpsimd(replay("pool"))
            block.sync(replay("sp"))


class Seq:
    def __init__(self, name, T, nt, past):
        self.name, self.T, self.nt, self.past = name, T, nt, past
        self.ntiles = T // nt


class K:
    def __init__(self, SEQ, depth=DEPTH):
        self.SEQ = SEQ
        self.depth = depth
        nc = self.nc = bass.Bass("TRN2", target_bir_lowering=False)
        S = self.S = Sched(nc)
        self.seqs = [Seq("p", SEQ, 128, 0), Seq("s", TS, TS, PAST)]
        self.d = {}
        di = lambda n, sh, dt=F32: self.d.__setitem__(n, nc.dram_tensor(n, list(sh), dt, kind="ExternalInput"))
        do = lambda n, sh, dt=F32: self.d.__setitem__(n, nc.dram_tensor(n, list(sh), dt, kind="ExternalOutput"))
        di("xp", [SEQ, D]); di("xs", [TS, D])
        di("st_hg", [2, 8, 128, 128]); di("st_rh", [1, D]); di("st_rc", [1, 3, D])
        di("c_ckv", [1, PAST, 256]); di("c_kr", [1, PAST, 64])
        di("cst", [128, 272]); di("cos_p", [SEQ, 32]); di("sin_p", [SEQ, 32]); di("cos_s", [TS, 32]); di("sin_s", [TS, 32])
        di("norm_mix", [4, D]); di("norm_ffn", [4, D]); di("norm_final", [1, D])
        di("hg_w_in", [2, D, 4096]); di("hg_lb_logits", [4, D]); di("hg_norm", [2, D]); di("hg_w_out", [2, D, D])
        di("rg_w_in", [1, D, 2048]); di("rg_conv_w", [1, 4, D]); di("rg_conv_b", [1, D])
        di("rg_w_a", [1, 8, 128, 128]); di("rg_b_a", [1, D]); di("rg_w_x", [1, 8, 128, 128]); di("rg_b_x", [1, D])
        di("rg_lambda", [1, D]); di("rg_w_out", [1, D, D])
        di("mla_w_down", [1, D, 704]); di("mla_q_norm", [1, 384]); di("mla_w_uq", [1, 384, 1536])
        di("mla_kv_norm", [1, 256]); di("mla_w_uk", [1, 256, 8, 128]); di("mla_w_uv", [1, 256, 8, 128])
        di("mla_w_out", [1, D, D])
        di("pk_w_query", [4, D, D]); di("pk_sub_keys", [4, 8, 2, 128, 64])
        di("pk_u", [depth, NEXP, D]); di("pk_v", [depth, NEXP, D])
        do("yp", [SEQ, D]); do("ys", [TS, D])
        do("hgp", [2, 8, 128, 128]); do("hgs", [2, 8, 128, 128])
        do("rhp", [1, D]); do("rhs", [1, D]); do("rcp", [1, 3, D]); do("rcs", [1, 3, D])
        do("ckp", [1, SEQ, 256]); do("cks", [1, TS, 256]); do("krp", [1, SEQ, 64]); do("krs", [1, TS, 64])
        self.dbg = os.environ.get("KDBG", "")
        if self.dbg:
            do("dbg", [128, 2048])
        self.scr = {"p": nc.dram_tensor("scr_p", [SEQ, D], F32, kind="Internal"),
                    "s": nc.dram_tensor("scr_s", [TS, D], F32, kind="Internal")}
        self.uvb = nc.dram_tensor("uvb", [depth * NEXP, 2 * D], BF16, kind="Internal")
        self.uvtok = Tok(None, "uvb")
        self.scr_tok = {(s.name, i): Tok(None, "scr") for s in self.seqs for i in range(s.ntiles)}
        self.ident = S.sb("ident", [128, 128], F32)
        self.maskU = S.sb("maskU", [128, 128], F32)
        self.ones = S.sb("ones", [128, 128], F32)
        self.iota16 = S.sb("iota16", [128, 16], F32)
        self.xb = [S.sb(f"xb{i}", [128, D], F32) for i in range(2)]
        self.xn = S.sb("xn", [128, D], F32)
        self.junk = S.sb("junk", [128, D], BF16)
        self.hT = S.sb("hT", [128, 8, 128], BF16)
        self.st = S.sb("st", [128, 8], F32)
        self.arena = nc.alloc_sbuf_tensor("arena", [128, ARENA], F32)
        self.psA = [S.ps(f"psA{i}", [128, 512], F32) for i in range(4)]
        self.psB = [S.ps(f"psB{i}", [128, 512], F32) for i in range(4)]
        self.psi = 0
        self.xi = 0
        self.consts()
        self.aoff = 0
        self.prepass()
        for L in range(depth):
            kind = L % 3
            S.barrier()
            self.aoff = 0
            if kind == 0:
                self.hgrn_pass(L)
            elif kind == 1:
                self.rglru_pass(L)
            else:
                self.mla_pass(L)
            S.barrier()
            self.aoff = 0
            self.peer_pass(L, last=(L == depth - 1))
        S.finish()

    def av(self, name, shape, dtype):
        n = int(np.prod(shape[1:]))
        n4 = n if dtype in (F32, I32, U32) else (n + 1) // 2
        ap = self.arena[:, self.aoff:self.aoff + n4]
        self.aoff += n4
        assert self.aoff <= ARENA, (name, self.aoff)
        if dtype != F32:
            ap = ap.bitcast(dtype)
            if dtype == BF16 and n % 2:
                ap = ap[:, 0:n]
        if len(shape) == 3:
            ap = ap.rearrange("p (a b) -> p a b", a=shape[1])
        elif len(shape) == 4:
            ap = ap.rearrange("p (a b c) -> p a b c", a=shape[1], b=shape[2])
        return Tok(ap, name)

    def psum(self):
        t = self.psA[self.psi % 4]
        self.psi += 1
        return t

    def mm(self, out_t, out_ap, a_t, a_ap, b_t, b_ap, start=True, stop=True):
        self.S.op("pe", lambda e: e.matmul(out_ap, a_ap, b_ap, start=start, stop=stop),
                  reads=[a_t, b_t], writes=[out_t])

    def tr(self, out_t, out_ap, in_t, in_ap, n_in_part):
        idn = self.ident
        self.S.op("pe", lambda e: e.transpose(out_ap, in_ap, idn.t[:n_in_part, :n_in_part]),
                  reads=[in_t, idn], writes=[out_t])

    def V(self, fn, r=(), w=()):
        self.S.op("dve", fn, r, w)

    def A(self, fn, r=(), w=()):
        self.S.op("act", fn, r, w)

    def consts(self):
        S = self.S
        c = self.d["cst"]
        S.dma("sp", self.ident.t[:], c[:, 0:128], (), [self.ident])
        S.dma("sp", self.maskU.t[:], c[:, 128:256], (), [self.maskU])
        S.dma("sp", self.iota16.t[:], c[:, 256:272], (), [self.iota16])
        self.V(lambda e: e.memset(self.ones.t[:], 1.0), (), [self.ones])

    def bc(self, row_ap, n=128):
        return row_ap.partition_broadcast(n)

    def load_w(self, dst, src_ap):
        kc = dst.t.shape[1]
        N = dst.t.shape[2]
        src = src_ap.rearrange("(kc p) n -> p kc n", p=128)
        q = "pool" if dst.t.dtype != F32 else "sp"
        for n0 in range(0, N, 1024):
            n1 = min(N, n0 + 1024)
            self.S.dma(q, dst.t[:, :, n0:n1], src[:, :, n0:n1], (), [dst])

    def load_x(self, seq, i, L, buf):
        src = (self.d["xp" if seq.name == "p" else "xs"] if L < 0 else self.scr[seq.name])
        r0 = i * seq.nt
        self.S.dma("sp", buf.t[:seq.nt, :], src[r0:r0 + seq.nt, :], [self.scr_tok[(seq.name, i)]], [buf])

    def store_x(self, seq, i, buf):
        r0 = i * seq.nt
        self.S.dma("sp", self.scr[seq.name][r0:r0 + seq.nt, :], buf.t[:seq.nt, :], [buf], [self.scr_tok[(seq.name, i)]])

    def tiles(self, first_layer_src):
        lst = [(s, i) for s in self.seqs for i in range(s.ntiles)]
        bufs = {}
        b0 = self.xb[self.xi % 2]; self.xi += 1
        self.load_x(lst[0][0], lst[0][1], first_layer_src, b0)
        bufs[0] = b0
        for j, (s, i) in enumerate(lst):
            if j + 1 < len(lst):
                b = self.xb[self.xi % 2]; self.xi += 1
                self.load_x(lst[j + 1][0], lst[j + 1][1], first_layer_src, b)
                bufs[j + 1] = b
            yield s, i, bufs.pop(j)

    def rstd(self, src_t, src_ap, nt, n, out_ap, scratch_ap):
        st = self.st
        jk = self.junk
        self.V(lambda e: e.scalar_tensor_tensor(out=scratch_ap, in0=src_ap, scalar=1.0, in1=src_ap, op0=ALU.mult, op1=ALU.mult,
                                                accum_out=st.t[:nt, 0:1]), [src_t], [jk, st])
        self.V(lambda e: e.tensor_scalar(out=st.t[:nt, 1:2], in0=st.t[:nt, 0:1], scalar1=1.0 / n, scalar2=EPS,
                                         op0=ALU.mult, op1=ALU.add), [st], [st])
        self.A(lambda e: e.activation(out=st.t[:nt, 2:3], in_=st.t[:nt, 1:2], func=AF.Sqrt), [st], [st])
        self.V(lambda e: e.reciprocal(out=out_ap, in_=st.t[:nt, 2:3]), [st], [st])

    def norm_T(self, x, gB, nt):
        xn, hT, st = self.xn, self.hT, self.st
        self.rstd(x, x.t[:nt, :], nt, D, st.t[:nt, 3:4], self.junk.t[:nt, :])
        self.V(lambda e: e.scalar_tensor_tensor(out=xn.t[:nt, :], in0=x.t[:nt, :], scalar=st.t[:nt, 3:4], in1=gB.t[:nt, :],
                                                op0=ALU.mult, op1=ALU.mult), [x, st, gB], [xn])
        self.transpose_to(xn, xn.t, nt, 8, hT, lambda kc: hT.t[:, kc, :nt])

    def transpose_to(self, src, src_ap, nt, nch, dst, dst_fn, scale=None):
        for g0 in range(0, nch, 4):
            g1 = min(nch, g0 + 4)
            pb = self.psum()
            for kc in range(g0, g1):
                self.tr(pb, pb.t[:, (kc - g0) * 128:(kc - g0) * 128 + nt], src, src_ap[:nt, kc * 128:(kc + 1) * 128], nt)
            for kc in range(g0, g1):
                o = dst_fn(kc)
                i_ = pb.t[:, (kc - g0) * 128:(kc - g0) * 128 + nt]
                if scale is None:
                    self.V(lambda e, o=o, i_=i_: e.tensor_copy(out=o, in_=i_), [pb], [dst])
                else:
                    self.V(lambda e, o=o, i_=i_: e.tensor_scalar(out=o, in0=i_, scalar1=scale, scalar2=None, op0=ALU.mult), [pb], [dst])

    def gelu_tanh(self, dst_t, dst_ap, src_t, src_ap, tmp_t, tmp_ap):
        self.V(lambda e: e.tensor_tensor(out=tmp_ap, in0=src_ap, in1=src_ap, op=ALU.mult), [src_t], [tmp_t])
        self.V(lambda e: e.tensor_scalar(out=tmp_ap, in0=tmp_ap, scalar1=GC1, scalar2=1.0, op0=ALU.mult, op1=ALU.add), [tmp_t], [tmp_t])
        self.V(lambda e: e.tensor_tensor(out=tmp_ap, in0=tmp_ap, in1=src_ap, op=ALU.mult), [tmp_t, src_t], [tmp_t])
        self.A(lambda e: e.activation(out=tmp_ap, in_=tmp_ap, func=AF.Sigmoid, scale=GC2), [tmp_t], [tmp_t])
        self.V(lambda e: e.tensor_tensor(out=dst_ap, in0=tmp_ap, in1=src_ap, op=ALU.mult), [tmp_t, src_t], [dst_t])

    def add_resid(self, x, nt, ybanks):
        for j, pb in enumerate(ybanks):
            self.V(lambda e, pb=pb, j=j: e.tensor_tensor(out=x.t[:nt, j * 512:(j + 1) * 512], in0=x.t[:nt, j * 512:(j + 1) * 512],
                                                         in1=pb.t[:nt, :], op=ALU.add), [x, pb], [x])

    def out_proj(self, lhs_t, lhs_fn, w, nt, x):
        banks = []
        for j in range(2):
            pb = self.psum()
            for kc in range(8):
                self.mm(pb, pb.t[:nt, :], lhs_t, lhs_fn(kc), w, w.t[:, kc, j * 512:(j + 1) * 512], kc == 0, kc == 7)
            banks.append(pb)
        self.add_resid(x, nt, banks)

    def prepass(self):
        S, d = self.S, self.d
        stg = [self.av(f"stg{i}", [128, 4, 2 * D], BF16) for i in range(2)]
        k = 0
        for l in range(self.depth):
            for c in range(NEXP // 512):
                r0 = c * 512
                st = stg[k % 2]; k += 1
                S.dma("pool", st.t[:, :, 0:D], d["pk_u"][l, r0:r0 + 512, :].rearrange("(p a) d -> p a d", a=4), (), [st])
                S.dma("pool", st.t[:, :, D:2 * D], d["pk_v"][l, r0:r0 + 512, :].rearrange("(p a) d -> p a d", a=4), (), [st])
                S.dma("sp", self.uvb[l * NEXP + r0:l * NEXP + r0 + 512, :].rearrange("(p a) f -> p a f", a=4), st.t[:], [st], [self.uvtok])

    def peer_pass(self, L, last):
        S, d = self.S, self.d
        wq = self.av("wq", [128, 8, D], BF16)
        BD = self.av("BD", [128, 8, 256], F32)
        gB = self.av("gffn", [128, D], F32)
        kraw = self.av("kraw", [128, 16, 64], F32)
        G = [self.av(f"G{i}", [128, 2 * D], BF16) for i in range(16)]
        dg = [self.av(f"dg{i}", [128, 128], BF16) for i in range(4)]
        tmpg = self.av("tmpg", [128, 128], F32)
        bufA = self.av("bufA", [128, 2048], F32)
        bufB = self.av("bufB", [128, 2048], F32)
        bufC = self.av("bufC", [128, 2048], F32)
        qT = self.av("qT", [128, 8, 128], F32)
        sv = self.av("sv", [128, 8, 2, 16], F32)
        si = self.av("si", [128, 8, 2, 16], U32)
        sif = self.av("sif", [128, 8, 2, 16], F32)
        ts = self.av("ts", [128, 8, 16], F32)
        tj = self.av("tj", [128, 8, 16], U32)
        ab = self.av("ab", [128, 2, 128], I32)
        abf = self.av("abf", [128, 2, 128], F32)
        sel = self.av("sel", [128, 2, 128], F32)
        eidx = self.av("eidx", [128, 128], I32)
        gate = self.av("gate", [128, 8, 16], F32)
        z = self.av("z", [128, 2, 8], F32)
        actp = self.av("actp", [128, 128], F32)
        acth = self.av("acth", [128, 128], F32)
        wgt = self.av("wgt", [128, 128], F32)
        if last:
            self.gfin = self.av("gfin", [128, D], F32)
            S.dma("sp", self.gfin.t[:], self.bc(d["norm_final"][0, :]), (), [self.gfin])
        self.load_w(wq, d["pk_w_query"][L])
        S.dma("sp", gB.t[:], self.bc(d["norm_ffn"][L, :]), (), [gB])
        S.dma("sp", kraw.t[:], d["pk_sub_keys"][L].rearrange("h p k d -> k (h p) d"), (), [kraw])
        self.V(lambda e: e.memset(BD.t[:], 0.0), (), [BD])
        for h in range(8):
            pb = self.psum()
            self.tr(pb, pb.t[:, 0:128], kraw, kraw.t[:, 2 * h:2 * h + 2, :].rearrange("k a d -> k (a d)"), 128)
            self.V(lambda e, h=h, pb=pb: e.tensor_copy(out=BD.t[0:64, h, 0:128], in_=pb.t[0:64, 0:128]), [pb], [BD])
            self.V(lambda e, h=h, pb=pb: e.tensor_copy(out=BD.t[64:128, h, 128:256], in_=pb.t[64:128, 0:128]), [pb], [BD])
        uvb = self.uvb.ap()
        gi = 0
        di = 0
        for seq, i, x in self.tiles(L):
            nt = seq.nt
            self.norm_T(x, gB, nt)
            xn, hT = self.xn, self.hT
            if "nopeer" in self.dbg:
                self.peer_tail(seq, i, x, nt, last)
                continue
            for c in range(8):
                pb = self.psum()
                for kc in range(8):
                    self.mm(pb, pb.t[:, :nt], wq, wq.t[:, kc, c * 128:(c + 1) * 128], hT, hT.t[:, kc, :nt], kc == 0, kc == 7)
                self.V(lambda e, c=c, pb=pb: e.tensor_copy(out=qT.t[:, c, :nt], in_=pb.t[:, :nt]), [pb], [qT])
            for j in range(4):
                pb = self.psum()
                for hh in range(2):
                    h = 2 * j + hh
                    self.mm(pb, pb.t[:nt, hh * 256:(hh + 1) * 256], qT, qT.t[:, h, :nt], BD, BD.t[:, h, :])
                self.V(lambda e, j=j, pb=pb: e.tensor_copy(out=bufA.t[:nt, j * 512:(j + 1) * 512], in_=pb.t[:nt, :]), [pb], [bufA])
            sW = bufA.t.rearrange("p (g k) -> p g k", g=16)
            sW2 = bufB.t.rearrange("p (g k) -> p g k", g=16)
            for g in range(16):
                h, p = g // 2, g % 2
                self.V(lambda e, g=g, h=h, p=p: e.max(out=sv.t[:nt, h, p, 0:8], in_=sW[:nt, g, :]), [bufA], [sv])
                self.V(lambda e, g=g, h=h, p=p: e.match_replace(out=sW2[:nt, g, :], in_to_replace=sv.t[:nt, h, p, 0:8],
                                                               in_values=sW[:nt, g, :], imm_value=-3.0e38), [bufA, sv], [bufB])
                self.V(lambda e, g=g, h=h, p=p: e.max(out=sv.t[:nt, h, p, 8:16], in_=sW2[:nt, g, :]), [bufB], [sv])
                self.V(lambda e, g=g, h=h, p=p: e.max_index(out=si.t[:nt, h, p, 0:8], in_max=sv.t[:nt, h, p, 0:8],
                                                           in_values=sW[:nt, g, :]), [bufA, sv], [si])
                self.V(lambda e, g=g, h=h, p=p: e.max_index(out=si.t[:nt, h, p, 8:16], in_max=sv.t[:nt, h, p, 8:16],
                                                           in_values=sW2[:nt, g, :]), [bufB, sv], [si])
            cand = bufA.t.rearrange("p (h a b) -> p h a b", h=8, a=16)
            cand2 = bufB.t.rearrange("p (h a b) -> p h a b", h=8, a=16)
            candf = bufA.t.rearrange("p (h n) -> p h n", h=8)
            cand2f = bufB.t.rearrange("p (h n) -> p h n", h=8)
            self.V(lambda e: e.tensor_tensor(out=cand[:nt], in0=sv.t[:nt, :, 0, :].unsqueeze(3).broadcast_to([nt, 8, 16, 16]),
                                             in1=sv.t[:nt, :, 1, :].unsqueeze(2).broadcast_to([nt, 8, 16, 16]), op=ALU.add), [sv], [bufA])
            for h in range(8):
                self.V(lambda e, h=h: e.max(out=ts.t[:nt, h, 0:8], in_=candf[:nt, h, :]), [bufA], [ts])
                self.V(lambda e, h=h: e.match_replace(out=cand2f[:nt, h, :], in_to_replace=ts.t[:nt, h, 0:8],
                                                      in_values=candf[:nt, h, :], imm_value=-3.0e38), [bufA, ts], [bufB])
                self.V(lambda e, h=h: e.max(out=ts.t[:nt, h, 8:16], in_=cand2f[:nt, h, :]), [bufB], [ts])
                self.V(lambda e, h=h: e.max_index(out=tj.t[:nt, h, 0:8], in_max=ts.t[:nt, h, 0:8], in_values=candf[:nt, h, :]), [bufA, ts], [tj])
                self.V(lambda e, h=h: e.max_index(out=tj.t[:nt, h, 8:16], in_max=ts.t[:nt, h, 8:16], in_values=cand2f[:nt, h, :]), [bufB, ts], [tj])
            self.V(lambda e: e.tensor_tensor(out=gate.t[:nt], in0=ts.t[:nt], in1=ts.t[:nt, :, 0:1].broadcast_to([nt, 8, 16]),
                                             op=ALU.subtract), [ts], [gate])
            self.A(lambda e: e.activation(out=gate.t[:nt], in_=gate.t[:nt], func=AF.Exp), [gate], [gate])
            self.V(lambda e: e.tensor_reduce(out=z.t[:nt, 0, :], in_=gate.t[:nt], axis=AX.X, op=ALU.add), [gate], [z])
            self.V(lambda e: e.reciprocal(out=z.t[:nt, 1, :], in_=z.t[:nt, 0, :]), [z], [z])
            self.V(lambda e: e.tensor_tensor(out=gate.t[:nt], in0=gate.t[:nt], in1=z.t[:nt, 1, :].unsqueeze(2).broadcast_to([nt, 8, 16]),
                                             op=ALU.mult), [gate, z], [gate])
            tji = tj.t.bitcast(I32).rearrange("p h k -> p (h k)")
            self.V(lambda e: e.tensor_single_scalar(out=ab.t[:nt, 0, :], in_=tji[:nt], scalar=4, op=ALU.arith_shift_right), [tj], [ab])
            self.V(lambda e: e.tensor_single_scalar(out=ab.t[:nt, 1, :], in_=tji[:nt], scalar=15, op=ALU.bitwise_and), [tj], [ab])
            self.V(lambda e: e.tensor_copy(out=abf.t[:nt], in_=ab.t[:nt]), [ab], [abf])
            self.V(lambda e: e.tensor_copy(out=sif.t[:nt], in_=si.t[:nt]), [si], [sif])
            oh = bufC.t.rearrange("p (r a) -> p r a", a=16)
            oh4 = bufC.t.rearrange("p (h k a) -> p h k a", h=8, k=16)
            for w_ in range(2):
                self.V(lambda e, w_=w_: e.tensor_tensor(out=oh[:nt], in0=abf.t[:nt, w_, :].unsqueeze(2).broadcast_to([nt, 128, 16]),
                                                        in1=self.iota16.t[:nt].unsqueeze(1).broadcast_to([nt, 128, 16]), op=ALU.is_equal),
                       [abf, self.iota16], [bufC])
                self.V(lambda e, w_=w_: e.tensor_tensor(out=oh4[:nt], in0=oh4[:nt],
                                                        in1=sif.t[:nt, :, w_, :].unsqueeze(2).broadcast_to([nt, 8, 16, 16]), op=ALU.mult),
                       [sif, bufC], [bufC])
                self.V(lambda e, w_=w_: e.tensor_reduce(out=sel.t[:nt, w_, :], in_=oh[:nt], axis=AX.X, op=ALU.add), [bufC], [sel])
            self.V(lambda e: e.scalar_tensor_tensor(out=abf.t[:nt, 0, :], in0=sel.t[:nt, 0, :], scalar=128.0, in1=sel.t[:nt, 1, :],
                                                    op0=ALU.mult, op1=ALU.add), [sel], [abf])
            if L:
                self.V(lambda e: e.tensor_scalar(out=abf.t[:nt, 0, :], in0=abf.t[:nt, 0, :], scalar1=float(L * NEXP), scalar2=None, op0=ALU.add), [abf], [abf])
            self.V(lambda e: e.tensor_copy(out=eidx.t[:nt], in_=abf.t[:nt, 0, :]), [abf], [eidx])
            if "nogather" in self.dbg:
                if i == 0 and seq.name == "p":
                    S.dma("sp", d["dbg"][:, 0:128], abf.t[:, 0, :], [abf], (), out_dram=True)
                    S.dma("sp", d["dbg"][:, 128:256], gate.t[:].rearrange("p h k -> p (h k)"), [gate], (), out_dram=True)
                    S.dma("sp", d["dbg"][:, 256:512], sv.t[:].rearrange("p h a k -> p (h a k)"), [sv], (), out_dram=True)
                    S.dma("sp", d["dbg"][:, 512:768], sif.t[:].rearrange("p h a k -> p (h a k)"), [sif], (), out_dram=True)
                    S.dma("sp", d["dbg"][:, 768:896], ts.t[:].rearrange("p h k -> p (h k)"), [ts], (), out_dram=True)
                self.peer_tail(seq, i, x, nt, last)
                continue
            acc = [self.psB[0], self.psB[1]]
            gflat = gate.t.rearrange("p h k -> p (h k)")
            for g in range(16):
                grp = []
                for j in range(8):
                    r = g * 8 + j
                    g_ = G[gi % 16]; gi += 1
                    grp.append(g_)
                    S.gather(g_.t[:nt, :], uvb, eidx.t[:nt, r:r + 1], [eidx, self.uvtok], [g_])
                    self.V(lambda e, g_=g_, r=r: e.scalar_tensor_tensor(out=self.junk.t[:nt, :], in0=g_.t[:nt, 0:D], scalar=1.0, in1=xn.t[:nt, :],
                                                                       op0=ALU.mult, op1=ALU.mult, accum_out=actp.t[:nt, r:r + 1]),
                           [g_, xn], [self.junk, actp])
                c0_, c1_ = g * 8, g * 8 + 8
                self.gelu_tanh(acth, acth.t[:nt, c0_:c1_], actp, actp.t[:nt, c0_:c1_], tmpg, tmpg.t[:nt, c0_:c1_])
                self.V(lambda e: e.tensor_tensor(out=wgt.t[:nt, c0_:c1_], in0=acth.t[:nt, c0_:c1_], in1=gflat[:nt, c0_:c1_], op=ALU.mult),
                       [acth, gate], [wgt])
                for j in range(8):
                    r = g * 8 + j
                    g_ = grp[j]
                    dgb = dg[di % 4]; di += 1
                    self.V(lambda e, dgb=dgb, r=r: e.tensor_scalar(out=dgb.t[:nt, :nt], in0=self.ident.t[:nt, :nt], scalar1=wgt.t[:nt, r:r + 1], scalar2=None,
                                                                  op0=ALU.mult), [self.ident, wgt], [dgb])
                    for hf in range(2):
                        self.mm(acc[hf], acc[hf].t[:nt, :], dgb, dgb.t[:nt, :nt], g_, g_.t[:nt, D + hf * 512:D + (hf + 1) * 512], r == 0, r == 127)
            self.add_resid(x, nt, acc)
            self.peer_tail(seq, i, x, nt, last)

    def peer_tail(self, seq, i, x, nt, last):
        S, d, xn = self.S, self.d, self.xn
        if True:
            if not last:
                self.store_x(seq, i, x)
            else:
                self.rstd(x, x.t[:nt, :], nt, D, self.st.t[:nt, 3:4], self.junk.t[:nt, :])
                self.V(lambda e: e.scalar_tensor_tensor(out=xn.t[:nt, :], in0=x.t[:nt, :], scalar=self.st.t[:nt, 3:4], in1=self.gfin.t[:nt, :],
                                                        op0=ALU.mult, op1=ALU.mult), [x, self.st, self.gfin], [xn])
                dst = d["yp" if seq.name == "p" else "ys"]
                S.dma("sp", dst[i * nt:(i + 1) * nt, :], xn.t[:nt, :], [xn], (), out_dram=True)

    def hgrn_pass(self, L):
        S, d = self.S, self.d
        slot = L // 3
        w_in = self.av("hg_w_in", [128, 8, 4096], BF16)
        w_out = self.av("hg_w_out", [128, 8, D], BF16)
        gB = self.av("gmix", [128, D], F32)
        gnB = self.av("gn", [128, D], F32)
        St = self.av("Sst", [128, 8, 128], F32)
        Sr = self.av("Sr", [128, 128], BF16)
        lbl = self.av("lbl", [128, 4, 8], F32)
        oml = self.av("oml", [128, 8], F32)
        qTa = self.av("qTa", [128, 8, 128], F32)
        kTa = self.av("kTa", [128, 8, 128], F32)
        lfa = self.av("lfa", [128, 8, 128], F32)
        bcA = self.av("bcA", [128, 8, 64], F32)
        dA = self.av("dA", [128, 8, 64], F32)
        exA = self.av("exA", [128, 3, 8, 64], F32)
        scA = self.av("scA", [128, 8, 2], F32)
        sc5 = self.av("sc5", [128, 8], F32)
        qt = self.av("qt", [128, 64], BF16)
        kt = self.av("kt", [128, 64], BF16)
        kh = self.av("kh", [128, 64], F32)
        khat = self.av("khat", [64, 128], BF16)
        scT = self.av("scT", [64, 64], BF16)
        iv = self.av("iv", [64, D], BF16)
        sg = self.av("sg", [64, D], F32)
        o_sb = self.av("o_sb", [64, D], F32)
        onT = self.av("onT", [128, 8, 128], BF16)
        self.load_w(w_in, d["hg_w_in"][slot])
        self.load_w(w_out, d["hg_w_out"][slot])
        S.dma("sp", gB.t[:], self.bc(d["norm_mix"][L, :]), (), [gB])
        S.dma("sp", gnB.t[:], self.bc(d["hg_norm"][slot, :]), (), [gnB])
        for l_ in range(4):
            S.dma("sp", lbl.t[:, l_, :], d["hg_lb_logits"][l_, :].rearrange("(h p) -> p h", p=128), (), [lbl], allow_slow_non_contiguous=True)
        self.A(lambda e: e.activation(out=lbl.t[:], in_=lbl.t[:], func=AF.Exp), [lbl], [lbl])
        self.V(lambda e: e.tensor_reduce(out=oml.t[:], in_=lbl.t[:].rearrange("p l h -> p h l"), axis=AX.X, op=ALU.add), [lbl], [oml])
        self.V(lambda e: e.reciprocal(out=oml.t[:], in_=oml.t[:]), [oml], [oml])
        if L == 0:
            self.V(lambda e: e.memset(oml.t[:], 1.0), (), [oml])
        else:
            self.V(lambda e: e.tensor_reduce(out=sc5.t[:], in_=lbl.t[:, 1:L + 1, :].rearrange("p l h -> p h l"), axis=AX.X, op=ALU.add), [lbl], [sc5])
            self.V(lambda e: e.tensor_tensor(out=sc5.t[:], in0=sc5.t[:], in1=oml.t[:], op=ALU.mult), [sc5, oml], [sc5])
            self.V(lambda e: e.tensor_scalar(out=oml.t[:], in0=sc5.t[:], scalar1=-1.0, scalar2=1.0, op0=ALU.mult, op1=ALU.add), [sc5], [oml])
        cur = None
        for seq, i, x in self.tiles(L - 1 if L == 0 else L):
            nt = seq.nt
            if cur is not seq:
                if cur is not None:
                    self.hg_out(cur, slot, St)
                cur = seq
                if seq.name == "p":
                    self.V(lambda e: e.memset(St.t[:], 0.0), (), [St])
                else:
                    S.dma("sp", St.t[:], d["st_hg"][slot].rearrange("h f i -> f h i"), (), [St])
            self.norm_T(x, gB, nt)
            hT = self.hT
            for h in range(8):
                pq = self.psum()
                pf_ = self.psum()
                for kc in range(8):
                    self.mm(pq, pq.t[:, :nt], w_in, w_in.t[:, kc, h * 128:(h + 1) * 128], hT, hT.t[:, kc, :nt], kc == 0, kc == 7)
                for kc in range(8):
                    self.mm(pf_, pf_.t[:, :nt], w_in, w_in.t[:, kc, 1024 + h * 128:1024 + (h + 1) * 128], hT, hT.t[:, kc, :nt], kc == 0, kc == 7)
                self.V(lambda e, h=h, pq=pq: e.tensor_copy(out=qTa.t[:, h, :nt], in_=pq.t[:, :nt]), [pq], [qTa])
                self.V(lambda e, h=h, pf_=pf_: e.tensor_copy(out=kTa.t[:, h, :nt], in_=pf_.t[:, :nt]), [pf_], [kTa])
            self.A(lambda e: e.activation(out=qTa.t[:, :, :nt], in_=qTa.t[:, :, :nt], func=AF.Silu), [qTa], [qTa])
            self.A(lambda e: e.activation(out=kTa.t[:, :, :nt], in_=kTa.t[:, :, :nt], func=AF.Sigmoid, scale=-1.0), [kTa], [kTa])
            self.V(lambda e: e.tensor_tensor(out=kTa.t[:, :, :nt], in0=kTa.t[:, :, :nt], in1=oml.t[:, :].unsqueeze(2).broadcast_to([128, 8, nt]),
                                             op=ALU.mult), [kTa, oml], [kTa])
            self.A(lambda e: e.activation(out=lfa.t[:, :, :nt], in_=kTa.t[:, :, :nt], func=AF.Ln, scale=-1.0, bias=1.0), [kTa], [lfa])
            CL = min(int(os.environ.get("KCL", "64")), nt)
            for c0 in range(0, nt, CL):
                for j in range(4):
                    pb = self.psum()
                    for kc in range(8):
                        self.mm(pb, pb.t[:CL, :], hT, hT.t[:, kc, c0:c0 + CL], w_in, w_in.t[:, kc, 2048 + j * 512:2048 + (j + 1) * 512], kc == 0, kc == 7)
                    if j < 2:
                        self.V(lambda e, j=j, pb=pb: e.tensor_copy(out=iv.t[:CL, j * 512:(j + 1) * 512], in_=pb.t[:CL, :]), [pb], [iv])
                    else:
                        self.V(lambda e, j=j, pb=pb: e.tensor_copy(out=sg.t[:CL, (j - 2) * 512:(j - 1) * 512], in_=pb.t[:CL, :]), [pb], [sg])
                self.A(lambda e: e.activation(out=sg.t[:CL, :], in_=sg.t[:CL, :], func=AF.Silu), [sg], [sg])
                mid = CL // 2 - 1
                for h in range(8):
                    self.V(lambda e, h=h: e.tensor_tensor_scan(out=bcA.t[:, h, :CL], data0=self.ones.t[:, :CL], data1=lfa.t[:, h, c0:c0 + CL],
                                                               initial=0.0, op0=ALU.mult, op1=ALU.add), [self.ones, lfa], [bcA])
                self.V(lambda e: e.tensor_copy(out=scA.t[:, :, 0:1], in_=bcA.t[:, :, mid:mid + 1]), [bcA], [scA])
                self.V(lambda e: e.tensor_copy(out=scA.t[:, :, 1:2], in_=bcA.t[:, :, CL - 1:CL]), [bcA], [scA])
                self.V(lambda e: e.tensor_tensor(out=dA.t[:, :, :CL], in0=bcA.t[:, :, :CL], in1=scA.t[:, :, 0:1].broadcast_to([128, 8, CL]),
                                                 op=ALU.subtract), [bcA, scA], [dA])
                self.A(lambda e: e.activation(out=exA.t[:, 0, :, :CL], in_=dA.t[:, :, :CL], func=AF.Exp), [dA], [exA])
                self.A(lambda e: e.activation(out=exA.t[:, 1, :, :CL], in_=dA.t[:, :, :CL], func=AF.Exp, scale=-1.0), [dA], [exA])
                self.V(lambda e: e.tensor_tensor(out=dA.t[:, :, :CL], in0=bcA.t[:, :, :CL], in1=scA.t[:, :, 1:2].broadcast_to([128, 8, CL]),
                                                 op=ALU.subtract), [bcA, scA], [dA])
                self.A(lambda e: e.activation(out=exA.t[:, 2, :, :CL], in_=dA.t[:, :, :CL], func=AF.Exp, scale=-1.0), [dA], [exA])
                self.A(lambda e: e.activation(out=scA.t[:, :, :], in_=scA.t[:, :, :], func=AF.Exp), [scA], [scA])
                for h in range(8):
                    self.V(lambda e, h=h: e.tensor_tensor(out=qt.t[:, :CL], in0=qTa.t[:, h, c0:c0 + CL], in1=exA.t[:, 0, h, :CL], op=ALU.mult), [qTa, exA], [qt])
                    self.V(lambda e, h=h: e.tensor_tensor(out=kt.t[:, :CL], in0=kTa.t[:, h, c0:c0 + CL], in1=exA.t[:, 1, h, :CL], op=ALU.mult), [kTa, exA], [kt])
                    self.V(lambda e, h=h: e.tensor_tensor(out=kh.t[:, :CL], in0=kTa.t[:, h, c0:c0 + CL], in1=exA.t[:, 2, h, :CL], op=ALU.mult), [kTa, exA], [kh])
                    self.V(lambda e, h=h: e.tensor_scalar(out=Sr.t[:], in0=St.t[:, h, :], scalar1=scA.t[:, h, 0:1], scalar2=None, op0=ALU.mult), [St, scA], [Sr])
                    pk = self.psum()
                    self.tr(pk, pk.t[:CL, 0:128], kh, kh.t[:, :CL], 128)
                    self.V(lambda e, pk=pk: e.tensor_copy(out=khat.t[:CL, :], in_=pk.t[:CL, 0:128]), [pk], [khat])
                    psc = self.psum()
                    self.mm(psc, psc.t[:CL, :CL], kt, kt.t[:, :CL], qt, qt.t[:, :CL])
                    self.V(lambda e, psc=psc: e.tensor_tensor(out=scT.t[:CL, :CL], in0=psc.t[:CL, :CL], in1=self.maskU.t[:CL, :CL], op=ALU.mult),
                           [psc, self.maskU], [scT])
                    po = self.psum()
                    self.mm(po, po.t[:CL, 0:128], scT, scT.t[:CL, :CL], iv, iv.t[:CL, h * 128:(h + 1) * 128], True, False)
                    self.mm(po, po.t[:CL, 0:128], qt, qt.t[:, :CL], Sr, Sr.t[:], False, True)
                    self.V(lambda e, h=h, po=po: e.tensor_copy(out=o_sb.t[:CL, h * 128:(h + 1) * 128], in_=po.t[:CL, 0:128]), [po], [o_sb])
                    pn = self.psum()
                    self.mm(pn, pn.t[:, 0:128], khat, khat.t[:CL, :], iv, iv.t[:CL, h * 128:(h + 1) * 128])
                    self.V(lambda e, h=h, pn=pn: e.scalar_tensor_tensor(out=St.t[:, h, :], in0=St.t[:, h, :], scalar=scA.t[:, h, 1:2], in1=pn.t[:, 0:128],
                                                                       op0=ALU.mult, op1=ALU.add), [St, scA, pn], [St])
                self.rstd(o_sb, o_sb.t[:CL, :], CL, D, self.st.t[:CL, 4:5], self.junk.t[:CL, :])
                self.V(lambda e: e.scalar_tensor_tensor(out=o_sb.t[:CL, :], in0=o_sb.t[:CL, :], scalar=self.st.t[:CL, 4:5], in1=gnB.t[:CL, :],
                                                        op0=ALU.mult, op1=ALU.mult), [o_sb, self.st, gnB], [o_sb])
                self.V(lambda e: e.tensor_tensor(out=o_sb.t[:CL, :], in0=o_sb.t[:CL, :], in1=sg.t[:CL, :], op=ALU.mult), [o_sb, sg], [o_sb])
                self.transpose_to(o_sb, o_sb.t, CL, 8, onT, lambda kc, c0=c0: onT.t[:, kc, c0:c0 + CL])
            self.out_proj(onT, lambda kc: onT.t[:, kc, :nt], w_out, nt, x)
            self.store_x(seq, i, x)
        self.hg_out(cur, slot, St)

    def hg_out(self, seq, slot, St):
        dst = self.d["hgp" if seq.name == "p" else "hgs"]
        self.S.dma("sp", dst[slot].rearrange("h f i -> f h i"), St.t[:], [St], (), out_dram=True)

    def rglru_pass(self, L):
        S, d = self.S, self.d
        w_in = self.av("rg_w_in", [128, 8, 2048], BF16)
        w_out = self.av("rg_w_out", [128, 8, D], BF16)
        wa = self.av("rg_wa", [128, 8, 128], F32)
        wx = self.av("rg_wx", [128, 8, 128], F32)
        gB = self.av("gmix", [128, D], F32)
        cw = self.av("cw", [128, 4, 8], F32)
        pv = self.av("pv", [128, 5, 8], F32)
        ub = self.av("ub", [128, 8, 3 + 128], F32)
        hprev = self.av("hprev", [128, 8], F32)
        g1 = self.av("g1", [128, 8, 128], F32)
        tmp = self.av("tmp", [128, 8, 128], F32)
        gel = self.av("gel", [128, 8, 128], F32)
        xc = self.av("xc", [128, 8, 128], F32)
        rr = self.av("rr", [128, 8, 128], F32)
        ig = self.av("ig", [128, 8, 128], F32)
        aa = self.av("aa", [128, 8, 128], F32)
        mm_ = self.av("mm_", [128, 8, 128], F32)
        hb = self.av("hb", [128, 8, 128], F32)
        hgT = self.av("hgT", [128, 8, 128], BF16)
        self.load_w(w_in, d["rg_w_in"][0])
        self.load_w(w_out, d["rg_w_out"][0])
        S.dma("sp", wa.t[:], d["rg_w_a"][0].rearrange("n c d -> c n d"), (), [wa])
        S.dma("sp", wx.t[:], d["rg_w_x"][0].rearrange("n c d -> c n d"), (), [wx])
        S.dma("sp", gB.t[:], self.bc(d["norm_mix"][L, :]), (), [gB])
        for j_ in range(4):
            S.dma("sp", cw.t[:, j_, :], d["rg_conv_w"][0][j_, :].rearrange("(n p) -> p n", p=128), (), [cw], allow_slow_non_contiguous=True)
        for k_, nm in enumerate(["rg_conv_b", "rg_b_a", "rg_b_x", "rg_lambda"]):
            S.dma("sp", pv.t[:, k_, :], d[nm][0].rearrange("(n p) -> p n", p=128), (), [pv], allow_slow_non_contiguous=True)
        self.A(lambda e: e.activation(out=pv.t[:, 3, :], in_=pv.t[:, 3, :], func=AF.Exp, scale=-1.0), [pv], [pv])
        self.A(lambda e: e.activation(out=pv.t[:, 3, :], in_=pv.t[:, 3, :], func=AF.Ln, bias=1.0), [pv], [pv])
        self.V(lambda e: e.tensor_scalar(out=pv.t[:, 4, :], in0=pv.t[:, 3, :], scalar1=-16.0, scalar2=None, op0=ALU.mult), [pv], [pv])
        self.V(lambda e: e.tensor_scalar(out=pv.t[:, 3, :], in0=pv.t[:, 3, :], scalar1=-8.0, scalar2=None, op0=ALU.mult), [pv], [pv])
        cur = None
        for seq, i, x in self.tiles(L):
            nt = seq.nt
            if cur is not seq:
                if cur is not None:
                    self.rg_out(cur, hprev, ub)
                cur = seq
                if seq.name == "p":
                    self.V(lambda e: e.memset(ub.t[:], 0.0), (), [ub])
                    self.V(lambda e: e.memset(hprev.t[:], 0.0), (), [hprev])
                else:
                    S.dma("sp", hprev.t[:], d["st_rh"][0].rearrange("(n p) -> p n", p=128), (), [hprev], allow_slow_non_contiguous=True)
                    for j_ in range(3):
                        S.dma("sp", ub.t[:, :, j_], d["st_rc"][0][j_, :].rearrange("(n p) -> p n", p=128), (), [ub], allow_slow_non_contiguous=True)
            self.norm_T(x, gB, nt)
            hT = self.hT
            for cc in range(8):
                pg = self.psum()
                pu = self.psum()
                for kc in range(8):
                    self.mm(pg, pg.t[:, :nt], w_in, w_in.t[:, kc, cc * 128:(cc + 1) * 128], hT, hT.t[:, kc, :nt], kc == 0, kc == 7)
                for kc in range(8):
                    self.mm(pu, pu.t[:, :nt], w_in, w_in.t[:, kc, 1024 + cc * 128:1024 + (cc + 1) * 128], hT, hT.t[:, kc, :nt], kc == 0, kc == 7)
                self.V(lambda e, pg=pg, cc=cc: e.tensor_copy(out=g1.t[:, cc, :nt], in_=pg.t[:, :nt]), [pg], [g1])
                self.V(lambda e, pu=pu, cc=cc: e.tensor_copy(out=ub.t[:, cc, 3:3 + nt], in_=pu.t[:, :nt]), [pu], [ub])
                self.V(lambda e, cc=cc: e.tensor_scalar(out=xc.t[:, cc, :nt], in0=ub.t[:, cc, 0:nt], scalar1=cw.t[:, 0, cc:cc + 1], scalar2=pv.t[:, 0, cc:cc + 1],
                                                        op0=ALU.mult, op1=ALU.add), [ub, cw, pv], [xc])
                for j in range(1, 4):
                    self.V(lambda e, cc=cc, j=j: e.scalar_tensor_tensor(out=xc.t[:, cc, :nt], in0=ub.t[:, cc, j:j + nt], scalar=cw.t[:, j, cc:cc + 1],
                                                                       in1=xc.t[:, cc, :nt], op0=ALU.mult, op1=ALU.add), [ub, cw, xc], [xc])
                pr = self.psum()
                pi_ = self.psum()
                self.mm(pr, pr.t[:, :nt], wa, wa.t[:, cc, :], xc, xc.t[:, cc, :nt])
                self.mm(pi_, pi_.t[:, :nt], wx, wx.t[:, cc, :], xc, xc.t[:, cc, :nt])
                self.V(lambda e, cc=cc, pr=pr: e.tensor_scalar(out=rr.t[:, cc, :nt], in0=pr.t[:, :nt], scalar1=pv.t[:, 1, cc:cc + 1], scalar2=None, op0=ALU.add), [pr, pv], [rr])
                self.V(lambda e, cc=cc, pi_=pi_: e.tensor_scalar(out=ig.t[:, cc, :nt], in0=pi_.t[:, :nt], scalar1=pv.t[:, 2, cc:cc + 1], scalar2=None, op0=ALU.add), [pi_, pv], [ig])
            self.gelu_tanh(gel, gel.t[:, :, :nt], g1, g1.t[:, :, :nt], tmp, tmp.t[:, :, :nt])
            self.A(lambda e: e.activation(out=rr.t[:, :, :nt], in_=rr.t[:, :, :nt], func=AF.Sigmoid), [rr], [rr])
            self.A(lambda e: e.activation(out=ig.t[:, :, :nt], in_=ig.t[:, :, :nt], func=AF.Sigmoid), [ig], [ig])
            self.V(lambda e: e.tensor_tensor(out=rr.t[:, :, :nt], in0=rr.t[:, :, :nt], in1=pv.t[:, 3, :].unsqueeze(2).broadcast_to([128, 8, nt]), op=ALU.mult), [rr, pv], [rr])
            self.A(lambda e: e.activation(out=aa.t[:, :, :nt], in_=rr.t[:, :, :nt], func=AF.Exp), [rr], [aa])
            self.A(lambda e: e.activation(out=mm_.t[:, :, :nt], in_=rr.t[:, :, :nt], func=AF.Exp, scale=2.0), [rr], [mm_])
            self.A(lambda e: e.activation(out=mm_.t[:, :, :nt], in_=mm_.t[:, :, :nt], func=AF.Sqrt, scale=-1.0, bias=1.0), [mm_], [mm_])
            self.V(lambda e: e.tensor_tensor(out=ig.t[:, :, :nt], in0=ig.t[:, :, :nt], in1=xc.t[:, :, :nt], op=ALU.mult), [ig, xc], [ig])
            self.V(lambda e: e.tensor_tensor(out=ig.t[:, :, :nt], in0=ig.t[:, :, :nt], in1=mm_.t[:, :, :nt], op=ALU.mult), [ig, mm_], [ig])
            for cc in range(8):
                self.V(lambda e, cc=cc: e.tensor_tensor_scan(out=hb.t[:, cc, :nt], data0=aa.t[:, cc, :nt], data1=ig.t[:, cc, :nt], initial=hprev.t[:, cc:cc + 1],
                                                             op0=ALU.mult, op1=ALU.add), [aa, ig, hprev], [hb])
                self.V(lambda e, cc=cc: e.tensor_copy(out=hprev.t[:, cc:cc + 1], in_=hb.t[:, cc, nt - 1:nt]), [hb], [hprev])
            self.V(lambda e: e.tensor_tensor(out=hgT.t[:, :, :nt], in0=hb.t[:, :, :nt], in1=gel.t[:, :, :nt], op=ALU.mult), [hb, gel], [hgT])
            self.V(lambda e: e.tensor_copy(out=tmp.t[:, :, 0:3], in_=ub.t[:, :, nt:nt + 3]), [ub], [tmp])
            self.V(lambda e: e.tensor_copy(out=ub.t[:, :, 0:3], in_=tmp.t[:, :, 0:3]), [tmp], [ub])
            self.out_proj(hgT, lambda kc: hgT.t[:, kc, :nt], w_out, nt, x)
            self.store_x(seq, i, x)
        self.rg_out(cur, hprev, ub)

    def rg_out(self, seq, hprev, ub):
        d = self.d
        p = seq.name == "p"
        self.S.dma("sp", d["rhp" if p else "rhs"][0].rearrange("(n p) -> p n", p=128), hprev.t[:], [hprev], (), out_dram=True, allow_slow_non_contiguous=True)
        for j_ in range(3):
            self.S.dma("sp", d["rcp" if p else "rcs"][0][j_, :].rearrange("(n p) -> p n", p=128), ub.t[:, :, j_], [ub], (), out_dram=True, allow_slow_non_contiguous=True)

    def mla_pass(self, L):
        S, d = self.S, self.d
        NB = max(self.SEQ // 128, (PAST + TS + 127) // 128)
        w_dn = self.av("w_dn", [128, 8, 704], BF16)
        w_uq = self.av("w_uq", [128, 3, 1536], BF16)
        w_ukT = self.av("w_ukT", [128, 8, 256], BF16)
        w_uv = self.av("w_uv", [128, 2, 1024], BF16)
        w_out = self.av("w_out", [128, 8, D], BF16)
        ukraw = self.av("ukraw", [128, 2, 1024], F32)
        gB = self.av("gmix", [128, D], F32)
        qnB = self.av("qnB", [128, 384], F32)
        kvB = self.av("kvB", [128, 256], F32)
        KTc = self.av("KTc", [128, 2, NB * 128], BF16)
        KTr = self.av("KTr", [128, NB * 128], BF16)
        Va = self.av("Va", [128, NB, 258], BF16)
        dn = self.av("dn", [128, 704], F32)
        cqn = self.av("cqn", [128, 384], F32)
        ckn = self.av("ckn", [128, 256], F32)
        krr = self.av("krr", [128, 64], F32)
        cs = self.av("cs", [128, 2, 32], F32)
        cqT = self.av("cqT", [128, 3, 128], BF16)
        qnT = self.av("qnT", [128, 128], BF16)
        qra = self.av("qra", [128, 8, 66], F32)
        qr0 = self.av("qr0", [128, 8, 64], F32)
        rt = self.av("rt", [128, 8, 32], F32)
        mx = self.av("mx", [128, 8, 4], F32)
        PT = self.av("PT", [128, 4, 128], BF16)
        ol = self.av("ol", [128, 256], F32)
        olT = self.av("olT", [128, 2, 128], BF16)
        oT = self.av("oT", [128, 8, 128], BF16)
        past_c_ap = ukraw.t[:, 0, :].rearrange("p (b c) -> p b c", b=4)
        past_r_ap = ukraw.t.rearrange("p a (b c) -> p (a b) c", c=64)
        qaT = {s.name: self.av("qaT" + s.name, [128, 2, 8 * s.nt], BF16) for s in self.seqs}
        qrT = {s.name: self.av("qrT" + s.name, [128, 8 * s.nt], BF16) for s in self.seqs}
        self.load_w(w_dn, d["mla_w_down"][0])
        self.load_w(w_uq, d["mla_w_uq"][0])
        self.load_w(w_out, d["mla_w_out"][0])
        S.dma("pool", w_uv.t[:], d["mla_w_uv"][0].rearrange("(kc p) h v -> p kc (h v)", p=128), (), [w_uv])
        S.dma("sp", ukraw.t[:], d["mla_w_uk"][0].rearrange("(kc p) h n -> p kc (h n)", p=128), (), [ukraw])
        S.dma("sp", gB.t[:], self.bc(d["norm_mix"][L, :]), (), [gB])
        S.dma("sp", qnB.t[:], self.bc(d["mla_q_norm"][0, :]), (), [qnB])
        S.dma("sp", kvB.t[:], self.bc(d["mla_kv_norm"][0, :]), (), [kvB])
        for h in range(8):
            pb = self.psum()
            for kc in range(2):
                self.tr(pb, pb.t[:, kc * 128:(kc + 1) * 128], ukraw, ukraw.t[:, kc, h * 128:(h + 1) * 128], 128)
            self.V(lambda e, h=h, pb=pb: e.tensor_copy(out=w_ukT.t[:, h, :], in_=pb.t[:, 0:256]), [pb], [w_ukT])
        self.V(lambda e: e.memset(KTr.t[64:65, :], 1.0), (), [KTr])
        self.V(lambda e: e.memset(Va.t[:, :, 256:257], 1.0), (), [Va])
        for seq in self.seqs:
            nt = seq.nt
            p = seq.name == "p"
            nb0 = seq.past // 128
            if seq.past:
                S.dma("pool", Va.t[:, 0:nb0, 0:256], d["c_ckv"][0].rearrange("(b p) c -> p b c", p=128), (), [Va])
                for b0 in range(0, nb0, 4):
                    S.dma("sp", past_c_ap, d["c_ckv"][0][b0 * 128:(b0 + 4) * 128, :].rearrange("(b p) c -> p b c", p=128), (), [ukraw])
                    for b in range(4):
                        pb = self.psum()
                        for kc in range(2):
                            self.tr(pb, pb.t[:, kc * 128:(kc + 1) * 128], ukraw, past_c_ap[:, b, kc * 128:(kc + 1) * 128], 128)
                        bb = b0 + b
                        self.V(lambda e, pb=pb, bb=bb: e.tensor_copy(out=KTc.t[:, :, bb * 128:(bb + 1) * 128],
                                                               in_=pb.t[:, 0:256].rearrange("p (a b) -> p a b", a=2)), [pb], [KTc])
                S.dma("sp", past_r_ap, d["c_kr"][0].rearrange("(b p) c -> p b c", p=128), (), [ukraw])
                for b0 in range(0, nb0, 4):
                    pb = self.psum()
                    for b in range(4):
                        self.tr(pb, pb.t[0:64, b * 128:(b + 1) * 128], ukraw, past_r_ap[:, b0 + b, :], 128)
                    self.V(lambda e, pb=pb, b0=b0: e.tensor_copy(out=KTr.t[0:64, b0 * 128:(b0 + 4) * 128], in_=pb.t[0:64, :]), [pb], [KTr])
            qa, qr = qaT[seq.name], qrT[seq.name]
            for (s_, i, x) in self.tiles_seq(seq, L):
                self.norm_T(x, gB, nt)
                hT = self.hT
                S.dma("sp", cs.t[:nt, 0, :], d["cos_p" if p else "cos_s"][i * nt:(i + 1) * nt, :], (), [cs])
                S.dma("sp", cs.t[:nt, 1, :], d["sin_p" if p else "sin_s"][i * nt:(i + 1) * nt, :], (), [cs])
                for (n0, n1) in ((0, 512), (512, 704)):
                    pb = self.psum()
                    for kc in range(8):
                        self.mm(pb, pb.t[:nt, 0:n1 - n0], hT, hT.t[:, kc, :nt], w_dn, w_dn.t[:, kc, n0:n1], kc == 0, kc == 7)
                    self.V(lambda e, pb=pb, n0=n0, n1=n1: e.tensor_copy(out=dn.t[:nt, n0:n1], in_=pb.t[:nt, 0:n1 - n0]), [pb], [dn])
                st = self.st
                self.rstd(dn, dn.t[:nt, 0:384], nt, 384, st.t[:nt, 5:6], self.junk.t[:nt, 0:384])
                self.V(lambda e: e.scalar_tensor_tensor(out=cqn.t[:nt, :], in0=dn.t[:nt, 0:384], scalar=st.t[:nt, 5:6], in1=qnB.t[:nt, :],
                                                        op0=ALU.mult, op1=ALU.mult), [dn, st, qnB], [cqn])
                self.rstd(dn, dn.t[:nt, 384:640], nt, 256, st.t[:nt, 6:7], self.junk.t[:nt, 0:256])
                self.V(lambda e: e.scalar_tensor_tensor(out=ckn.t[:nt, :], in0=dn.t[:nt, 384:640], scalar=st.t[:nt, 6:7], in1=kvB.t[:nt, :],
                                                        op0=ALU.mult, op1=ALU.mult), [dn, st, kvB], [ckn])
                S.dma("sp", d["ckp" if p else "cks"][0][i * nt:(i + 1) * nt, :], ckn.t[:nt, :], [ckn], (), out_dram=True)
                self.rope(krr.t[:nt, 0:32], krr.t[:nt, 32:64], dn.t[:nt, 640:672], dn.t[:nt, 672:704], cs.t[:nt, 0, :], cs.t[:nt, 1, :],
                          rt.t[:nt, 0, :], [dn, cs], krr, rt)
                S.dma("sp", d["krp" if p else "krs"][0][i * nt:(i + 1) * nt, :], krr.t[:nt, :], [krr], (), out_dram=True)
                kb = nb0 + (i * nt) // 128
                kcol = seq.past + i * nt
                self.V(lambda e, kb=kb: e.tensor_copy(out=Va.t[:nt, kb, 0:256], in_=ckn.t[:nt, :]), [ckn], [Va])
                pb = self.psum()
                for kc in range(2):
                    self.tr(pb, pb.t[:, kc * 128:kc * 128 + nt], ckn, ckn.t[:nt, kc * 128:(kc + 1) * 128], nt)
                for kc in range(2):
                    self.V(lambda e, pb=pb, kc=kc, kcol=kcol: e.tensor_copy(out=KTc.t[:, kc, kcol:kcol + nt], in_=pb.t[:, kc * 128:kc * 128 + nt]), [pb], [KTc])
                pb = self.psum()
                self.tr(pb, pb.t[0:64, 0:nt], krr, krr.t[:nt, :], nt)
                self.V(lambda e, pb=pb, kcol=kcol: e.tensor_copy(out=KTr.t[0:64, kcol:kcol + nt], in_=pb.t[0:64, 0:nt]), [pb], [KTr])
                self.transpose_to(cqn, cqn.t, nt, 3, cqT, lambda kc: cqT.t[:, kc, :nt])
                pb = self.psum()
                for kc in range(3):
                    self.mm(pb, pb.t[:nt, :], cqT, cqT.t[:, kc, :nt], w_uq,
                            w_uq.t[:, kc, :].rearrange("p (h n) -> p h n", h=8)[:, :, 128:192], kc == 0, kc == 2)
                self.V(lambda e, pb=pb: e.tensor_scalar(out=qr0.t[:nt], in0=pb.t[:nt, :].rearrange("p (h n) -> p h n", h=8), scalar1=QSCALE, scalar2=None, op0=ALU.mult), [pb], [qr0])
                cosb = cs.t[:nt, 0, :].unsqueeze(1).broadcast_to([nt, 8, 32])
                sinb = cs.t[:nt, 1, :].unsqueeze(1).broadcast_to([nt, 8, 32])
                self.rope(qra.t[:nt, :, 0:32], qra.t[:nt, :, 32:64], qr0.t[:nt, :, 0:32], qr0.t[:nt, :, 32:64], cosb, sinb, rt.t[:nt], [qr0, cs], qra, rt)
                for h in range(8):
                    pn = self.psum()
                    for kc in range(3):
                        self.mm(pn, pn.t[:, :nt], w_uq, w_uq.t[:, kc, h * 192:h * 192 + 128], cqT, cqT.t[:, kc, :nt], kc == 0, kc == 2)
                    self.V(lambda e, pn=pn: e.tensor_scalar(out=qnT.t[:, :nt], in0=pn.t[:, :nt], scalar1=QSCALE, scalar2=None, op0=ALU.mult), [pn], [qnT])
                    pa = self.psum()
                    for kc in range(2):
                        self.mm(pa, pa.t[:, kc * 128:kc * 128 + nt], w_ukT, w_ukT.t[:, h, kc * 128:(kc + 1) * 128], qnT, qnT.t[:, :nt])
                    for kc in range(2):
                        self.V(lambda e, pa=pa, kc=kc, h=h: e.tensor_copy(out=qa.t[:, kc, h * nt:(h + 1) * nt], in_=pa.t[:, kc * 128:kc * 128 + nt]), [pa], [qa])
                nkeys = kcol + nt
                self.V(lambda e: e.memset(qra.t[:nt, :, 64:65], 0.0), (), [qra])
                self.qr_transpose(qra, qr, nt)
                for h in range(8):
                    for k0 in range(0, nkeys, 512):
                        k1 = min(nkeys, k0 + 512)
                        pb = self.psum()
                        for kc in range(2):
                            self.mm(pb, pb.t[:nt, 0:k1 - k0], qa, qa.t[:, kc, h * nt:(h + 1) * nt], KTc, KTc.t[:, kc, k0:k1], kc == 0, False)
                        self.mm(pb, pb.t[:nt, 0:k1 - k0], qr, qr.t[0:64, h * nt:(h + 1) * nt], KTr, KTr.t[0:64, k0:k1], False, True)
                        ci = 1 + (k0 // 512) % 2 if k0 else 0
                        self.V(lambda e, pb=pb, h=h, ci=ci, k0=k0, k1=k1: e.tensor_reduce(out=mx.t[:nt, h, ci:ci + 1], in_=pb.t[:nt, 0:k1 - k0],
                                                                                         axis=AX.X, op=ALU.max), [pb], [mx])
                        if k0:
                            self.V(lambda e, h=h, ci=ci: e.tensor_tensor(out=mx.t[:nt, h, 0:1], in0=mx.t[:nt, h, 0:1], in1=mx.t[:nt, h, ci:ci + 1],
                                                                         op=ALU.max), [mx], [mx])
                self.V(lambda e: e.tensor_scalar(out=qra.t[:nt, :, 64:65], in0=mx.t[:nt, :, 0:1], scalar1=-1.0, scalar2=None, op0=ALU.mult), [mx], [qra])
                self.qr_transpose(qra, qr, nt)
                nkb = (nkeys + 127) // 128
                for hg in range(2):
                    acc = self.psB
                    for kb_ in range(nkb):
                        k0 = kb_ * 128
                        kl = min(128, nkeys - k0)
                        pb = self.psum()
                        for kc in range(2):
                            self.mm(pb, pb.t[:kl, 0:4 * nt], KTc, KTc.t[:, kc, k0:k0 + kl], qa, qa.t[:, kc, hg * 4 * nt:(hg + 1) * 4 * nt], kc == 0, False)
                        self.mm(pb, pb.t[:kl, 0:4 * nt], KTr, KTr.t[0:65, k0:k0 + kl], qr, qr.t[0:65, hg * 4 * nt:(hg + 1) * 4 * nt], False, True)
                        self.A(lambda e, pb=pb, kl=kl: e.activation(out=PT.t[:kl, :, :nt], in_=pb.t[:kl, 0:4 * nt].rearrange("p (h t) -> p h t", h=4),
                                                                    func=AF.Exp), [pb], [PT])
                        if p and kb_ == nkb - 1:
                            self.V(lambda e: e.memset(PT.t[64:128, :, 0:64], 0.0), (), [PT])
                        for hh in range(4):
                            self.mm(acc[hh], acc[hh].t[:nt, 0:257], PT, PT.t[:kl, hh, :nt], Va, Va.t[:kl, kb_, 0:257], kb_ == 0, kb_ == nkb - 1)
                    for hh in range(4):
                        h = hg * 4 + hh
                        a_ = acc[hh]
                        self.V(lambda e, a_=a_: e.reciprocal(out=st.t[:nt, 7:8], in_=a_.t[:nt, 256:257]), [a_], [st])
                        self.V(lambda e, a_=a_: e.tensor_scalar(out=ol.t[:nt, :], in0=a_.t[:nt, 0:256], scalar1=st.t[:nt, 7:8], scalar2=None, op0=ALU.mult),
                               [a_, st], [ol])
                        self.transpose_to(ol, ol.t, nt, 2, olT, lambda kc: olT.t[:, kc, :nt])
                        po = self.psum()
                        for kc in range(2):
                            self.mm(po, po.t[:, :nt], w_uv, w_uv.t[:, kc, h * 128:(h + 1) * 128], olT, olT.t[:, kc, :nt], kc == 0, kc == 1)
                        self.V(lambda e, po=po, h=h: e.tensor_copy(out=oT.t[:, h, :nt], in_=po.t[:, :nt]), [po], [oT])
                self.out_proj(oT, lambda kc: oT.t[:, kc, :nt], w_out, nt, x)
                self.store_x(seq, i, x)

    def qr_transpose(self, qra, qr, nt):
        for h0 in range(0, 8, 4):
            pb = self.psum()
            for hh in range(4):
                self.tr(pb, pb.t[0:65, hh * nt:(hh + 1) * nt], qra, qra.t[:nt, h0 + hh, 0:65], nt)
            self.V(lambda e, pb=pb, h0=h0: e.tensor_copy(out=qr.t[0:65, h0 * nt:(h0 + 4) * nt], in_=pb.t[0:65, 0:4 * nt]), [pb], [qr])

    def rope(self, o1, o2, x1, x2, cos, sin, tmp, rtoks, otok, ttok):
        self.V(lambda e: e.tensor_tensor(out=o1, in0=x1, in1=cos, op=ALU.mult), rtoks, [otok])
        self.V(lambda e: e.tensor_tensor(out=tmp, in0=x2, in1=sin, op=ALU.mult), rtoks, [ttok])
        self.V(lambda e: e.tensor_tensor(out=o1, in0=o1, in1=tmp, op=ALU.subtract), [otok, ttok], [otok])
        self.V(lambda e: e.tensor_tensor(out=o2, in0=x2, in1=cos, op=ALU.mult), rtoks, [otok])
        self.V(lambda e: e.tensor_tensor(out=tmp, in0=x1, in1=sin, op=ALU.mult), rtoks, [ttok])
        self.V(lambda e: e.tensor_tensor(out=o2, in0=o2, in1=tmp, op=ALU.add), [otok, ttok], [otok])

    def tiles_seq(self, seq, L):
        for i in range(seq.ntiles):
            b = self.xb[self.xi % 2]; self.xi += 1
            self.load_x(seq, i, L, b)
            yield seq, i, b


_CACHE = {}


def rope_tab(pos):
    inv = (np.float32(10000.0) ** (-np.arange(0, 64, 2, dtype=np.float32) / np.float32(64))).astype(np.float32)
    ang = pos.astype(np.float32)[:, None] * inv[None, :]
    return np.cos(ang).astype(np.float32), np.sin(ang).astype(np.float32)


def make_in_maps(inp, SEQ, ncores, depth=DEPTH):
    f = lambda a: np.ascontiguousarray(np.asarray(a, dtype=np.float32))
    cp, sp_ = rope_tab(np.arange(SEQ))
    cs_, ss_ = rope_tab(PAST + np.arange(TS))
    shared = {k: f(inp[k]) for k in ["norm_mix", "norm_ffn", "hg_w_in", "hg_lb_logits", "hg_norm", "hg_w_out", "rg_w_in", "rg_conv_w",
                                     "rg_conv_b", "rg_w_a", "rg_b_a", "rg_w_x", "rg_b_x", "rg_lambda", "rg_w_out", "mla_w_down", "mla_q_norm",
                                     "mla_w_uq", "mla_kv_norm", "mla_w_uk", "mla_w_uv", "mla_w_out", "pk_w_query", "pk_sub_keys", "pk_u", "pk_v"]}
    shared["norm_final"] = f(inp["norm_final"]).reshape(1, D)
    shared["pk_u"] = shared["pk_u"][:depth]
    shared["pk_v"] = shared["pk_v"][:depth]
    cst = np.zeros((128, 272), np.float32)
    cst[:, 0:128] = np.eye(128, dtype=np.float32)
    cst[:, 128:256] = np.triu(np.ones((128, 128), np.float32))
    cst[:, 256:272] = np.arange(16, dtype=np.float32)[None, :]
    shared.update(cos_p=cp, sin_p=sp_, cos_s=cs_, sin_s=ss_, cst=cst)
    maps = []
    for c in range(ncores):
        m = dict(shared)
        m["xp"] = f(inp["x_prompt"][c // 2, :SEQ])
        m["xs"] = f(inp["x_sample"][c])
        m["st_hg"] = f(inp["state_hgrn"][:, c])
        m["st_rh"] = f(inp["state_rglru_h"][:, c])
        m["st_rc"] = f(inp["state_rglru_conv"][:, c])
        m["c_ckv"] = f(inp["cache_mla_ckv"][:, c])
        m["c_kr"] = f(inp["cache_mla_krope"][:, c])
        maps.append(m)
    return maps


def assemble(res, SEQ, ncores=8):
    r = res
    ev = list(range(0, ncores, 2))
    st = lambda k, cs: np.stack([r[c][k] for c in cs], axis=0)
    yp = st("yp", ev)
    ys = st("ys", range(ncores))
    hgp = np.stack([r[c]["hgp"] for c in ev], axis=1)
    hgs = np.stack([r[c]["hgs"] for c in range(ncores)], axis=1)
    rhp = np.stack([r[c]["rhp"] for c in ev], axis=1)
    rhs = np.stack([r[c]["rhs"] for c in range(ncores)], axis=1)
    rcp = np.stack([r[c]["rcp"] for c in ev], axis=1)
    rcs = np.stack([r[c]["rcs"] for c in range(ncores)], axis=1)
    ckp = np.stack([r[c]["ckp"] for c in ev], axis=1)
    cks = np.stack([r[c]["cks"] for c in range(ncores)], axis=1)
    krp = np.stack([r[c]["krp"] for c in ev], axis=1)
    krs = np.stack([r[c]["krs"] for c in range(ncores)], axis=1)
    return (yp, ys, hgp, hgs, rhp, rhs, rcp, rcs, ckp, cks, krp, krs)


def kernel(**inputs):
    SEQ = inputs["x_prompt"].shape[1]
    key = SEQ
    if key not in _CACHE:
        _CACHE[key] = K(SEQ).nc
    nc = _CACHE[key]
    maps = make_in_maps(inputs, SEQ, 8)
    res = run_bass_kernel_spmd(nc, maps, core_ids=list(range(8)))
    return assemble(res.results, SEQ, 8)
```

```python
import os
import numpy as np
import concourse.bass as bass
import concourse.mybir as mybir
from concourse.bass_utils import run_bass_kernel_spmd

F32 = mybir.dt.float32
BF16 = mybir.dt.bfloat16
I32 = mybir.dt.int32
U32 = mybir.dt.uint32
AF = mybir.ActivationFunctionType
ALU = mybir.AluOpType
AX = mybir.AxisListType

D = 1024
DEPTH = 4
EPS = 1e-6
NEXP = 16384
PAST = 4096
TS = 32
QSCALE = 192.0 ** -0.5
GC1 = 0.044715
GC2 = 1.5957691216057308
ARENA = 42000


class Tok:
    __slots__ = ("t", "lw", "rd", "name")

    def __init__(self, t=None, name=""):
        self.t = t
        self.lw = {}
        self.rd = {}
        self.name = name


class Rec:
    def __init__(self):
        self.call = None

    def __getattr__(self, name):
        def f(*a, **k):
            self.call = (name, a, k)
            return self
        return f


def _rec(fn):
    r = Rec()
    fn(r)
    return r.call


class Sched:
    ENG = ("pe", "dve", "act", "pool", "sp")
    NSLOT = {"sp": 24, "pool": 24}

    def __init__(self, nc):
        self.nc = nc
        self.prog = {k: [] for k in self.ENG}
        self.sem = {k: nc.alloc_semaphore(f"s_{k}") for k in self.ENG}
        self.cnt = {k: 0 for k in self.ENG}
        self.waited = {k: {} for k in self.ENG}
        self.slots = {q: [[nc.alloc_semaphore(f"d_{q}{i}"), 0] for i in range(n)] for q, n in self.NSLOT.items()}
        self.slot_i = {q: 0 for q in self.NSLOT}
        self.semobj = {}
        for k in self.ENG:
            self.semobj[("e", k)] = self.sem[k]
        for q, sl in self.slots.items():
            for i, s in enumerate(sl):
                self.semobj[("d", q, i)] = s[0]
        self.out_events = []
        self.n_inst = 0

    def sb(self, name, shape, dtype):
        return Tok(self.nc.alloc_sbuf_tensor(name, list(shape), dtype), name)

    def ps(self, name, shape, dtype):
        return Tok(self.nc.alloc_psum_tensor(name, list(shape), dtype), name)

    def _wait(self, eng, key, val):
        if eng == "pe" and key == ("e", "pe"):
            return
        w = self.waited[eng]
        if w.get(key, 0) >= val:
            return
        w[key] = val
        self.prog[eng].append(("w", key, val))

    def _deps(self, eng, reads, writes):
        for t in reads:
            for k, v in t.lw.items():
                self._wait(eng, k, v)
        for t in writes:
            for k, v in t.lw.items():
                self._wait(eng, k, v)
            for k, v in t.rd.items():
                self._wait(eng, k, v)

    def _commit(self, ev, reads, writes):
        k, v = ev
        for t in reads:
            if t.rd.get(k, 0) < v:
                t.rd[k] = v
        for t in writes:
            t.lw = {k: v}
            t.rd = {}

    def _ser(self, eng):
        if not os.environ.get("KSER"):
            return
        for k in self.ENG:
            if self.cnt[k] > 0:
                self._wait(eng, ("e", k), self.cnt[k])
        for q, sl in self.slots.items():
            for i, s in enumerate(sl):
                if s[1] > 0:
                    self._wait(eng, ("d", q, i), s[1])

    def op(self, eng, fn, reads=(), writes=()):
        self._ser(eng)
        self._deps(eng, reads, writes)
        self.cnt[eng] += 1
        ev = (("e", eng), self.cnt[eng])
        self.prog[eng].append(("o", _rec(fn), ev[0], 1))
        self._commit(ev, reads, writes)
        self.n_inst += 1
        return ev

    def _dma_common(self, q, fn, reads, writes, out_dram):
        self._ser(q)
        self._deps(q, reads, writes)
        i = self.slot_i[q]
        self.slot_i[q] = (i + 1) % len(self.slots[q])
        s = self.slots[q][i]
        key = ("d", q, i)
        if s[1] > 0:
            self._wait(q, key, s[1])
        s[1] += 16
        ev = (key, s[1])
        self.prog[q].append(("o", _rec(fn), key, 16))
        self._commit(ev, reads, writes)
        if out_dram:
            self.out_events.append(ev)
        self.n_inst += 1
        return ev

    def dma(self, q, out, in_, reads=(), writes=(), out_dram=False, **kw):
        return self._dma_common(q, lambda e: e.dma_start(out=out, in_=in_, **kw), reads, writes, out_dram)

    def gather(self, out, table, idx_ap, reads=(), writes=()):
        def fn(e):
            return e.indirect_dma_start(out=out, out_offset=None, in_=table,
                                        in_offset=bass.IndirectOffsetOnAxis(ap=idx_ap, axis=0))
        return self._dma_common("pool", fn, reads, writes, False)

    def barrier(self):
        evs = [(("e", k), self.cnt[k]) for k in self.ENG if self.cnt[k] > 0]
        for q, sl in self.slots.items():
            for i, s in enumerate(sl):
                if s[1] > 0:
                    evs.append((("d", q, i), s[1]))
        for e in self.ENG:
            for k, v in evs:
                self._wait(e, k, v)

    def finish(self):
        for (k, v) in self.out_events:
            self._wait("sp", k, v)
        nc = self.nc
        S = self
        with nc.Block() as block:
            def replay(eng_name):
                def body(e):
                    for item in S.prog[eng_name]:
                        if item[0] == "w":
                            e.wait_ge(S.semobj[item[1]], item[2])
                        else:
                            _, (name, a, k), key, inc = item
                            getattr(e, name)(*a, **k).then_inc(S.semobj[key], inc)
                return body

            block.tensor(replay("pe"))
            block.vector(replay("dve"))
            block.scalar(replay("act"))
            block.gpsimd(replay("pool"))
            block.sync(replay("sp"))


class Seq:
    def __init__(self, name, T, nt, past):
        self.name, self.T, self.nt, self.past = name, T, nt, past
        self.ntiles = T // nt


class K:
    def __init__(self, SEQ, depth=DEPTH):
        self.SEQ = SEQ
        self.depth = depth
        nc = self.nc = bass.Bass("TRN2", target_bir_lowering=False)
        S = self.S = Sched(nc)
        self.seqs = [Seq("p", SEQ, 128, 0), Seq("s", TS, TS, PAST)]
        self.d = {}
        di = lambda n, sh, dt=F32: self.d.__setitem__(n, nc.dram_tensor(n, list(sh), dt, kind="ExternalInput"))
        do = lambda n, sh, dt=F32: self.d.__setitem__(n, nc.dram_tensor(n, list(sh), dt, kind="ExternalOutput"))
        di("xp", [SEQ, D]); di("xs", [TS, D])
        di("st_hg", [2, 8, 128, 128]); di("st_rh", [1, D]); di("st_rc", [1, 3, D])
        di("c_ckv", [1, PAST, 256]); di("c_kr", [1, PAST, 64])
        di("cst", [128, 272]); di("cos_p", [SEQ, 32]); di("sin_p", [SEQ, 32]); di("cos_s", [TS, 32]); di("sin_s", [TS, 32])
        di("norm_mix", [4, D]); di("norm_ffn", [4, D]); di("norm_final", [1, D])
        di("hg_w_in", [2, D, 4096]); di("hg_lb_logits", [4, D]); di("hg_norm", [2, D]); di("hg_w_out", [2, D, D])
        di("rg_w_in", [1, D, 2048]); di("rg_conv_w", [1, 4, D]); di("rg_conv_b", [1, D])
        di("rg_w_a", [1, 8, 128, 128]); di("rg_b_a", [1, D]); di("rg_w_x", [1, 8, 128, 128]); di("rg_b_x", [1, D])
        di("rg_lambda", [1, D]); di("rg_w_out", [1, D, D])
        di("mla_w_down", [1, D, 704]); di("mla_q_norm", [1, 384]); di("mla_w_uq", [1, 384, 1536])
        di("mla_kv_norm", [1, 256]); di("mla_w_uk", [1, 256, 8, 128]); di("mla_w_uv", [1, 256, 8, 128])
        di("mla_w_out", [1, D, D])
        di("pk_w_query", [4, D, D]); di("pk_sub_keys", [4, 8, 2, 128, 64])
        di("pk_u", [depth, NEXP, D]); di("pk_v", [depth, NEXP, D])
        do("yp", [SEQ, D]); do("ys", [TS, D])
        do("hgp", [2, 8, 128, 128]); do("hgs", [2, 8, 128, 128])
        do("rhp", [1, D]); do("rhs", [1, D]); do("rcp", [1, 3, D]); do("rcs", [1, 3, D])
        do("ckp", [1, SEQ, 256]); do("cks", [1, TS, 256]); do("krp", [1, SEQ, 64]); do("krs", [1, TS, 64])
        self.dbg = os.environ.get("KDBG", "")
        if self.dbg:
            do("dbg", [128, 2048])
        self.scr = {"p": nc.dram_tensor("scr_p", [SEQ, D], F32, kind="Internal"),
                    "s": nc.dram_tensor("scr_s", [TS, D], F32, kind="Internal")}
        self.uvb = nc.dram_tensor("uvb", [depth * NEXP, 2 * D], BF16, kind="Internal")
        self.uvtok = Tok(None, "uvb")
        self.scr_tok = {(s.name, i): Tok(None, "scr") for s in self.seqs for i in range(s.ntiles)}
        self.ident = S.sb("ident", [128, 128], F32)
        self.maskU = S.sb("maskU", [128, 128], F32)
        self.ones = S.sb("ones", [128, 128], F32)
        self.iota16 = S.sb("iota16", [128, 16], F32)
        self.xb = [S.sb(f"xb{i}", [128, D], F32) for i in range(2)]
        self.xn = S.sb("xn", [128, D], F32)
        self.junk = S.sb("junk", [128, D], BF16)
        self.hT = S.sb("hT", [128, 8, 128], BF16)
        self.st = S.sb("st", [128, 8], F32)
        self.arena = nc.alloc_sbuf_tensor("arena", [128, ARENA], F32)
        self.psA = [S.ps(f"psA{i}", [128, 512], F32) for i in range(4)]
        self.psB = [S.ps(f"psB{i}", [128, 512], F32) for i in range(4)]
        self.psi = 0
        self.ps_n = 4
        self.xi = 0
        self.consts()
        self.aoff = 0
        self.prepass()
        for L in range(depth):
            kind = L % 3
            S.barrier()
            self.aoff = 0
            self.ps_n = 4 if kind == 2 else 8
            if kind == 0:
                self.hgrn_pass(L)
            elif kind == 1:
                self.rglru_pass(L)
            else:
                self.mla_pass(L)
            S.barrier()
            self.aoff = 0
            self.ps_n = 4
            self.peer_pass(L, last=(L == depth - 1))
        S.finish()

    def av(self, name, shape, dtype):
        n = int(np.prod(shape[1:]))
        n4 = n if dtype in (F32, I32, U32) else (n + 1) // 2
        ap = self.arena[:, self.aoff:self.aoff + n4]
        self.aoff += n4
        assert self.aoff <= ARENA, (name, self.aoff)
        if dtype != F32:
            ap = ap.bitcast(dtype)
            if dtype == BF16 and n % 2:
                ap = ap[:, 0:n]
        if len(shape) == 3:
            ap = ap.rearrange("p (a b) -> p a b", a=shape[1])
        elif len(shape) == 4:
            ap = ap.rearrange("p (a b c) -> p a b c", a=shape[1], b=shape[2])
        return Tok(ap, name)

    def psum(self):
        pool = self.psA if self.ps_n == 4 else (self.psA + self.psB)
        t = pool[self.psi % self.ps_n]
        self.psi += 1
        return t

    def mm(self, out_t, out_ap, a_t, a_ap, b_t, b_ap, start=True, stop=True):
        self.S.op("pe", lambda e: e.matmul(out_ap, a_ap, b_ap, start=start, stop=stop),
                  reads=[a_t, b_t], writes=[out_t])

    def tr(self, out_t, out_ap, in_t, in_ap, n_in_part):
        idn = self.ident
        self.S.op("pe", lambda e: e.transpose(out_ap, in_ap, idn.t[:n_in_part, :n_in_part]),
                  reads=[in_t, idn], writes=[out_t])

    def V(self, fn, r=(), w=()):
        self.S.op("dve", fn, r, w)

    def A(self, fn, r=(), w=()):
        self.S.op("act", fn, r, w)

    def consts(self):
        S = self.S
        c = self.d["cst"]
        S.dma("sp", self.ident.t[:], c[:, 0:128], (), [self.ident])
        S.dma("sp", self.maskU.t[:], c[:, 128:256], (), [self.maskU])
        S.dma("sp", self.iota16.t[:], c[:, 256:272], (), [self.iota16])
        self.V(lambda e: e.memset(self.ones.t[:], 1.0), (), [self.ones])

    def bc(self, row_ap, n=128):
        return row_ap.partition_broadcast(n)

    def load_w(self, dst, src_ap):
        kc = dst.t.shape[1]
        N = dst.t.shape[2]
        src = src_ap.rearrange("(kc p) n -> p kc n", p=128)
        q = "pool" if dst.t.dtype != F32 else "sp"
        for n0 in range(0, N, 1024):
            n1 = min(N, n0 + 1024)
            self.S.dma(q, dst.t[:, :, n0:n1], src[:, :, n0:n1], (), [dst])

    def load_x(self, seq, i, L, buf):
        src = (self.d["xp" if seq.name == "p" else "xs"] if L < 0 else self.scr[seq.name])
        r0 = i * seq.nt
        self.S.dma("sp", buf.t[:seq.nt, :], src[r0:r0 + seq.nt, :], [self.scr_tok[(seq.name, i)]], [buf])

    def store_x(self, seq, i, buf):
        r0 = i * seq.nt
        self.S.dma("sp", self.scr[seq.name][r0:r0 + seq.nt, :], buf.t[:seq.nt, :], [buf], [self.scr_tok[(seq.name, i)]])

    def tiles(self, first_layer_src):
        lst = [(s, i) for s in self.seqs for i in range(s.ntiles)]
        bufs = {}
        b0 = self.xb[self.xi % 2]; self.xi += 1
        self.load_x(lst[0][0], lst[0][1], first_layer_src, b0)
        bufs[0] = b0
        for j, (s, i) in enumerate(lst):
            if j + 1 < len(lst):
                b = self.xb[self.xi % 2]; self.xi += 1
                self.load_x(lst[j + 1][0], lst[j + 1][1], first_layer_src, b)
                bufs[j + 1] = b
            yield s, i, bufs.pop(j)

    def rstd(self, src_t, src_ap, nt, n, out_ap, scratch_ap):
        st = self.st
        jk = self.junk
        self.V(lambda e: e.scalar_tensor_tensor(out=scratch_ap, in0=src_ap, scalar=1.0, in1=src_ap, op0=ALU.mult, op1=ALU.mult,
                                                accum_out=st.t[:nt, 0:1]), [src_t], [jk, st])
        self.V(lambda e: e.tensor_scalar(out=st.t[:nt, 1:2], in0=st.t[:nt, 0:1], scalar1=1.0 / n, scalar2=EPS,
                                         op0=ALU.mult, op1=ALU.add), [st], [st])
        self.A(lambda e: e.activation(out=st.t[:nt, 2:3], in_=st.t[:nt, 1:2], func=AF.Sqrt), [st], [st])
        self.V(lambda e: e.reciprocal(out=out_ap, in_=st.t[:nt, 2:3]), [st], [st])

    def norm_T(self, x, gB, nt):
        xn, hT, st = self.xn, self.hT, self.st
        self.rstd(x, x.t[:nt, :], nt, D, st.t[:nt, 3:4], self.junk.t[:nt, :])
        self.V(lambda e: e.scalar_tensor_tensor(out=xn.t[:nt, :], in0=x.t[:nt, :], scalar=st.t[:nt, 3:4], in1=gB.t[:nt, :],
                                                op0=ALU.mult, op1=ALU.mult), [x, st, gB], [xn])
        self.transpose_to(xn, xn.t, nt, 8, hT, lambda kc: hT.t[:, kc, :nt])

    def transpose_to(self, src, src_ap, nt, nch, dst, dst_fn, scale=None):
        for g0 in range(0, nch, 4):
            g1 = min(nch, g0 + 4)
            pb = self.psum()
            for kc in range(g0, g1):
                self.tr(pb, pb.t[:, (kc - g0) * 128:(kc - g0) * 128 + nt], src, src_ap[:nt, kc * 128:(kc + 1) * 128], nt)
            for kc in range(g0, g1):
                o = dst_fn(kc)
                i_ = pb.t[:, (kc - g0) * 128:(kc - g0) * 128 + nt]
                if scale is None:
                    self.V(lambda e, o=o, i_=i_: e.tensor_copy(out=o, in_=i_), [pb], [dst])
                else:
                    self.V(lambda e, o=o, i_=i_: e.tensor_scalar(out=o, in0=i_, scalar1=scale, scalar2=None, op0=ALU.mult), [pb], [dst])

    def gelu_tanh(self, dst_t, dst_ap, src_t, src_ap, tmp_t, tmp_ap):
        self.V(lambda e: e.tensor_tensor(out=tmp_ap, in0=src_ap, in1=src_ap, op=ALU.mult), [src_t], [tmp_t])
        self.V(lambda e: e.tensor_scalar(out=tmp_ap, in0=tmp_ap, scalar1=GC1, scalar2=1.0, op0=ALU.mult, op1=ALU.add), [tmp_t], [tmp_t])
        self.V(lambda e: e.tensor_tensor(out=tmp_ap, in0=tmp_ap, in1=src_ap, op=ALU.mult), [tmp_t, src_t], [tmp_t])
        self.A(lambda e: e.activation(out=tmp_ap, in_=tmp_ap, func=AF.Sigmoid, scale=GC2), [tmp_t], [tmp_t])
        self.V(lambda e: e.tensor_tensor(out=dst_ap, in0=tmp_ap, in1=src_ap, op=ALU.mult), [tmp_t, src_t], [dst_t])

    def add_resid(self, x, nt, ybanks):
        for j, pb in enumerate(ybanks):
            self.V(lambda e, pb=pb, j=j: e.tensor_tensor(out=x.t[:nt, j * 512:(j + 1) * 512], in0=x.t[:nt, j * 512:(j + 1) * 512],
                                                         in1=pb.t[:nt, :], op=ALU.add), [x, pb], [x])

    def out_proj(self, lhs_t, lhs_fn, w, nt, x):
        banks = []
        for j in range(2):
            pb = self.psum()
            for kc in range(8):
                self.mm(pb, pb.t[:nt, :], lhs_t, lhs_fn(kc), w, w.t[:, kc, j * 512:(j + 1) * 512], kc == 0, kc == 7)
            banks.append(pb)
        self.add_resid(x, nt, banks)

    def prepass(self):
        S, d = self.S, self.d
        stg = [self.av(f"stg{i}", [128, 4, 2 * D], BF16) for i in range(2)]
        k = 0
        for l in range(self.depth):
            for c in range(NEXP // 512):
                r0 = c * 512
                st = stg[k % 2]; k += 1
                S.dma("pool", st.t[:, :, 0:D], d["pk_u"][l, r0:r0 + 512, :].rearrange("(p a) d -> p a d", a=4), (), [st])
                S.dma("pool", st.t[:, :, D:2 * D], d["pk_v"][l, r0:r0 + 512, :].rearrange("(p a) d -> p a d", a=4), (), [st])
                S.dma("sp", self.uvb[l * NEXP + r0:l * NEXP + r0 + 512, :].rearrange("(p a) f -> p a f", a=4), st.t[:], [st], [self.uvtok])

    def peer_pass(self, L, last):
        S, d = self.S, self.d
        wq = self.av("wq", [128, 8, D], BF16)
        BD = self.av("BD", [128, 8, 256], F32)
        gB = self.av("gffn", [128, D], F32)
        kraw = self.av("kraw", [128, 16, 64], F32)
        G = [self.av(f"G{i}", [128, 2 * D], BF16) for i in range(16)]
        dg = [self.av(f"dg{i}", [128, 128], BF16) for i in range(4)]
        tmpg = self.av("tmpg", [128, 128], F32)
        bufA = self.av("bufA", [128, 2048], F32)
        bufB = self.av("bufB", [128, 2048], F32)
        bufC = self.av("bufC", [128, 2048], F32)
        qT = self.av("qT", [128, 8, 128], F32)
        sv = self.av("sv", [128, 8, 2, 16], F32)
        si = self.av("si", [128, 8, 2, 16], U32)
        sif = self.av("sif", [128, 8, 2, 16], F32)
        ts = self.av("ts", [128, 8, 16], F32)
        tj = self.av("tj", [128, 8, 16], U32)
        ab = self.av("ab", [128, 2, 128], I32)
        abf = self.av("abf", [128, 2, 128], F32)
        sel = self.av("sel", [128, 2, 128], F32)
        eidx = self.av("eidx", [128, 128], I32)
        gate = self.av("gate", [128, 8, 16], F32)
        z = self.av("z", [128, 2, 8], F32)
        actp = self.av("actp", [128, 128], F32)
        acth = self.av("acth", [128, 128], F32)
        wgt = self.av("wgt", [128, 128], F32)
        if last:
            self.gfin = self.av("gfin", [128, D], F32)
            S.dma("sp", self.gfin.t[:], self.bc(d["norm_final"][0, :]), (), [self.gfin])
        self.load_w(wq, d["pk_w_query"][L])
        S.dma("sp", gB.t[:], self.bc(d["norm_ffn"][L, :]), (), [gB])
        S.dma("sp", kraw.t[:], d["pk_sub_keys"][L].rearrange("h p k d -> k (h p) d"), (), [kraw])
        self.V(lambda e: e.memset(BD.t[:], 0.0), (), [BD])
        for h in range(8):
            pb = self.psum()
            self.tr(pb, pb.t[:, 0:128], kraw, kraw.t[:, 2 * h:2 * h + 2, :].rearrange("k a d -> k (a d)"), 128)
            self.V(lambda e, h=h, pb=pb: e.tensor_copy(out=BD.t[0:64, h, 0:128], in_=pb.t[0:64, 0:128]), [pb], [BD])
            self.V(lambda e, h=h, pb=pb: e.tensor_copy(out=BD.t[64:128, h, 128:256], in_=pb.t[64:128, 0:128]), [pb], [BD])
        uvb = self.uvb.ap()
        gi = 0
        di = 0
        for seq, i, x in self.tiles(L):
            nt = seq.nt
            self.norm_T(x, gB, nt)
            xn, hT = self.xn, self.hT
            if "nopeer" in self.dbg:
                self.peer_tail(seq, i, x, nt, last)
                continue
            for c in range(8):
                pb = self.psum()
                for kc in range(8):
                    self.mm(pb, pb.t[:, :nt], wq, wq.t[:, kc, c * 128:(c + 1) * 128], hT, hT.t[:, kc, :nt], kc == 0, kc == 7)
                self.V(lambda e, c=c, pb=pb: e.tensor_copy(out=qT.t[:, c, :nt], in_=pb.t[:, :nt]), [pb], [qT])
            for j in range(4):
                pb = self.psum()
                for hh in range(2):
                    h = 2 * j + hh
                    self.mm(pb, pb.t[:nt, hh * 256:(hh + 1) * 256], qT, qT.t[:, h, :nt], BD, BD.t[:, h, :])
                self.V(lambda e, j=j, pb=pb: e.tensor_copy(out=bufA.t[:nt, j * 512:(j + 1) * 512], in_=pb.t[:nt, :]), [pb], [bufA])
            sW = bufA.t.rearrange("p (g k) -> p g k", g=16)
            sW2 = bufB.t.rearrange("p (g k) -> p g k", g=16)
            for g in range(16):
                h, p = g // 2, g % 2
                self.V(lambda e, g=g, h=h, p=p: e.max(out=sv.t[:nt, h, p, 0:8], in_=sW[:nt, g, :]), [bufA], [sv])
                self.V(lambda e, g=g, h=h, p=p: e.match_replace(out=sW2[:nt, g, :], in_to_replace=sv.t[:nt, h, p, 0:8],
                                                               in_values=sW[:nt, g, :], imm_value=-3.0e38), [bufA, sv], [bufB])
                self.V(lambda e, g=g, h=h, p=p: e.max(out=sv.t[:nt, h, p, 8:16], in_=sW2[:nt, g, :]), [bufB], [sv])
                self.V(lambda e, g=g, h=h, p=p: e.max_index(out=si.t[:nt, h, p, 0:8], in_max=sv.t[:nt, h, p, 0:8],
                                                           in_values=sW[:nt, g, :]), [bufA, sv], [si])
                self.V(lambda e, g=g, h=h, p=p: e.max_index(out=si.t[:nt, h, p, 8:16], in_max=sv.t[:nt, h, p, 8:16],
                                                           in_values=sW2[:nt, g, :]), [bufB, sv], [si])
            cand = bufA.t.rearrange("p (h a b) -> p h a b", h=8, a=16)
            cand2 = bufB.t.rearrange("p (h a b) -> p h a b", h=8, a=16)
            candf = bufA.t.rearrange("p (h n) -> p h n", h=8)
            cand2f = bufB.t.rearrange("p (h n) -> p h n", h=8)
            self.V(lambda e: e.tensor_tensor(out=cand[:nt], in0=sv.t[:nt, :, 0, :].unsqueeze(3).broadcast_to([nt, 8, 16, 16]),
                                             in1=sv.t[:nt, :, 1, :].unsqueeze(2).broadcast_to([nt, 8, 16, 16]), op=ALU.add), [sv], [bufA])
            for h in range(8):
                self.V(lambda e, h=h: e.max(out=ts.t[:nt, h, 0:8], in_=candf[:nt, h, :]), [bufA], [ts])
                self.V(lambda e, h=h: e.match_replace(out=cand2f[:nt, h, :], in_to_replace=ts.t[:nt, h, 0:8],
                                                      in_values=candf[:nt, h, :], imm_value=-3.0e38), [bufA, ts], [bufB])
                self.V(lambda e, h=h: e.max(out=ts.t[:nt, h, 8:16], in_=cand2f[:nt, h, :]), [bufB], [ts])
                self.V(lambda e, h=h: e.max_index(out=tj.t[:nt, h, 0:8], in_max=ts.t[:nt, h, 0:8], in_values=candf[:nt, h, :]), [bufA, ts], [tj])
                self.V(lambda e, h=h: e.max_index(out=tj.t[:nt, h, 8:16], in_max=ts.t[:nt, h, 8:16], in_values=cand2f[:nt, h, :]), [bufB, ts], [tj])
            self.V(lambda e: e.tensor_tensor(out=gate.t[:nt], in0=ts.t[:nt], in1=ts.t[:nt, :, 0:1].broadcast_to([nt, 8, 16]),
                                             op=ALU.subtract), [ts], [gate])
            self.A(lambda e: e.activation(out=gate.t[:nt], in_=gate.t[:nt], func=AF.Exp), [gate], [gate])
            self.V(lambda e: e.tensor_reduce(out=z.t[:nt, 0, :], in_=gate.t[:nt], axis=AX.X, op=ALU.add), [gate], [z])
            self.V(lambda e: e.reciprocal(out=z.t[:nt, 1, :], in_=z.t[:nt, 0, :]), [z], [z])
            self.V(lambda e: e.tensor_tensor(out=gate.t[:nt], in0=gate.t[:nt], in1=z.t[:nt, 1, :].unsqueeze(2).broadcast_to([nt, 8, 16]),
                                             op=ALU.mult), [gate, z], [gate])
            tji = tj.t.bitcast(I32).rearrange("p h k -> p (h k)")
            self.V(lambda e: e.tensor_single_scalar(out=ab.t[:nt, 0, :], in_=tji[:nt], scalar=4, op=ALU.arith_shift_right), [tj], [ab])
            self.V(lambda e: e.tensor_single_scalar(out=ab.t[:nt, 1, :], in_=tji[:nt], scalar=15, op=ALU.bitwise_and), [tj], [ab])
            self.V(lambda e: e.tensor_copy(out=abf.t[:nt], in_=ab.t[:nt]), [ab], [abf])
            self.V(lambda e: e.tensor_copy(out=sif.t[:nt], in_=si.t[:nt]), [si], [sif])
            oh = bufC.t.rearrange("p (r a) -> p r a", a=16)
            oh4 = bufC.t.rearrange("p (h k a) -> p h k a", h=8, k=16)
            for w_ in range(2):
                self.V(lambda e, w_=w_: e.tensor_tensor(out=oh[:nt], in0=abf.t[:nt, w_, :].unsqueeze(2).broadcast_to([nt, 128, 16]),
                                                        in1=self.iota16.t[:nt].unsqueeze(1).broadcast_to([nt, 128, 16]), op=ALU.is_equal),
                       [abf, self.iota16], [bufC])
                self.V(lambda e, w_=w_: e.tensor_tensor(out=oh4[:nt], in0=oh4[:nt],
                                                        in1=sif.t[:nt, :, w_, :].unsqueeze(2).broadcast_to([nt, 8, 16, 16]), op=ALU.mult),
                       [sif, bufC], [bufC])
                self.V(lambda e, w_=w_: e.tensor_reduce(out=sel.t[:nt, w_, :], in_=oh[:nt], axis=AX.X, op=ALU.add), [bufC], [sel])
            self.V(lambda e: e.scalar_tensor_tensor(out=abf.t[:nt, 0, :], in0=sel.t[:nt, 0, :], scalar=128.0, in1=sel.t[:nt, 1, :],
                                                    op0=ALU.mult, op1=ALU.add), [sel], [abf])
            if L:
                self.V(lambda e: e.tensor_scalar(out=abf.t[:nt, 0, :], in0=abf.t[:nt, 0, :], scalar1=float(L * NEXP), scalar2=None, op0=ALU.add), [abf], [abf])
            self.V(lambda e: e.tensor_copy(out=eidx.t[:nt], in_=abf.t[:nt, 0, :]), [abf], [eidx])
            if "nogather" in self.dbg:
                if i == 0 and seq.name == "p":
                    S.dma("sp", d["dbg"][:, 0:128], abf.t[:, 0, :], [abf], (), out_dram=True)
                    S.dma("sp", d["dbg"][:, 128:256], gate.t[:].rearrange("p h k -> p (h k)"), [gate], (), out_dram=True)
                    S.dma("sp", d["dbg"][:, 256:512], sv.t[:].rearrange("p h a k -> p (h a k)"), [sv], (), out_dram=True)
                    S.dma("sp", d["dbg"][:, 512:768], sif.t[:].rearrange("p h a k -> p (h a k)"), [sif], (), out_dram=True)
                    S.dma("sp", d["dbg"][:, 768:896], ts.t[:].rearrange("p h k -> p (h k)"), [ts], (), out_dram=True)
                self.peer_tail(seq, i, x, nt, last)
                continue
            acc = [self.psB[0], self.psB[1]]
            gflat = gate.t.rearrange("p h k -> p (h k)")
            for g in range(16):
                grp = []
                for j in range(8):
                    r = g * 8 + j
                    g_ = G[gi % 16]; gi += 1
                    grp.append(g_)
                    S.gather(g_.t[:nt, :], uvb, eidx.t[:nt, r:r + 1], [eidx, self.uvtok], [g_])
                    self.V(lambda e, g_=g_, r=r: e.scalar_tensor_tensor(out=self.junk.t[:nt, :], in0=g_.t[:nt, 0:D], scalar=1.0, in1=xn.t[:nt, :],
                                                                       op0=ALU.mult, op1=ALU.mult, accum_out=actp.t[:nt, r:r + 1]),
                           [g_, xn], [self.junk, actp])
                c0_, c1_ = g * 8, g * 8 + 8
                self.gelu_tanh(acth, acth.t[:nt, c0_:c1_], actp, actp.t[:nt, c0_:c1_], tmpg, tmpg.t[:nt, c0_:c1_])
                self.V(lambda e: e.tensor_tensor(out=wgt.t[:nt, c0_:c1_], in0=acth.t[:nt, c0_:c1_], in1=gflat[:nt, c0_:c1_], op=ALU.mult),
                       [acth, gate], [wgt])
                for j in range(8):
                    r = g * 8 + j
                    g_ = grp[j]
                    dgb = dg[di % 4]; di += 1
                    self.V(lambda e, dgb=dgb, r=r: e.tensor_scalar(out=dgb.t[:nt, :nt], in0=self.ident.t[:nt, :nt], scalar1=wgt.t[:nt, r:r + 1], scalar2=None,
                                                                  op0=ALU.mult), [self.ident, wgt], [dgb])
                    for hf in range(2):
                        self.mm(acc[hf], acc[hf].t[:nt, :], dgb, dgb.t[:nt, :nt], g_, g_.t[:nt, D + hf * 512:D + (hf + 1) * 512], r == 0, r == 127)
            self.add_resid(x, nt, acc)
            self.peer_tail(seq, i, x, nt, last)

    def peer_tail(self, seq, i, x, nt, last):
        S, d, xn = self.S, self.d, self.xn
        if True:
            if not last:
                self.store_x(seq, i, x)
            else:
                self.rstd(x, x.t[:nt, :], nt, D, self.st.t[:nt, 3:4], self.junk.t[:nt, :])
                self.V(lambda e: e.scalar_tensor_tensor(out=xn.t[:nt, :], in0=x.t[:nt, :], scalar=self.st.t[:nt, 3:4], in1=self.gfin.t[:nt, :],
                                                        op0=ALU.mult, op1=ALU.mult), [x, self.st, self.gfin], [xn])
                dst = d["yp" if seq.name == "p" else "ys"]
                S.dma("sp", dst[i * nt:(i + 1) * nt, :], xn.t[:nt, :], [xn], (), out_dram=True)

    def hgrn_pass(self, L):
        S, d = self.S, self.d
        slot = L // 3
        w_in = self.av("hg_w_in", [128, 8, 4096], BF16)
        w_out = self.av("hg_w_out", [128, 8, D], BF16)
        gB = self.av("gmix", [128, D], F32)
        gnB = self.av("gn", [128, D], F32)
        St = self.av("Sst", [128, 8, 128], F32)
        lbl = self.av("lbl", [128, 4, 8], F32)
        oml = self.av("oml", [128, 8], F32)
        qTa = self.av("qTa", [128, 8, 128], F32)
        kTa = self.av("kTa", [128, 8, 128], F32)
        lfa = self.av("lfa", [128, 8, 128], F32)
        bcA = self.av("bcA", [128, 8, 64], F32)
        dA = self.av("dA", [128, 8, 64], F32)
        exA = self.av("exA", [128, 3, 8, 64], F32)
        scA = self.av("scA", [128, 8, 2], F32)
        sc5 = self.av("sc5", [128, 8], F32)
        qt_l = [self.av(f"qt{i}", [128, 64], BF16) for i in range(3)]
        kt_l = [self.av(f"kt{i}", [128, 64], BF16) for i in range(3)]
        kh_l = [self.av(f"kh{i}", [128, 64], F32) for i in range(3)]
        khat_l = [self.av(f"khat{i}", [64, 128], BF16) for i in range(3)]
        scT_l = [self.av(f"scT{i}", [64, 64], BF16) for i in range(3)]
        Sr_l = [self.av(f"Sr{i}", [128, 128], BF16) for i in range(3)]
        iv = self.av("iv", [64, D], BF16)
        sg = self.av("sg", [64, D], F32)
        o_sb = self.av("o_sb", [64, D], F32)
        onT = self.av("onT", [128, 8, 128], BF16)
        self.load_w(w_in, d["hg_w_in"][slot])
        self.load_w(w_out, d["hg_w_out"][slot])
        S.dma("sp", gB.t[:], self.bc(d["norm_mix"][L, :]), (), [gB])
        S.dma("sp", gnB.t[:], self.bc(d["hg_norm"][slot, :]), (), [gnB])
        for l_ in range(4):
            S.dma("sp", lbl.t[:, l_, :], d["hg_lb_logits"][l_, :].rearrange("(h p) -> p h", p=128), (), [lbl], allow_slow_non_contiguous=True)
        self.A(lambda e: e.activation(out=lbl.t[:], in_=lbl.t[:], func=AF.Exp), [lbl], [lbl])
        self.V(lambda e: e.tensor_reduce(out=oml.t[:], in_=lbl.t[:].rearrange("p l h -> p h l"), axis=AX.X, op=ALU.add), [lbl], [oml])
        self.V(lambda e: e.reciprocal(out=oml.t[:], in_=oml.t[:]), [oml], [oml])
        if L == 0:
            self.V(lambda e: e.memset(oml.t[:], 1.0), (), [oml])
        else:
            self.V(lambda e: e.tensor_reduce(out=sc5.t[:], in_=lbl.t[:, 1:L + 1, :].rearrange("p l h -> p h l"), axis=AX.X, op=ALU.add), [lbl], [sc5])
            self.V(lambda e: e.tensor_tensor(out=sc5.t[:], in0=sc5.t[:], in1=oml.t[:], op=ALU.mult), [sc5, oml], [sc5])
            self.V(lambda e: e.tensor_scalar(out=oml.t[:], in0=sc5.t[:], scalar1=-1.0, scalar2=1.0, op0=ALU.mult, op1=ALU.add), [sc5], [oml])
        cur = None
        for seq, i, x in self.tiles(L - 1 if L == 0 else L):
            nt = seq.nt
            if cur is not seq:
                if cur is not None:
                    self.hg_out(cur, slot, St)
                cur = seq
                if seq.name == "p":
                    self.V(lambda e: e.memset(St.t[:], 0.0), (), [St])
                else:
                    S.dma("sp", St.t[:], d["st_hg"][slot].rearrange("h f i -> f h i"), (), [St])
            self.norm_T(x, gB, nt)
            hT = self.hT
            for h in range(8):
                pq = self.psum()
                pf_ = self.psum()
                for kc in range(8):
                    self.mm(pq, pq.t[:, :nt], w_in, w_in.t[:, kc, h * 128:(h + 1) * 128], hT, hT.t[:, kc, :nt], kc == 0, kc == 7)
                for kc in range(8):
                    self.mm(pf_, pf_.t[:, :nt], w_in, w_in.t[:, kc, 1024 + h * 128:1024 + (h + 1) * 128], hT, hT.t[:, kc, :nt], kc == 0, kc == 7)
                self.V(lambda e, h=h, pq=pq: e.tensor_copy(out=qTa.t[:, h, :nt], in_=pq.t[:, :nt]), [pq], [qTa])
                self.V(lambda e, h=h, pf_=pf_: e.tensor_copy(out=kTa.t[:, h, :nt], in_=pf_.t[:, :nt]), [pf_], [kTa])
            self.A(lambda e: e.activation(out=qTa.t[:, :, :nt], in_=qTa.t[:, :, :nt], func=AF.Silu), [qTa], [qTa])
            self.A(lambda e: e.activation(out=kTa.t[:, :, :nt], in_=kTa.t[:, :, :nt], func=AF.Sigmoid, scale=-1.0), [kTa], [kTa])
            self.V(lambda e: e.tensor_tensor(out=kTa.t[:, :, :nt], in0=kTa.t[:, :, :nt], in1=oml.t[:, :].unsqueeze(2).broadcast_to([128, 8, nt]),
                                             op=ALU.mult), [kTa, oml], [kTa])
            self.A(lambda e: e.activation(out=lfa.t[:, :, :nt], in_=kTa.t[:, :, :nt], func=AF.Ln, scale=-1.0, bias=1.0), [kTa], [lfa])
            CL = min(int(os.environ.get("KCL", "64")), nt)
            for c0 in range(0, nt, CL):
                for j in range(4):
                    pb = self.psum()
                    for kc in range(8):
                        self.mm(pb, pb.t[:CL, :], hT, hT.t[:, kc, c0:c0 + CL], w_in, w_in.t[:, kc, 2048 + j * 512:2048 + (j + 1) * 512], kc == 0, kc == 7)
                    if j < 2:
                        self.V(lambda e, j=j, pb=pb: e.tensor_copy(out=iv.t[:CL, j * 512:(j + 1) * 512], in_=pb.t[:CL, :]), [pb], [iv])
                    else:
                        self.V(lambda e, j=j, pb=pb: e.tensor_copy(out=sg.t[:CL, (j - 2) * 512:(j - 1) * 512], in_=pb.t[:CL, :]), [pb], [sg])
                self.A(lambda e: e.activation(out=sg.t[:CL, :], in_=sg.t[:CL, :], func=AF.Silu), [sg], [sg])
                mid = CL // 2 - 1
                for h in range(8):
                    self.V(lambda e, h=h: e.tensor_tensor_scan(out=bcA.t[:, h, :CL], data0=self.ones.t[:, :CL], data1=lfa.t[:, h, c0:c0 + CL],
                                                               initial=0.0, op0=ALU.mult, op1=ALU.add), [self.ones, lfa], [bcA])
                self.V(lambda e: e.tensor_copy(out=scA.t[:, :, 0:1], in_=bcA.t[:, :, mid:mid + 1]), [bcA], [scA])
                self.V(lambda e: e.tensor_copy(out=scA.t[:, :, 1:2], in_=bcA.t[:, :, CL - 1:CL]), [bcA], [scA])
                self.V(lambda e: e.tensor_tensor(out=dA.t[:, :, :CL], in0=bcA.t[:, :, :CL], in1=scA.t[:, :, 0:1].broadcast_to([128, 8, CL]),
                                                 op=ALU.subtract), [bcA, scA], [dA])
                self.A(lambda e: e.activation(out=exA.t[:, 0, :, :CL], in_=dA.t[:, :, :CL], func=AF.Exp), [dA], [exA])
                self.A(lambda e: e.activation(out=exA.t[:, 1, :, :CL], in_=dA.t[:, :, :CL], func=AF.Exp, scale=-1.0), [dA], [exA])
                self.V(lambda e: e.tensor_tensor(out=dA.t[:, :, :CL], in0=bcA.t[:, :, :CL], in1=scA.t[:, :, 1:2].broadcast_to([128, 8, CL]),
                                                 op=ALU.subtract), [bcA, scA], [dA])
                self.A(lambda e: e.activation(out=exA.t[:, 2, :, :CL], in_=dA.t[:, :, :CL], func=AF.Exp, scale=-1.0), [dA], [exA])
                self.A(lambda e: e.activation(out=scA.t[:, :, :], in_=scA.t[:, :, :], func=AF.Exp), [scA], [scA])
                for h in range(8):
                    qt, kt, kh, khat, scT, Sr = qt_l[h % 3], kt_l[h % 3], kh_l[h % 3], khat_l[h % 3], scT_l[h % 3], Sr_l[h % 3]
                    self.V(lambda e, h=h: e.tensor_tensor(out=qt.t[:, :CL], in0=qTa.t[:, h, c0:c0 + CL], in1=exA.t[:, 0, h, :CL], op=ALU.mult), [qTa, exA], [qt])
                    self.V(lambda e, h=h: e.tensor_tensor(out=kt.t[:, :CL], in0=kTa.t[:, h, c0:c0 + CL], in1=exA.t[:, 1, h, :CL], op=ALU.mult), [kTa, exA], [kt])
                    self.V(lambda e, h=h: e.tensor_tensor(out=kh.t[:, :CL], in0=kTa.t[:, h, c0:c0 + CL], in1=exA.t[:, 2, h, :CL], op=ALU.mult), [kTa, exA], [kh])
                    self.V(lambda e, h=h: e.tensor_scalar(out=Sr.t[:], in0=St.t[:, h, :], scalar1=scA.t[:, h, 0:1], scalar2=None, op0=ALU.mult), [St, scA], [Sr])
                    pk = self.psum()
                    self.tr(pk, pk.t[:CL, 0:128], kh, kh.t[:, :CL], 128)
                    self.V(lambda e, pk=pk: e.tensor_copy(out=khat.t[:CL, :], in_=pk.t[:CL, 0:128]), [pk], [khat])
                    psc = self.psum()
                    self.mm(psc, psc.t[:CL, :CL], kt, kt.t[:, :CL], qt, qt.t[:, :CL])
                    self.V(lambda e, psc=psc: e.tensor_tensor(out=scT.t[:CL, :CL], in0=psc.t[:CL, :CL], in1=self.maskU.t[:CL, :CL], op=ALU.mult),
                           [psc, self.maskU], [scT])
                    po = self.psum()
                    self.mm(po, po.t[:CL, 0:128], scT, scT.t[:CL, :CL], iv, iv.t[:CL, h * 128:(h + 1) * 128], True, False)
                    self.mm(po, po.t[:CL, 0:128], qt, qt.t[:, :CL], Sr, Sr.t[:], False, True)
                    self.V(lambda e, h=h, po=po: e.tensor_copy(out=o_sb.t[:CL, h * 128:(h + 1) * 128], in_=po.t[:CL, 0:128]), [po], [o_sb])
                    pn = self.psum()
                    self.mm(pn, pn.t[:, 0:128], khat, khat.t[:CL, :], iv, iv.t[:CL, h * 128:(h + 1) * 128])
                    self.V(lambda e, h=h, pn=pn: e.scalar_tensor_tensor(out=St.t[:, h, :], in0=St.t[:, h, :], scalar=scA.t[:, h, 1:2], in1=pn.t[:, 0:128],
                                                                       op0=ALU.mult, op1=ALU.add), [St, scA, pn], [St])
                self.rstd(o_sb, o_sb.t[:CL, :], CL, D, self.st.t[:CL, 4:5], self.junk.t[:CL, :])
                self.V(lambda e: e.scalar_tensor_tensor(out=o_sb.t[:CL, :], in0=o_sb.t[:CL, :], scalar=self.st.t[:CL, 4:5], in1=gnB.t[:CL, :],
                                                        op0=ALU.mult, op1=ALU.mult), [o_sb, self.st, gnB], [o_sb])
                self.V(lambda e: e.tensor_tensor(out=o_sb.t[:CL, :], in0=o_sb.t[:CL, :], in1=sg.t[:CL, :], op=ALU.mult), [o_sb, sg], [o_sb])
                self.transpose_to(o_sb, o_sb.t, CL, 8, onT, lambda kc, c0=c0: onT.t[:, kc, c0:c0 + CL])
            self.out_proj(onT, lambda kc: onT.t[:, kc, :nt], w_out, nt, x)
            self.store_x(seq, i, x)
        self.hg_out(cur, slot, St)

    def hg_out(self, seq, slot, St):
        dst = self.d["hgp" if seq.name == "p" else "hgs"]
        self.S.dma("sp", dst[slot].rearrange("h f i -> f h i"), St.t[:], [St], (), out_dram=True)

    def rglru_pass(self, L):
        S, d = self.S, self.d
        w_in = self.av("rg_w_in", [128, 8, 2048], BF16)
        w_out = self.av("rg_w_out", [128, 8, D], BF16)
        wa = self.av("rg_wa", [128, 8, 128], F32)
        wx = self.av("rg_wx", [128, 8, 128], F32)
        gB = self.av("gmix", [128, D], F32)
        cw = self.av("cw", [128, 4, 8], F32)
        pv = self.av("pv", [128, 5, 8], F32)
        ub = self.av("ub", [128, 8, 3 + 128], F32)
        hprev = self.av("hprev", [128, 8], F32)
        g1 = self.av("g1", [128, 8, 128], F32)
        tmp = self.av("tmp", [128, 8, 128], F32)
        gel = self.av("gel", [128, 8, 128], F32)
        xc = self.av("xc", [128, 8, 128], F32)
        rr = self.av("rr", [128, 8, 128], F32)
        ig = self.av("ig", [128, 8, 128], F32)
        aa = self.av("aa", [128, 8, 128], F32)
        mm_ = self.av("mm_", [128, 8, 128], F32)
        hb = self.av("hb", [128, 8, 128], F32)
        hgT = self.av("hgT", [128, 8, 128], BF16)
        self.load_w(w_in, d["rg_w_in"][0])
        self.load_w(w_out, d["rg_w_out"][0])
        S.dma("sp", wa.t[:], d["rg_w_a"][0].rearrange("n c d -> c n d"), (), [wa])
        S.dma("sp", wx.t[:], d["rg_w_x"][0].rearrange("n c d -> c n d"), (), [wx])
        S.dma("sp", gB.t[:], self.bc(d["norm_mix"][L, :]), (), [gB])
        for j_ in range(4):
            S.dma("sp", cw.t[:, j_, :], d["rg_conv_w"][0][j_, :].rearrange("(n p) -> p n", p=128), (), [cw], allow_slow_non_contiguous=True)
        for k_, nm in enumerate(["rg_conv_b", "rg_b_a", "rg_b_x", "rg_lambda"]):
            S.dma("sp", pv.t[:, k_, :], d[nm][0].rearrange("(n p) -> p n", p=128), (), [pv], allow_slow_non_contiguous=True)
        self.A(lambda e: e.activation(out=pv.t[:, 3, :], in_=pv.t[:, 3, :], func=AF.Exp, scale=-1.0), [pv], [pv])
        self.A(lambda e: e.activation(out=pv.t[:, 3, :], in_=pv.t[:, 3, :], func=AF.Ln, bias=1.0), [pv], [pv])
        self.V(lambda e: e.tensor_scalar(out=pv.t[:, 4, :], in0=pv.t[:, 3, :], scalar1=-16.0, scalar2=None, op0=ALU.mult), [pv], [pv])
        self.V(lambda e: e.tensor_scalar(out=pv.t[:, 3, :], in0=pv.t[:, 3, :], scalar1=-8.0, scalar2=None, op0=ALU.mult), [pv], [pv])
        cur = None
        for seq, i, x in self.tiles(L):
            nt = seq.nt
            if cur is not seq:
                if cur is not None:
                    self.rg_out(cur, hprev, ub)
                cur = seq
                if seq.name == "p":
                    self.V(lambda e: e.memset(ub.t[:], 0.0), (), [ub])
                    self.V(lambda e: e.memset(hprev.t[:], 0.0), (), [hprev])
                else:
                    S.dma("sp", hprev.t[:], d["st_rh"][0].rearrange("(n p) -> p n", p=128), (), [hprev], allow_slow_non_contiguous=True)
                    for j_ in range(3):
                        S.dma("sp", ub.t[:, :, j_], d["st_rc"][0][j_, :].rearrange("(n p) -> p n", p=128), (), [ub], allow_slow_non_contiguous=True)
            self.norm_T(x, gB, nt)
            hT = self.hT
            for cc in range(8):
                pg = self.psum()
                pu = self.psum()
                for kc in range(8):
                    self.mm(pg, pg.t[:, :nt], w_in, w_in.t[:, kc, cc * 128:(cc + 1) * 128], hT, hT.t[:, kc, :nt], kc == 0, kc == 7)
                for kc in range(8):
                    self.mm(pu, pu.t[:, :nt], w_in, w_in.t[:, kc, 1024 + cc * 128:1024 + (cc + 1) * 128], hT, hT.t[:, kc, :nt], kc == 0, kc == 7)
                self.V(lambda e, pg=pg, cc=cc: e.tensor_copy(out=g1.t[:, cc, :nt], in_=pg.t[:, :nt]), [pg], [g1])
                self.V(lambda e, pu=pu, cc=cc: e.tensor_copy(out=ub.t[:, cc, 3:3 + nt], in_=pu.t[:, :nt]), [pu], [ub])
                self.V(lambda e, cc=cc: e.tensor_scalar(out=xc.t[:, cc, :nt], in0=ub.t[:, cc, 0:nt], scalar1=cw.t[:, 0, cc:cc + 1], scalar2=pv.t[:, 0, cc:cc + 1],
                                                        op0=ALU.mult, op1=ALU.add), [ub, cw, pv], [xc])
                for j in range(1, 4):
                    self.V(lambda e, cc=cc, j=j: e.scalar_tensor_tensor(out=xc.t[:, cc, :nt], in0=ub.t[:, cc, j:j + nt], scalar=cw.t[:, j, cc:cc + 1],
                                                                       in1=xc.t[:, cc, :nt], op0=ALU.mult, op1=ALU.add), [ub, cw, xc], [xc])
                pr = self.psum()
                pi_ = self.psum()
                self.mm(pr, pr.t[:, :nt], wa, wa.t[:, cc, :], xc, xc.t[:, cc, :nt])
                self.mm(pi_, pi_.t[:, :nt], wx, wx.t[:, cc, :], xc, xc.t[:, cc, :nt])
                self.V(lambda e, cc=cc, pr=pr: e.tensor_scalar(out=rr.t[:, cc, :nt], in0=pr.t[:, :nt], scalar1=pv.t[:, 1, cc:cc + 1], scalar2=None, op0=ALU.add), [pr, pv], [rr])
                self.V(lambda e, cc=cc, pi_=pi_: e.tensor_scalar(out=ig.t[:, cc, :nt], in0=pi_.t[:, :nt], scalar1=pv.t[:, 2, cc:cc + 1], scalar2=None, op0=ALU.add), [pi_, pv], [ig])
            self.gelu_tanh(gel, gel.t[:, :, :nt], g1, g1.t[:, :, :nt], tmp, tmp.t[:, :, :nt])
            self.A(lambda e: e.activation(out=rr.t[:, :, :nt], in_=rr.t[:, :, :nt], func=AF.Sigmoid), [rr], [rr])
            self.A(lambda e: e.activation(out=ig.t[:, :, :nt], in_=ig.t[:, :, :nt], func=AF.Sigmoid), [ig], [ig])
            self.V(lambda e: e.tensor_tensor(out=rr.t[:, :, :nt], in0=rr.t[:, :, :nt], in1=pv.t[:, 3, :].unsqueeze(2).broadcast_to([128, 8, nt]), op=ALU.mult), [rr, pv], [rr])
            self.A(lambda e: e.activation(out=aa.t[:, :, :nt], in_=rr.t[:, :, :nt], func=AF.Exp), [rr], [aa])
            self.A(lambda e: e.activation(out=mm_.t[:, :, :nt], in_=rr.t[:, :, :nt], func=AF.Exp, scale=2.0), [rr], [mm_])
            self.A(lambda e: e.activation(out=mm_.t[:, :, :nt], in_=mm_.t[:, :, :nt], func=AF.Sqrt, scale=-1.0, bias=1.0), [mm_], [mm_])
            self.V(lambda e: e.tensor_tensor(out=ig.t[:, :, :nt], in0=ig.t[:, :, :nt], in1=xc.t[:, :, :nt], op=ALU.mult), [ig, xc], [ig])
            self.V(lambda e: e.tensor_tensor(out=ig.t[:, :, :nt], in0=ig.t[:, :, :nt], in1=mm_.t[:, :, :nt], op=ALU.mult), [ig, mm_], [ig])
            for cc in range(8):
                self.V(lambda e, cc=cc: e.tensor_tensor_scan(out=hb.t[:, cc, :nt], data0=aa.t[:, cc, :nt], data1=ig.t[:, cc, :nt], initial=hprev.t[:, cc:cc + 1],
                                                             op0=ALU.mult, op1=ALU.add), [aa, ig, hprev], [hb])
                self.V(lambda e, cc=cc: e.tensor_copy(out=hprev.t[:, cc:cc + 1], in_=hb.t[:, cc, nt - 1:nt]), [hb], [hprev])
            self.V(lambda e: e.tensor_tensor(out=hgT.t[:, :, :nt], in0=hb.t[:, :, :nt], in1=gel.t[:, :, :nt], op=ALU.mult), [hb, gel], [hgT])
            self.V(lambda e: e.tensor_copy(out=tmp.t[:, :, 0:3], in_=ub.t[:, :, nt:nt + 3]), [ub], [tmp])
            self.V(lambda e: e.tensor_copy(out=ub.t[:, :, 0:3], in_=tmp.t[:, :, 0:3]), [tmp], [ub])
            self.out_proj(hgT, lambda kc: hgT.t[:, kc, :nt], w_out, nt, x)
            self.store_x(seq, i, x)
        self.rg_out(cur, hprev, ub)

    def rg_out(self, seq, hprev, ub):
        d = self.d
        p = seq.name == "p"
        self.S.dma("sp", d["rhp" if p else "rhs"][0].rearrange("(n p) -> p n", p=128), hprev.t[:], [hprev], (), out_dram=True, allow_slow_non_contiguous=True)
        for j_ in range(3):
            self.S.dma("sp", d["rcp" if p else "rcs"][0][j_, :].rearrange("(n p) -> p n", p=128), ub.t[:, :, j_], [ub], (), out_dram=True, allow_slow_non_contiguous=True)

    def mla_pass(self, L):
        S, d = self.S, self.d
        NB = max(self.SEQ // 128, (PAST + TS + 127) // 128)
        w_dn = self.av("w_dn", [128, 8, 704], BF16)
        w_uq = self.av("w_uq", [128, 3, 1536], BF16)
        w_ukT = self.av("w_ukT", [128, 8, 256], BF16)
        w_uv = self.av("w_uv", [128, 2, 1024], BF16)
        w_out = self.av("w_out", [128, 8, D], BF16)
        ukraw = self.av("ukraw", [128, 2, 1024], F32)
        gB = self.av("gmix", [128, D], F32)
        qnB = self.av("qnB", [128, 384], F32)
        kvB = self.av("kvB", [128, 256], F32)
        KTc = self.av("KTc", [128, 2, NB * 128], BF16)
        KTr = self.av("KTr", [128, NB * 128], BF16)
        Va = self.av("Va", [128, NB, 258], BF16)
        dn = self.av("dn", [128, 704], F32)
        cqn = self.av("cqn", [128, 384], F32)
        ckn = self.av("ckn", [128, 256], F32)
        krr = self.av("krr", [128, 64], F32)
        cs = self.av("cs", [128, 2, 32], F32)
        cqT = self.av("cqT", [128, 3, 128], BF16)
        qnT = self.av("qnT", [128, 128], BF16)
        qra = self.av("qra", [128, 8, 66], F32)
        qr0 = self.av("qr0", [128, 8, 64], F32)
        rt = self.av("rt", [128, 8, 32], F32)
        mx = self.av("mx", [128, 8, 4], F32)
        PT_l = [self.av(f"PT{i}", [128, 4, 128], BF16) for i in range(2)]
        pti = 0
        ol = self.av("ol", [128, 256], F32)
        olT = self.av("olT", [128, 2, 128], BF16)
        oT = self.av("oT", [128, 8, 128], BF16)
        past_c_ap = ukraw.t[:, 0, :].rearrange("p (b c) -> p b c", b=4)
        past_r_ap = ukraw.t.rearrange("p a (b c) -> p (a b) c", c=64)
        qaT = {s.name: self.av("qaT" + s.name, [128, 2, 8 * s.nt], BF16) for s in self.seqs}
        qrT = {s.name: self.av("qrT" + s.name, [128, 8 * s.nt], BF16) for s in self.seqs}
        self.load_w(w_dn, d["mla_w_down"][0])
        self.load_w(w_uq, d["mla_w_uq"][0])
        self.load_w(w_out, d["mla_w_out"][0])
        S.dma("pool", w_uv.t[:], d["mla_w_uv"][0].rearrange("(kc p) h v -> p kc (h v)", p=128), (), [w_uv])
        S.dma("sp", ukraw.t[:], d["mla_w_uk"][0].rearrange("(kc p) h n -> p kc (h n)", p=128), (), [ukraw])
        S.dma("sp", gB.t[:], self.bc(d["norm_mix"][L, :]), (), [gB])
        S.dma("sp", qnB.t[:], self.bc(d["mla_q_norm"][0, :]), (), [qnB])
        S.dma("sp", kvB.t[:], self.bc(d["mla_kv_norm"][0, :]), (), [kvB])
        for h in range(8):
            pb = self.psum()
            for kc in range(2):
                self.tr(pb, pb.t[:, kc * 128:(kc + 1) * 128], ukraw, ukraw.t[:, kc, h * 128:(h + 1) * 128], 128)
            self.V(lambda e, h=h, pb=pb: e.tensor_copy(out=w_ukT.t[:, h, :], in_=pb.t[:, 0:256]), [pb], [w_ukT])
        self.V(lambda e: e.memset(KTr.t[64:65, :], 1.0), (), [KTr])
        self.V(lambda e: e.memset(Va.t[:, :, 256:257], 1.0), (), [Va])
        for seq in self.seqs:
            nt = seq.nt
            p = seq.name == "p"
            nb0 = seq.past // 128
            if seq.past:
                S.dma("pool", Va.t[:, 0:nb0, 0:256], d["c_ckv"][0].rearrange("(b p) c -> p b c", p=128), (), [Va])
                for b0 in range(0, nb0, 4):
                    S.dma("sp", past_c_ap, d["c_ckv"][0][b0 * 128:(b0 + 4) * 128, :].rearrange("(b p) c -> p b c", p=128), (), [ukraw])
                    for b in range(4):
                        pb = self.psum()
                        for kc in range(2):
                            self.tr(pb, pb.t[:, kc * 128:(kc + 1) * 128], ukraw, past_c_ap[:, b, kc * 128:(kc + 1) * 128], 128)
                        bb = b0 + b
                        self.V(lambda e, pb=pb, bb=bb: e.tensor_copy(out=KTc.t[:, :, bb * 128:(bb + 1) * 128],
                                                               in_=pb.t[:, 0:256].rearrange("p (a b) -> p a b", a=2)), [pb], [KTc])
                S.dma("sp", past_r_ap, d["c_kr"][0].rearrange("(b p) c -> p b c", p=128), (), [ukraw])
                for b0 in range(0, nb0, 4):
                    pb = self.psum()
                    for b in range(4):
                        self.tr(pb, pb.t[0:64, b * 128:(b + 1) * 128], ukraw, past_r_ap[:, b0 + b, :], 128)
                    self.V(lambda e, pb=pb, b0=b0: e.tensor_copy(out=KTr.t[0:64, b0 * 128:(b0 + 4) * 128], in_=pb.t[0:64, :]), [pb], [KTr])
            qa, qr = qaT[seq.name], qrT[seq.name]
            for (s_, i, x) in self.tiles_seq(seq, L):
                self.norm_T(x, gB, nt)
                hT = self.hT
                S.dma("sp", cs.t[:nt, 0, :], d["cos_p" if p else "cos_s"][i * nt:(i + 1) * nt, :], (), [cs])
                S.dma("sp", cs.t[:nt, 1, :], d["sin_p" if p else "sin_s"][i * nt:(i + 1) * nt, :], (), [cs])
                for (n0, n1) in ((0, 512), (512, 704)):
                    pb = self.psum()
                    for kc in range(8):
                        self.mm(pb, pb.t[:nt, 0:n1 - n0], hT, hT.t[:, kc, :nt], w_dn, w_dn.t[:, kc, n0:n1], kc == 0, kc == 7)
                    self.V(lambda e, pb=pb, n0=n0, n1=n1: e.tensor_copy(out=dn.t[:nt, n0:n1], in_=pb.t[:nt, 0:n1 - n0]), [pb], [dn])
                st = self.st
                self.rstd(dn, dn.t[:nt, 0:384], nt, 384, st.t[:nt, 5:6], self.junk.t[:nt, 0:384])
                self.V(lambda e: e.scalar_tensor_tensor(out=cqn.t[:nt, :], in0=dn.t[:nt, 0:384], scalar=st.t[:nt, 5:6], in1=qnB.t[:nt, :],
                                                        op0=ALU.mult, op1=ALU.mult), [dn, st, qnB], [cqn])
                self.rstd(dn, dn.t[:nt, 384:640], nt, 256, st.t[:nt, 6:7], self.junk.t[:nt, 0:256])
                self.V(lambda e: e.scalar_tensor_tensor(out=ckn.t[:nt, :], in0=dn.t[:nt, 384:640], scalar=st.t[:nt, 6:7], in1=kvB.t[:nt, :],
                                                        op0=ALU.mult, op1=ALU.mult), [dn, st, kvB], [ckn])
                S.dma("sp", d["ckp" if p else "cks"][0][i * nt:(i + 1) * nt, :], ckn.t[:nt, :], [ckn], (), out_dram=True)
                self.rope(krr.t[:nt, 0:32], krr.t[:nt, 32:64], dn.t[:nt, 640:672], dn.t[:nt, 672:704], cs.t[:nt, 0, :], cs.t[:nt, 1, :],
                          rt.t[:nt, 0, :], [dn, cs], krr, rt)
                S.dma("sp", d["krp" if p else "krs"][0][i * nt:(i + 1) * nt, :], krr.t[:nt, :], [krr], (), out_dram=True)
                kb = nb0 + (i * nt) // 128
                kcol = seq.past + i * nt
                self.V(lambda e, kb=kb: e.tensor_copy(out=Va.t[:nt, kb, 0:256], in_=ckn.t[:nt, :]), [ckn], [Va])
                pb = self.psum()
                for kc in range(2):
                    self.tr(pb, pb.t[:, kc * 128:kc * 128 + nt], ckn, ckn.t[:nt, kc * 128:(kc + 1) * 128], nt)
                for kc in range(2):
                    self.V(lambda e, pb=pb, kc=kc, kcol=kcol: e.tensor_copy(out=KTc.t[:, kc, kcol:kcol + nt], in_=pb.t[:, kc * 128:kc * 128 + nt]), [pb], [KTc])
                pb = self.psum()
                self.tr(pb, pb.t[0:64, 0:nt], krr, krr.t[:nt, :], nt)
                self.V(lambda e, pb=pb, kcol=kcol: e.tensor_copy(out=KTr.t[0:64, kcol:kcol + nt], in_=pb.t[0:64, 0:nt]), [pb], [KTr])
                self.transpose_to(cqn, cqn.t, nt, 3, cqT, lambda kc: cqT.t[:, kc, :nt])
                pb = self.psum()
                for kc in range(3):
                    self.mm(pb, pb.t[:nt, :], cqT, cqT.t[:, kc, :nt], w_uq,
                            w_uq.t[:, kc, :].rearrange("p (h n) -> p h n", h=8)[:, :, 128:192], kc == 0, kc == 2)
                self.V(lambda e, pb=pb: e.tensor_scalar(out=qr0.t[:nt], in0=pb.t[:nt, :].rearrange("p (h n) -> p h n", h=8), scalar1=QSCALE, scalar2=None, op0=ALU.mult), [pb], [qr0])
                cosb = cs.t[:nt, 0, :].unsqueeze(1).broadcast_to([nt, 8, 32])
                sinb = cs.t[:nt, 1, :].unsqueeze(1).broadcast_to([nt, 8, 32])
                self.rope(qra.t[:nt, :, 0:32], qra.t[:nt, :, 32:64], qr0.t[:nt, :, 0:32], qr0.t[:nt, :, 32:64], cosb, sinb, rt.t[:nt], [qr0, cs], qra, rt)
                for h in range(8):
                    pn = self.psum()
                    for kc in range(3):
                        self.mm(pn, pn.t[:, :nt], w_uq, w_uq.t[:, kc, h * 192:h * 192 + 128], cqT, cqT.t[:, kc, :nt], kc == 0, kc == 2)
                    self.V(lambda e, pn=pn: e.tensor_scalar(out=qnT.t[:, :nt], in0=pn.t[:, :nt], scalar1=QSCALE, scalar2=None, op0=ALU.mult), [pn], [qnT])
                    pa = self.psum()
                    for kc in range(2):
                        self.mm(pa, pa.t[:, kc * 128:kc * 128 + nt], w_ukT, w_ukT.t[:, h, kc * 128:(kc + 1) * 128], qnT, qnT.t[:, :nt])
                    for kc in range(2):
                        self.V(lambda e, pa=pa, kc=kc, h=h: e.tensor_copy(out=qa.t[:, kc, h * nt:(h + 1) * nt], in_=pa.t[:, kc * 128:kc * 128 + nt]), [pa], [qa])
                nkeys = kcol + nt
                self.V(lambda e: e.memset(qra.t[:nt, :, 64:65], 0.0), (), [qra])
                self.qr_transpose(qra, qr, nt)
                for h in range(8):
                    for k0 in range(0, nkeys, 512):
                        k1 = min(nkeys, k0 + 512)
                        pb = self.psum()
                        for kc in range(2):
                            self.mm(pb, pb.t[:nt, 0:k1 - k0], qa, qa.t[:, kc, h * nt:(h + 1) * nt], KTc, KTc.t[:, kc, k0:k1], kc == 0, False)
                        self.mm(pb, pb.t[:nt, 0:k1 - k0], qr, qr.t[0:64, h * nt:(h + 1) * nt], KTr, KTr.t[0:64, k0:k1], False, True)
                        ci = 1 + (k0 // 512) % 2 if k0 else 0
                        self.V(lambda e, pb=pb, h=h, ci=ci, k0=k0, k1=k1: e.tensor_reduce(out=mx.t[:nt, h, ci:ci + 1], in_=pb.t[:nt, 0:k1 - k0],
                                                                                         axis=AX.X, op=ALU.max), [pb], [mx])
                        if k0:
                            self.V(lambda e, h=h, ci=ci: e.tensor_tensor(out=mx.t[:nt, h, 0:1], in0=mx.t[:nt, h, 0:1], in1=mx.t[:nt, h, ci:ci + 1],
                                                                         op=ALU.max), [mx], [mx])
                self.V(lambda e: e.tensor_scalar(out=qra.t[:nt, :, 64:65], in0=mx.t[:nt, :, 0:1], scalar1=-1.0, scalar2=None, op0=ALU.mult), [mx], [qra])
                self.qr_transpose(qra, qr, nt)
                nkb = (nkeys + 127) // 128
                for hg in range(2):
                    acc = self.psB
                    for kb_ in range(nkb):
                        k0 = kb_ * 128
                        kl = min(128, nkeys - k0)
                        pb = self.psum()
                        PT = PT_l[pti % 2]; pti += 1
                        for kc in range(2):
                            self.mm(pb, pb.t[:kl, 0:4 * nt], KTc, KTc.t[:, kc, k0:k0 + kl], qa, qa.t[:, kc, hg * 4 * nt:(hg + 1) * 4 * nt], kc == 0, False)
                        self.mm(pb, pb.t[:kl, 0:4 * nt], KTr, KTr.t[0:65, k0:k0 + kl], qr, qr.t[0:65, hg * 4 * nt:(hg + 1) * 4 * nt], False, True)
                        self.A(lambda e, pb=pb, kl=kl: e.activation(out=PT.t[:kl, :, :nt], in_=pb.t[:kl, 0:4 * nt].rearrange("p (h t) -> p h t", h=4),
                                                                    func=AF.Exp), [pb], [PT])
                        if p and kb_ == nkb - 1:
                            self.V(lambda e: e.memset(PT.t[64:128, :, 0:64], 0.0), (), [PT])
                        for hh in range(4):
                            self.mm(acc[hh], acc[hh].t[:nt, 0:257], PT, PT.t[:kl, hh, :nt], Va, Va.t[:kl, kb_, 0:257], kb_ == 0, kb_ == nkb - 1)
                    for hh in range(4):
                        h = hg * 4 + hh
                        a_ = acc[hh]
                        self.V(lambda e, a_=a_: e.reciprocal(out=st.t[:nt, 7:8], in_=a_.t[:nt, 256:257]), [a_], [st])
                        self.V(lambda e, a_=a_: e.tensor_scalar(out=ol.t[:nt, :], in0=a_.t[:nt, 0:256], scalar1=st.t[:nt, 7:8], scalar2=None, op0=ALU.mult),
                               [a_, st], [ol])
                        self.transpose_to(ol, ol.t, nt, 2, olT, lambda kc: olT.t[:, kc, :nt])
                        po = self.psum()
                        for kc in range(2):
                            self.mm(po, po.t[:, :nt], w_uv, w_uv.t[:, kc, h * 128:(h + 1) * 128], olT, olT.t[:, kc, :nt], kc == 0, kc == 1)
                        self.V(lambda e, po=po, h=h: e.tensor_copy(out=oT.t[:, h, :nt], in_=po.t[:, :nt]), [po], [oT])
                self.out_proj(oT, lambda kc: oT.t[:, kc, :nt], w_out, nt, x)
                self.store_x(seq, i, x)

    def qr_transpose(self, qra, qr, nt):
        for h0 in range(0, 8, 4):
            pb = self.psum()
            for hh in range(4):
                self.tr(pb, pb.t[0:65, hh * nt:(hh + 1) * nt], qra, qra.t[:nt, h0 + hh, 0:65], nt)
            self.V(lambda e, pb=pb, h0=h0: e.tensor_copy(out=qr.t[0:65, h0 * nt:(h0 + 4) * nt], in_=pb.t[0:65, 0:4 * nt]), [pb], [qr])

    def rope(self, o1, o2, x1, x2, cos, sin, tmp, rtoks, otok, ttok):
        self.V(lambda e: e.tensor_tensor(out=o1, in0=x1, in1=cos, op=ALU.mult), rtoks, [otok])
        self.V(lambda e: e.tensor_tensor(out=tmp, in0=x2, in1=sin, op=ALU.mult), rtoks, [ttok])
        self.V(lambda e: e.tensor_tensor(out=o1, in0=o1, in1=tmp, op=ALU.subtract), [otok, ttok], [otok])
        self.V(lambda e: e.tensor_tensor(out=o2, in0=x2, in1=cos, op=ALU.mult), rtoks, [otok])
        self.V(lambda e: e.tensor_tensor(out=tmp, in0=x1, in1=sin, op=ALU.mult), rtoks, [ttok])
        self.V(lambda e: e.tensor_tensor(out=o2, in0=o2, in1=tmp, op=ALU.add), [otok, ttok], [otok])

    def tiles_seq(self, seq, L):
        for i in range(seq.ntiles):
            b = self.xb[self.xi % 2]; self.xi += 1
            self.load_x(seq, i, L, b)
            yield seq, i, b


_CACHE = {}


def rope_tab(pos):
    inv = (np.float32(10000.0) ** (-np.arange(0, 64, 2, dtype=np.float32) / np.float32(64))).astype(np.float32)
    ang = pos.astype(np.float32)[:, None] * inv[None, :]
    return np.cos(ang).astype(np.float32), np.sin(ang).astype(np.float32)


def make_in_maps(inp, SEQ, ncores, depth=DEPTH):
    f = lambda a: np.ascontiguousarray(np.asarray(a, dtype=np.float32))
    cp, sp_ = rope_tab(np.arange(SEQ))
    cs_, ss_ = rope_tab(PAST + np.arange(TS))
    shared = {k: f(inp[k]) for k in ["norm_mix", "norm_ffn", "hg_w_in", "hg_lb_logits", "hg_norm", "hg_w_out", "rg_w_in", "rg_conv_w",
                                     "rg_conv_b", "rg_w_a", "rg_b_a", "rg_w_x", "rg_b_x", "rg_lambda", "rg_w_out", "mla_w_down", "mla_q_norm",
                                     "mla_w_uq", "mla_kv_norm", "mla_w_uk", "mla_w_uv", "mla_w_out", "pk_w_query", "pk_sub_keys", "pk_u", "pk_v"]}
    shared["norm_final"] = f(inp["norm_final"]).reshape(1, D)
    shared["pk_u"] = shared["pk_u"][:depth]
    shared["pk_v"] = shared["pk_v"][:depth]
    cst = np.zeros((128, 272), np.float32)
    cst[:, 0:128] = np.eye(128, dtype=np.float32)
    cst[:, 128:256] = np.triu(np.ones((128, 128), np.float32))
    cst[:, 256:272] = np.arange(16, dtype=np.float32)[None, :]
    shared.update(cos_p=cp, sin_p=sp_, cos_s=cs_, sin_s=ss_, cst=cst)
    maps = []
    for c in range(ncores):
        m = dict(shared)
        m["xp"] = f(inp["x_prompt"][c // 2, :SEQ])
        m["xs"] = f(inp["x_sample"][c])
        m["st_hg"] = f(inp["state_hgrn"][:, c])
        m["st_rh"] = f(inp["state_rglru_h"][:, c])
        m["st_rc"] = f(inp["state_rglru_conv"][:, c])
        m["c_ckv"] = f(inp["cache_mla_ckv"][:, c])
        m["c_kr"] = f(inp["cache_mla_krope"][:, c])
        maps.append(m)
    return maps


def assemble(res, SEQ, ncores=8):
    r = res
    ev = list(range(0, ncores, 2))
    st = lambda k, cs: np.stack([r[c][k] for c in cs], axis=0)
    yp = st("yp", ev)
    ys = st("ys", range(ncores))
    hgp = np.stack([r[c]["hgp"] for c in ev], axis=1)
    hgs = np.stack([r[c]["hgs"] for c in range(ncores)], axis=1)
    rhp = np.stack([r[c]["rhp"] for c in ev], axis=1)
    rhs = np.stack([r[c]["rhs"] for c in range(ncores)], axis=1)
    rcp = np.stack([r[c]["rcp"] for c in ev], axis=1)
    rcs = np.stack([r[c]["rcs"] for c in range(ncores)], axis=1)
    ckp = np.stack([r[c]["ckp"] for c in ev], axis=1)
    cks = np.stack([r[c]["cks"] for c in range(ncores)], axis=1)
    krp = np.stack([r[c]["krp"] for c in ev], axis=1)
    krs = np.stack([r[c]["krs"] for c in range(ncores)], axis=1)
    return (yp, ys, hgp, hgs, rhp, rhs, rcp, rcs, ckp, cks, krp, krs)


def kernel(**inputs):
    SEQ = inputs["x_prompt"].shape[1]
    key = SEQ
    if key not in _CACHE:
        _CACHE[key] = K(SEQ).nc
    nc = _CACHE[key]
    maps = make_in_maps(inputs, SEQ, 8)
    res = run_bass_kernel_spmd(nc, maps, core_ids=list(range(8)))
    return assemble(res.results, SEQ, 8)
```
